# Optimizing a Trainium2 kernel written in Bass

```python
import jax
import jax.numpy as jnp
from jax import lax

D_MODEL = 1024
BATCH = 16
SEQ = 2048
DEPTH = 1

N_POOL_GROUPS = 4
POOL_WINDOWS = (2, 4, 8, 16)
POOL_WIDTH = D_MODEL
POOL_GROUP_DIM = POOL_WIDTH // N_POOL_GROUPS
CONV_WIDTH = D_MODEL
CONV_KSIZE = 3
N_BRANCHES = 2
D_IN = POOL_WIDTH + 3 * CONV_WIDTH + N_BRANCHES * D_MODEL
D_FF = ((8 * D_MODEL // 3 + 127) // 128) * 128
FFN_KSIZE = 3
RMS_EPS = 1e-6

kernel_name = 'hybrid_pool_shortconv_gated_block'


def rms_norm(x, g):
    xf = x.astype(jnp.float32)
    inv = lax.rsqrt(jnp.mean(xf * xf, axis=-1, keepdims=True) + RMS_EPS)
    return (xf * inv * g.astype(jnp.float32)).astype(x.dtype)


def causal_depthwise_conv(u, w):
    k, c = w.shape
    return lax.conv_general_dilated(
        u, w.astype(u.dtype)[:, None, :], window_strides=(1,), padding=[(k - 1, 0)],
        dimension_numbers=('NWC', 'WIO', 'NWC'), feature_group_count=c)


def causal_multiscale_pool(u):
    s = u.shape[1]
    pos = jnp.arange(1, s + 1, dtype=jnp.float32)
    outs = []
    for gi, win in enumerate(POOL_WINDOWS):
        ug = u[..., gi * POOL_GROUP_DIM:(gi + 1) * POOL_GROUP_DIM].astype(jnp.float32)
        csum = jnp.pad(jnp.cumsum(ug, axis=1), ((0, 0), (1, 0), (0, 0)))
        upper = csum[:, 1:]
        lower = jnp.pad(csum[:, :s + 1 - win], ((0, 0), (win - 1, 0), (0, 0)))
        count = jnp.minimum(pos, float(win))[None, :, None]
        outs.append(((upper - lower) / count - ug).astype(u.dtype))
    return jnp.stack(outs, axis=2)


def token_mixer(h, w_in, pool_w, pool_scale, w_pool_proj, conv_w, w_conv_out, w_o):
    b, s, _ = h.shape
    z = jnp.einsum('bsd,de->bse', h, w_in)
    splits = [POOL_WIDTH,
              POOL_WIDTH + CONV_WIDTH,
              POOL_WIDTH + 2 * CONV_WIDTH,
              POOL_WIDTH + 3 * CONV_WIDTH,
              POOL_WIDTH + 3 * CONV_WIDTH + D_MODEL]
    z_pool, z_b, z_c, z_v, z_gpool, z_gconv = jnp.split(z, splits, axis=-1)
    p = causal_multiscale_pool(z_pool)
    p = jnp.einsum('bsgc,gce->bsge', p, pool_w).reshape(b, s, POOL_WIDTH) * pool_scale
    y_pool = jnp.einsum('bsp,pd->bsd', p, w_pool_proj)
    y_conv = jnp.einsum('bsc,cd->bsd', z_b * causal_depthwise_conv(z_c * z_v, conv_w), w_conv_out)
    merged = jax.nn.sigmoid(z_gpool) * y_pool + jax.nn.sigmoid(z_gconv) * y_conv
    return jnp.einsum('bsd,de->bse', merged, w_o)


def channel_mixer(h, w_up, ffn_conv_w, ffn_conv_b, w_down):
    u = causal_depthwise_conv(jnp.einsum('bsd,df->bsf', h, w_up), ffn_conv_w) + ffn_conv_b
    gate, val = jnp.split(u, 2, axis=-1)
    return jnp.einsum('bsf,fd->bsd', jax.nn.silu(gate) * val, w_down)


def _normal(k, shape, scale):
    return jax.random.normal(k, shape, jnp.float32) * scale


def setup_inputs(seed: int = 0) -> dict:
    key = jax.random.key(seed)
    ks = jax.random.split(key, 15)
    return {
        'x': _normal(ks[0], (BATCH, SEQ, D_MODEL), 1.0),
        'norm_mix': 1.0 + _normal(ks[1], (DEPTH, D_MODEL), 0.1),
        'w_in': _normal(ks[2], (DEPTH, D_MODEL, D_IN), D_MODEL ** -0.5),
        'pool_w': _normal(ks[3], (DEPTH, N_POOL_GROUPS, POOL_GROUP_DIM, POOL_GROUP_DIM), POOL_GROUP_DIM ** -0.5),
        'pool_scale': 1.0 + _normal(ks[4], (DEPTH, POOL_WIDTH), 0.1),
        'w_pool_proj': _normal(ks[5], (DEPTH, POOL_WIDTH, D_MODEL), POOL_WIDTH ** -0.5),
        'conv_w': _normal(ks[6], (DEPTH, CONV_KSIZE, CONV_WIDTH), CONV_KSIZE ** -0.5),
        'w_conv_out': _normal(ks[7], (DEPTH, CONV_WIDTH, D_MODEL), CONV_WIDTH ** -0.5),
        'w_o': _normal(ks[8], (DEPTH, D_MODEL, D_MODEL), D_MODEL ** -0.5),
        'norm_ffn': 1.0 + _normal(ks[9], (DEPTH, D_MODEL), 0.1),
        'w_up': _normal(ks[10], (DEPTH, D_MODEL, 2 * D_FF), D_MODEL ** -0.5),
        'ffn_conv_w': _normal(ks[11], (DEPTH, FFN_KSIZE, 2 * D_FF), FFN_KSIZE ** -0.5),
        'ffn_conv_b': _normal(ks[12], (DEPTH, 2 * D_FF), 0.02),
        'w_down': _normal(ks[13], (DEPTH, D_FF, D_MODEL), D_FF ** -0.5),
        'norm_final': 1.0 + _normal(ks[14], (D_MODEL,), 0.1),
    }


def reference(x, norm_mix, w_in, pool_w, pool_scale, w_pool_proj, conv_w, w_conv_out, w_o,
              norm_ffn, w_up, ffn_conv_w, ffn_conv_b, w_down, norm_final):
    for layer in range(DEPTH):
        h = rms_norm(x, norm_mix[layer])
        x = x + token_mixer(h, w_in[layer], pool_w[layer], pool_scale[layer], w_pool_proj[layer],
                            conv_w[layer], w_conv_out[layer], w_o[layer])
        h = rms_norm(x, norm_ffn[layer])
        x = x + channel_mixer(h, w_up[layer], ffn_conv_w[layer], ffn_conv_b[layer], w_down[layer])
    return rms_norm(x, norm_final)
```

```python
from contextlib import ExitStack

import os
import numpy as np
import concourse.bass as bass
import concourse.mybir as mybir
from concourse.bass_utils import run_bass_kernel_spmd

F32 = mybir.dt.float32
BF16 = mybir.dt.bfloat16
AF = mybir.ActivationFunctionType
ALU = mybir.AluOpType

N_CORES = 8
D = 1024
SEQ = 2048
TOK_PER_CORE = 4096
T = 512
NT = TOK_PER_CORE // T
TILES_PER_SEQ = SEQ // T
D_FF = 2816
NJ = D_FF // 128
EPS = 1e-6
R = 8
NBANK = 6
NT512 = 7
NU = 3
NS = 3
POOL_WINDOWS = (2, 4, 8, 16)
DBG = os.environ.get("KDBG", "")
MUL_ENG = "dve" if "dvemul" in DBG else "pool"


_CACHE = {}


class Stream:
    def __init__(self, name):
        self.name = name
        self.count = 0
        self.ops = []
        self.waited = {}


class Prog:
    def __init__(self):
        self.streams = {n: Stream(n) for n in ("pe", "act", "dve", "pool", "sp")}
        self.res = {}
        self.semval = {}

    def _deps(self, reads, writes):
        evs = []
        for k in reads:
            r = self.res.get(k)
            if r and r[0] is not None:
                evs.append(r[0])
        for k in writes:
            r = self.res.get(k)
            if r:
                if r[0] is not None:
                    evs.append(r[0])
                evs.extend(r[1])
        return evs

    def _commit(self, ev, reads, writes):
        for k in reads:
            r = self.res.setdefault(k, [None, []])
            r[1].append(ev)
        for k in writes:
            self.res[k] = [ev, []]

    def _waits(self, st, evs):
        need = {}
        for sem, val in evs:
            if st.name == "pe" and sem == "pe":
                continue
            if st.waited.get(sem, 0) >= val:
                continue
            if need.get(sem, 0) < val:
                need[sem] = val
        for sem, val in need.items():
            st.waited[sem] = val
        return list(need.items())

    def op(self, eng, fn, reads=(), writes=()):
        st = self.streams[eng]
        waits = self._waits(st, self._deps(reads, writes))
        st.count += 1
        ev = (eng, st.count)
        st.ops.append((waits, fn, eng, 1))
        self._commit(ev, reads, writes)
        return ev

    def dma(self, queue, sem, fn, reads=(), writes=()):
        st = self.streams[queue]
        evs = self._deps(reads, writes)
        prev = self.semval.get(sem, 0)
        if prev:
            evs.append((sem, prev))
        waits = self._waits(st, evs)
        val = prev + 16
        self.semval[sem] = val
        ev = (sem, val)
        st.ops.append((waits, fn, sem, 16))
        self._commit(ev, reads, writes)
        return ev

    def wait_only(self, eng, evs):
        st = self.streams[eng]
        waits = self._waits(st, evs)
        if waits:
            st.ops.append((waits, None, None, 0))


def build_program(NT=NT, TOK=TOK_PER_CORE, STOP=99):
    nc = bass.Bass("TRN2", target_bir_lowering=False)
    dt = nc.dram_tensor

    x_d = dt("x", [TOK, D], F32, kind="ExternalInput").ap()
    out_d = dt("out", [TOK, D], F32, kind="ExternalOutput").ap()
    w_in_d = dt("w_in", [D, 6 * D], F32, kind="ExternalInput").ap()
    pool_w_d = dt("pool_w", [1024, 256], F32, kind="ExternalInput").ap()
    w_pp_d = dt("w_pool_proj", [D, D], F32, kind="ExternalInput").ap()
    w_co_d = dt("w_conv_out", [D, D], F32, kind="ExternalInput").ap()
    w_o_d = dt("w_o", [D, D], F32, kind="ExternalInput").ap()
    w_up_d = dt("w_up", [D, 2 * D_FF], F32, kind="ExternalInput").ap()
    w_dn_d = dt("w_down", [D_FF, D], F32, kind="ExternalInput").ap()
    vecs_d = dt("vecs", [128, 24], F32, kind="ExternalInput").ap()
    cw_d = dt("cw", [128, 24], F32, kind="ExternalInput").ap()
    fw_d = dt("fw", [128, 132], F32, kind="ExternalInput").ap()
    fb_d = dt("fb", [128, 44], F32, kind="ExternalInput").ap()
    gfin_d = dt("gfin", [128, D], F32, kind="ExternalInput").ap()
    aux_d = dt("aux", [128, 144], F32, kind="ExternalInput").ap()

    wv_in = w_in_d.rearrange("(k p) n -> p k n", p=128)
    wv_pw = pool_w_d.rearrange("(gk p) c -> p gk c", p=128)
    wv_pp = w_pp_d.rearrange("(k p) n -> p k n", p=128)
    wv_co = w_co_d.rearrange("(k p) n -> p k n", p=128)
    wv_o = w_o_d.rearrange("(k p) n -> p k n", p=128)
    wv_up = w_up_d.rearrange("(k p) n -> p k n", p=128)
    wv_dn = w_dn_d.rearrange("(k p) n -> p k n", p=128)

    units = []

    def add_unit(name, src, nk, ncols):
        units.append((name, src, nk, ncols))

    add_unit("Z0", wv_in[:, :, 0:512], 8, 512)
    add_unit("Z1", wv_in[:, :, 512:1024], 8, 512)
    add_unit("PW", wv_pw, 8, 256)
    for hh in range(2):
        add_unit(f"C{hh}", wv_in[:, :, 2048 + hh * 512:2048 + (hh + 1) * 512], 8, 512)
        add_unit(f"V{hh}", wv_in[:, :, 3072 + hh * 512:3072 + (hh + 1) * 512], 8, 512)
        add_unit(f"B{hh}", wv_in[:, :, 1024 + hh * 512:1024 + (hh + 1) * 512], 8, 512)
    for hh in range(2):
        add_unit(f"PP{hh}", wv_pp[:, :, hh * 512:(hh + 1) * 512], 8, 512)
        add_unit(f"CO{hh}", wv_co[:, :, hh * 512:(hh + 1) * 512], 8, 512)
        add_unit(f"GP{hh}", wv_in[:, :, 4096 + hh * 512:4096 + (hh + 1) * 512], 8, 512)
        add_unit(f"GC{hh}", wv_in[:, :, 5120 + hh * 512:5120 + (hh + 1) * 512], 8, 512)
    for hh in range(2):
        add_unit(f"WO{hh}", wv_o[:, :, hh * 512:(hh + 1) * 512], 8, 512)
    for u in range(6):
        nc_ = 512 if u < 5 else 256
        add_unit(f"UG{u}", wv_up[:, :, u * 512:u * 512 + nc_], 8, nc_)
        add_unit(f"UV{u}", wv_up[:, :, D_FF + u * 512:D_FF + u * 512 + nc_], 8, nc_)
    KG = [(0, 8), (8, 16), (16, 22)]
    for hh in range(2):
        for gi, (k0, k1) in enumerate(KG):
            add_unit(f"WD{hh}{gi}", wv_dn[:, k0:k1, hh * 512:(hh + 1) * 512], k1 - k0, 512)
    NUNITS = len(units)
    uidx = {u[0]: i for i, u in enumerate(units)}

    scr_d = dt("wscr", [NUNITS, 128, 8, 512], BF16, kind="Internal").ap()

    P = Prog()
    _CACHE["prog"] = P
    es = ExitStack()
    E = es.enter_context
    with es:
        xt = [E(nc.sbuf_tensor(f"xt{b}", [128, 4, D], F32)) for b in range(2)]
        xs = E(nc.sbuf_tensor("xs0", [128, 4, D], BF16))
        hT = [E(nc.sbuf_tensor(f"hT{i}", [128, 8, T], BF16)) for i in range(2)]
        A_t = E(nc.sbuf_tensor("A_t", [128, 8, T], BF16))
        p2_t = E(nc.sbuf_tensor("p2_t", [128, 8, T], BF16))
        cv_t = E(nc.sbuf_tensor("cv_t", [128, 8, T], BF16))
        aT_t = E(nc.sbuf_tensor("aT_t", [128, NJ, T], BF16))
        slots = [E(nc.sbuf_tensor(f"slot{r}", [128, 8, 512], BF16)) for r in range(R)]
        t512 = [E(nc.sbuf_tensor(f"t512_{i}", [128, T], F32)) for i in range(NT512)]
        ubuf = [E(nc.sbuf_tensor(f"ubuf{i}", [128, T + 16], F32)) for i in range(NU)]
        sbuf_ = [E(nc.sbuf_tensor(f"sbuf{i}", [128, T + 16], F32)) for i in range(NS)]
        junk = E(nc.sbuf_tensor("junk", [128, D], BF16))
        vecs = E(nc.sbuf_tensor("vecs_t", [128, 24], F32))
        cw = E(nc.sbuf_tensor("cw_t", [128, 3, 8], F32))
        fw = E(nc.sbuf_tensor("fw_t", [128, 3, 44], F32))
        fb = E(nc.sbuf_tensor("fb_t", [128, 44], F32))
        gfin = E(nc.sbuf_tensor("gfin_t", [128, D], F32))
        aux = E(nc.sbuf_tensor("aux_t", [128, 144], F32))
        identb = E(nc.sbuf_tensor("identb", [128, 128], BF16))
        mhalf = E(nc.sbuf_tensor("mhalf", [128, 4], F32))
        ss = [E(nc.sbuf_tensor(f"ss{i}", [128, 4], F32)) for i in range(3)]
        ms = [E(nc.sbuf_tensor(f"ms{i}", [128, 4], F32)) for i in range(3)]
        rstd = [E(nc.sbuf_tensor(f"rstd{i}", [128, 4], F32)) for i in range(3)]
        phalo = E(nc.sbuf_tensor("phalo", [128, 8, 16], F32))
        chalo = E(nc.sbuf_tensor("chalo", [128, 8, 2], F32))
        fhalo = E(nc.sbuf_tensor("fhalo", [128, 44, 2], F32))
        fH = E(nc.sbuf_tensor("fH", [128, 44, 2], F32))
        ftmp = E(nc.sbuf_tensor("ftmp", [128, 44], F32))
        pfix = E(nc.sbuf_tensor("pfix", [128, 16], F32))
        banks = [E(nc.psum_tensor(f"bank{i}", [128, 512], F32)) for i in range(NBANK)]
        tbank = [E(nc.psum_tensor(f"tbank{i}", [128, 512], BF16)) for i in range(2)]
        sem_names = ["pe", "act", "dve", "pool", "const", "x0", "x1"] + [f"w{r}" for r in range(R)]
        sems = {n: E(nc.semaphore("sem_" + n)) for n in sem_names}

        g1 = lambda k: vecs[:, k:k + 1]
        g2 = lambda k: vecs[:, 8 + k:9 + k]
        psc = lambda k: vecs[:, 16 + k:17 + k]
        invcnt = aux[:, 128:144]

        cnt = {"bank": 0, "t512": 0, "u": 0, "tb": 0, "s": 0}

        def new_bank():
            i = cnt["bank"] % NBANK
            cnt["bank"] += 1
            return banks[i], ("bank", i)

        def new_t512():
            i = cnt["t512"] % NT512
            cnt["t512"] += 1
            return t512[i], ("t512", i)

        def new_u():
            i = cnt["u"] % NU
            cnt["u"] += 1
            return ubuf[i], ("u", i), ("uh", i)

        def new_s():
            i = cnt["s"] % NS
            cnt["s"] += 1
            return sbuf_[i], ("sb", i)

        def new_tb():
            i = cnt["tb"] % 2
            cnt["tb"] += 1
            return tbank[i], ("tb", i)

        P.dma("pool", "const", lambda e: e.dma_start(out=vecs[:], in_=vecs_d), writes=[("c", "vecs")])
        P.dma("pool", "const", lambda e: e.dma_start(out=cw[:].rearrange("p a b -> p (a b)"), in_=cw_d),
              writes=[("c", "cw")])
        P.dma("pool", "const", lambda e: e.dma_start(out=fw[:].rearrange("p a b -> p (a b)"), in_=fw_d),
              writes=[("c", "fw")])
        P.dma("pool", "const", lambda e: e.dma_start(out=fb[:], in_=fb_d), writes=[("c", "fb")])
        P.dma("pool", "const", lambda e: e.dma_start(out=gfin[:], in_=gfin_d), writes=[("c", "gfin")])
        P.dma("pool", "const", lambda e: e.dma_start(out=aux[:], in_=aux_d), writes=[("c", "aux")])
        P.op("pool", lambda e: e.memset(mhalf[:], -0.5), writes=[("c", "mhalf")])
        P.op("dve", lambda e: e.tensor_copy(out=identb[:], in_=aux[:, 0:128]),
             reads=[("c", "aux")], writes=[("c", "identb")])
        CONST_KEYS = [("c", n) for n in ("vecs", "cw", "fw", "fb", "gfin", "aux", "mhalf", "identb")]

        per_tile = ["C0", "V0", "B0", "C1", "V1", "B1", "PW",
                    "PP0", "CO0", "GP0", "GC0", "PP1", "CO1", "GP1", "GC1", "WO0", "WO1"]
        ffn_units = [f"U{t}{u}" for u in range(6) for t in ("G", "V")] + \
                    [f"WD{hh}{gi}" for hh in range(2) for gi in range(3)]
        sched = ["Z0", "Z1"]
        Gof = {(0, "Z0"): 0, (0, "Z1"): 1}
        for ti_ in range(NT):
            for nm in per_tile:
                Gof[(ti_, nm)] = len(sched)
                sched.append(nm)
            for nm in ffn_units:
                Gof[(ti_, nm)] = len(sched)
                sched.append(nm)
            if ti_ + 1 < NT:
                for nm in ("Z0", "Z1"):
                    Gof[(ti_ + 1, nm)] = len(sched)
                    sched.append(nm)
        seen_units = set()

        def slot_key(G):
            return ("slot", G % R)

        def emit_load(G):
            if G >= len(sched):
                return
            loaded.add(G)
            n = uidx[sched[G]]
            name, src, nk, ncols = units[n]
            slot = slots[G % R]
            semn = f"w{G % R}"
            if name not in seen_units:
                seen_units.add(name)
                dst = slot[:, 0:nk, 0:ncols]
                P.dma("pool", semn, lambda e: e.dma_start(out=dst, in_=src), writes=[slot_key(G)])
                if NT > 1:
                    P.dma("sp", semn, lambda e: e.dma_start(out=scr_d[n], in_=slot[:]),
                          reads=[slot_key(G)], writes=[("scr", n)])
            else:
                P.dma("sp", semn, lambda e: e.dma_start(out=slot[:], in_=scr_d[n]),
                      reads=[("scr", n)], writes=[slot_key(G)])

        loaded = set()
        used_max = [-1]

        def use_unit(ti, name):
            G = Gof[(ti, name)]
            assert G in loaded, (ti, name, G)
            assert G + R > used_max[0], (ti, name, G, used_max[0])
            used_max[0] = max(used_max[0], G)
            return slots[G % R], slot_key(G)

        def done_unit(ti, name):
            emit_load(Gof[(ti, name)] + R)

        def emit_xload(ti):
            if ti >= NT:
                return
            b = ti % 2
            src = x_d[ti * T:(ti + 1) * T, :].rearrange("(s p) d -> p s d", p=128)
            P.dma("sp", f"x{b}", lambda e: e.dma_start(out=xt[b][:], in_=src),
                  writes=[("xt", b, s, h) for s in range(4) for h in range(2)])

        def emit_xstore(ti):
            b = ti % 2
            dst = out_d[ti * T:(ti + 1) * T, :].rearrange("(s p) d -> p s d", p=128)
            return P.dma("sp", f"x{b}", lambda e: e.dma_start(out=dst, in_=xt[b][:]),
                         reads=[("xt", b, s, h) for s in range(4) for h in range(2)])

        def mm_group(bank_ap, bank_key, pairs, reads):
            n = len(pairs)

            def fn(e):
                ins = None
                for i, (l, r) in enumerate(pairs):
                    ins = e.matmul(bank_ap[:], lhsT=l, rhs=r, start=(i == 0), stop=(i == n - 1))
                return ins
            return P.op("pe", fn, reads=reads, writes=[bank_key])

        def norm_sumsq(which, b, s):
            P.op("act", lambda e: e.activation(out=junk[:], in_=xt[b][:, s, :], func=AF.Square,
                                               accum_out=ss[which][:, s:s + 1]),
                 reads=[("xt", b, s, 0), ("xt", b, s, 1)], writes=[("ss", which, s)])

        def norm_rstd(which):
            allk = lambda n: [(n, which, s) for s in range(4)]
            P.op("dve", lambda e: e.tensor_scalar(out=ms[which][:], in0=ss[which][:],
                                                  scalar1=1.0 / D, scalar2=EPS, op0=ALU.mult, op1=ALU.add),
                 reads=allk("ss"), writes=allk("ms"))
            P.op("pool", lambda e: e.tensor_tensor(out=rstd[which][:], in0=ms[which][:], in1=mhalf[:], op=ALU.pow),
                 reads=allk("ms") + [("c", "mhalf")], writes=allk("rstd"))

        def norm_scale(which, b, s):
            P.op("act", lambda e: e.activation(out=xs[:, s, :], in_=xt[b][:, s, :], func=AF.Copy,
                                               scale=rstd[which][:, s:s + 1]),
                 reads=[("xt", b, s, 0), ("xt", b, s, 1), ("rstd", which, s)], writes=[("xs", s)])

        def transposes(hb, gfun, act_only=False):
            for k in range(8):
                tb_ap, tb_key = new_tb()

                def fn(e, k=k, tb_ap=tb_ap):
                    ins = None
                    for s in range(4):
                        ins = e.transpose(out=tb_ap[:, s * 128:(s + 1) * 128],
                                          in_=xs[:, s, k * 128:(k + 1) * 128], identity=identb[:])
                    return ins
                P.op("pe", fn, reads=[("xs", s) for s in range(4)] + [("c", "identb")], writes=[tb_key])
                if act_only or k % 2 == 0:
                    P.op("act", lambda e, k=k, tb_ap=tb_ap: e.activation(out=hT[hb][:, k, :], in_=tb_ap[:],
                                                                         func=AF.Copy, scale=gfun(k)),
                         reads=[tb_key, ("c", "vecs")], writes=[("hT", hb, k)])
                else:
                    P.op("dve", lambda e, k=k, tb_ap=tb_ap: e.tensor_scalar(out=hT[hb][:, k, :], in0=tb_ap[:],
                                                                            scalar1=gfun(k), scalar2=None,
                                                                            op0=ALU.mult),
                         reads=[tb_key, ("c", "vecs")], writes=[("hT", hb, k)])

        def emit_norm1(ti):
            b = ti % 2
            for s in range(4):
                norm_sumsq(0, b, s)
            norm_rstd(0)
            for s in range(4):
                norm_scale(0, b, s)

        def phase_P(ti):
            first = (ti % TILES_PER_SEQ == 0)
            hb = 0
            hTk = [("hT", hb, k) for k in range(8)]
            for c in range(8):
                g = c // 2
                win = POOL_WINDOWS[g]
                uname = "Z0" if c < 4 else "Z1"
                slot, skey = use_unit(ti, uname)
                cc = c % 4
                bk, bkey = new_bank()
                mm_group(bk, bkey, [(slot[:, k, cc * 128:(cc + 1) * 128], hT[hb][:, k, :]) for k in range(8)],
                         reads=[skey] + hTk)
                if cc == 3:
                    done_unit(ti, uname)
                U, ukey, uhkey = new_u()
                if first:
                    P.op("pool", lambda e, U=U: e.memset(U[:, 0:16], 0.0), writes=[uhkey])
                else:
                    P.op("pool", lambda e, U=U, c=c: e.tensor_copy(out=U[:, 0:16], in_=phalo[:, c, :]),
                         reads=[("ph", c)], writes=[uhkey])
                P.op("act", lambda e, U=U, bk=bk: e.activation(out=U[:, 16:16 + T], in_=bk[:], func=AF.Copy),
                     reads=[bkey], writes=[ukey])
                P.op("pool", lambda e, U=U, c=c: e.tensor_copy(out=phalo[:, c, :], in_=U[:, T:T + 16]),
                     reads=[ukey], writes=[("ph", c)])
                src, srckeys = U, [ukey, uhkey]
                off = 1
                for lvl in range(g + 1):
                    S, skey2 = new_s()
                    lo = 2 * off - 1
                    P.op("dve", lambda e, S=S, src=src, off=off, lo=lo: e.tensor_tensor(
                        out=S[:, lo:T + 16], in0=src[:, lo:T + 16], in1=src[:, lo - off:T + 16 - off], op=ALU.add),
                        reads=srckeys, writes=[skey2])
                    src, srckeys = S, [skey2]
                    off *= 2
                P.op("dve", lambda e, S=src, U=U, c=c, win=win: e.scalar_tensor_tensor(
                    out=A_t[:, c, :], in0=S[:, 16:16 + T], scalar=1.0 / win, in1=U[:, 16:16 + T],
                    op0=ALU.mult, op1=ALU.subtract),
                    reads=srckeys + [ukey], writes=[("A", c)])
                if first:
                    nfix = win - 1
                    P.op("dve", lambda e, S=src, nfix=nfix: e.tensor_tensor(
                        out=pfix[:, 0:nfix], in0=S[:, 16:16 + nfix], in1=invcnt[:, 0:nfix], op=ALU.mult),
                        reads=srckeys + [("c", "aux")], writes=[("pfix",)])
                    P.op("dve", lambda e, U=U, c=c, nfix=nfix: e.tensor_tensor(
                        out=A_t[:, c, 0:nfix], in0=pfix[:, 0:nfix], in1=U[:, 16:16 + nfix], op=ALU.subtract),
                        reads=[("pfix",), ukey], writes=[("A", c)])

        def phase_P2(ti):
            slot, skey = use_unit(ti, "PW")
            for g in range(4):
                for o in range(2):
                    c = 2 * g + o
                    bk, bkey = new_bank()
                    mm_group(bk, bkey,
                             [(slot[:, 2 * g + k2, o * 128:(o + 1) * 128], A_t[:, 2 * g + k2, :]) for k2 in range(2)],
                             reads=[skey, ("A", 2 * g), ("A", 2 * g + 1)])
                    if c % 2 == 0:
                        P.op("act", lambda e, bk=bk, c=c: e.activation(out=p2_t[:, c, :], in_=bk[:], func=AF.Copy,
                                                                      scale=psc(c)),
                             reads=[bkey, ("c", "vecs")], writes=[("p2", c)])
                    else:
                        P.op("dve", lambda e, bk=bk, c=c: e.tensor_scalar(out=p2_t[:, c, :], in0=bk[:],
                                                                         scalar1=psc(c), scalar2=None, op0=ALU.mult),
                             reads=[bkey, ("c", "vecs")], writes=[("p2", c)])
            done_unit(ti, "PW")

        def phase_C(ti):
            first = (ti % TILES_PER_SEQ == 0)
            hb = 0
            hTk = [("hT", hb, k) for k in range(8)]
            for j in range(8):
                hh, jj = divmod(j, 4)
                sc, kc = use_unit(ti, f"C{hh}")
                sv, kv = use_unit(ti, f"V{hh}")
                sb_, kb = use_unit(ti, f"B{hh}")
                bc, bckey = new_bank()
                mm_group(bc, bckey, [(sc[:, k, jj * 128:(jj + 1) * 128], hT[hb][:, k, :]) for k in range(8)],
                         reads=[kc] + hTk)
                bv, bvkey = new_bank()
                mm_group(bv, bvkey, [(sv[:, k, jj * 128:(jj + 1) * 128], hT[hb][:, k, :]) for k in range(8)],
                         reads=[kv] + hTk)
                bb, bbkey = new_bank()
                mm_group(bb, bbkey, [(sb_[:, k, jj * 128:(jj + 1) * 128], hT[hb][:, k, :]) for k in range(8)],
                         reads=[kb] + hTk)
                if jj == 3:
                    done_unit(ti, f"C{hh}")
                    done_unit(ti, f"V{hh}")
                    done_unit(ti, f"B{hh}")
                zc, zckey = new_t512()
                P.op("act", lambda e, zc=zc, bc=bc: e.activation(out=zc[:], in_=bc[:], func=AF.Copy),
                     reads=[bckey], writes=[zckey])
                CV, cvkey, cvhkey = new_u()
                if first:
                    P.op("pool", lambda e, CV=CV: e.memset(CV[:, 0:2], 0.0), writes=[cvhkey])
                else:
                    P.op("pool", lambda e, CV=CV, j=j: e.tensor_copy(out=CV[:, 0:2], in_=chalo[:, j, :]),
                         reads=[("ch", j)], writes=[cvhkey])
                P.op("dve", lambda e, CV=CV, bv=bv, zc=zc: e.tensor_tensor(out=CV[:, 2:2 + T], in0=bv[:], in1=zc[:],
                                                                           op=ALU.mult),
                     reads=[bvkey, zckey], writes=[cvkey])
                P.op("pool", lambda e, CV=CV, j=j: e.tensor_copy(out=chalo[:, j, :], in_=CV[:, T:T + 2]),
                     reads=[cvkey], writes=[("ch", j)])
                acc, acckey = new_t512()
                P.op("act", lambda e, acc=acc, CV=CV, j=j: e.activation(out=acc[:], in_=CV[:, 2:2 + T], func=AF.Copy,
                                                                       scale=cw[:, 2, j:j + 1]),
                     reads=[cvkey, ("c", "cw")], writes=[acckey])
                P.op("dve", lambda e, acc=acc, CV=CV, j=j: e.scalar_tensor_tensor(
                    out=acc[:], in0=CV[:, 1:1 + T], scalar=cw[:, 1, j:j + 1], in1=acc[:], op0=ALU.mult, op1=ALU.add),
                    reads=[cvkey, cvhkey, acckey, ("c", "cw")], writes=[acckey])
                P.op("dve", lambda e, acc=acc, CV=CV, j=j: e.scalar_tensor_tensor(
                    out=acc[:], in0=CV[:, 0:T], scalar=cw[:, 0, j:j + 1], in1=acc[:], op0=ALU.mult, op1=ALU.add),
                    reads=[cvkey, cvhkey, acckey, ("c", "cw")], writes=[acckey])
                P.op("dve", lambda e, acc=acc, bb=bb, j=j: e.tensor_tensor(out=cv_t[:, j, :], in0=bb[:], in1=acc[:],
                                                                          op=ALU.mult),
                     reads=[bbkey, acckey], writes=[("cv", j)])

        def phase_M(ti):
            hb = 0
            hTk = [("hT", hb, k) for k in range(8)]
            for oc in range(8):
                hh, oo = divmod(oc, 4)
                spp, kpp = use_unit(ti, f"PP{hh}")
                sco, kco = use_unit(ti, f"CO{hh}")
                sgp_, kgp = use_unit(ti, f"GP{hh}")
                sgc_, kgc = use_unit(ti, f"GC{hh}")
                col = slice(oo * 128, (oo + 1) * 128)
                bgp, bgpkey = new_bank()
                mm_group(bgp, bgpkey, [(sgp_[:, k, col], hT[hb][:, k, :]) for k in range(8)], reads=[kgp] + hTk)
                bgc, bgckey = new_bank()
                mm_group(bgc, bgckey, [(sgc_[:, k, col], hT[hb][:, k, :]) for k in range(8)], reads=[kgc] + hTk)
                byp, bypkey = new_bank()
                mm_group(byp, bypkey, [(spp[:, k, col], p2_t[:, k, :]) for k in range(8)],
                         reads=[kpp] + [("p2", k) for k in range(8)])
                byc, byckey = new_bank()
                mm_group(byc, byckey, [(sco[:, k, col], cv_t[:, k, :]) for k in range(8)],
                         reads=[kco] + [("cv", k) for k in range(8)])
                if oo == 3:
                    for nm in ("PP", "CO", "GP", "GC"):
                        done_unit(ti, f"{nm}{hh}")
                s1, s1key = new_t512()
                s2, s2key = new_t512()
                P.op("act", lambda e, s1=s1, bgp=bgp: e.activation(out=s1[:], in_=bgp[:], func=AF.Sigmoid),
                     reads=[bgpkey], writes=[s1key])
                P.op("act", lambda e, s2=s2, bgc=bgc: e.activation(out=s2[:], in_=bgc[:], func=AF.Sigmoid),
                     reads=[bgckey], writes=[s2key])
                P.op("dve", lambda e, s1=s1, byp=byp: e.tensor_tensor(out=s1[:], in0=byp[:], in1=s1[:], op=ALU.mult),
                     reads=[bypkey, s1key], writes=[s1key])
                P.op("dve", lambda e, s2=s2, byc=byc: e.tensor_tensor(out=s2[:], in0=byc[:], in1=s2[:], op=ALU.mult),
                     reads=[byckey, s2key], writes=[s2key])
                P.op("dve", lambda e, s1=s1, s2=s2, oc=oc: e.tensor_tensor(out=A_t[:, oc, :], in0=s1[:], in1=s2[:],
                                                                          op=ALU.add),
                     reads=[s1key, s2key], writes=[("A", oc)])

        def phase_O(ti):
            b = ti % 2
            for hh in range(2):
                swo, kwo = use_unit(ti, f"WO{hh}")
                for s in range(4):
                    bk, bkey = new_bank()
                    mm_group(bk, bkey, [(A_t[:, k, s * 128:(s + 1) * 128], swo[:, k, :]) for k in range(8)],
                             reads=[kwo] + [("A", k) for k in range(8)])
                    P.op("dve", lambda e, bk=bk, s=s, hh=hh, b=b: e.tensor_tensor(
                        out=xt[b][:, s, hh * 512:(hh + 1) * 512], in0=bk[:], in1=xt[b][:, s, hh * 512:(hh + 1) * 512],
                        op=ALU.add),
                        reads=[bkey, ("xt", b, s, hh)], writes=[("xt", b, s, hh)])
                    if hh == 1:
                        norm_sumsq(1, b, s)
                done_unit(ti, f"WO{hh}")
            norm_rstd(1)
            for s in range(4):
                norm_scale(1, b, s)

        def phase_F(ti):
            first = (ti % TILES_PER_SEQ == 0)
            h2k = [("hT", 1, k) for k in range(8)]
            if first:
                P.op("pool", lambda e: e.memset(fH[:].rearrange("p a b -> p (a b)"), 0.0), writes=[("fH",)])
            else:
                fhk = [("fh", q) for q in range(44)]
                P.op("pool", lambda e: e.tensor_tensor(out=fH[:, :, 0], in0=fw[:, 1, :], in1=fhalo[:, :, 1],
                                                       op=ALU.mult),
                     reads=[("c", "fw")] + fhk, writes=[("fH",)])
                P.op("pool", lambda e: e.tensor_tensor(out=ftmp[:], in0=fw[:, 0, :], in1=fhalo[:, :, 0],
                                                       op=ALU.mult),
                     reads=[("c", "fw")] + fhk, writes=[("ftmp",)])
                P.op("pool", lambda e: e.tensor_tensor(out=fH[:, :, 0], in0=fH[:, :, 0], in1=ftmp[:], op=ALU.add),
                     reads=[("ftmp",), ("fH",)], writes=[("fH",)])
                P.op("pool", lambda e: e.tensor_tensor(out=fH[:, :, 1], in0=fw[:, 0, :], in1=fhalo[:, :, 1],
                                                       op=ALU.mult),
                     reads=[("c", "fw")] + fhk, writes=[("fH",)])
            pend = [None]

            def flush_pair():
                if pend[0] is None:
                    return
                ag, agkey, av, avkey, jp = pend[0]
                pend[0] = None
                P.op("act", lambda e, ag=ag: e.activation(out=ag[:], in_=ag[:], func=AF.Silu),
                     reads=[agkey], writes=[agkey])
                P.op("pool", lambda e, ag=ag, av=av, jp=jp: e.tensor_tensor(out=aT_t[:, jp, :], in0=ag[:], in1=av[:],
                                                                           op=ALU.mult),
                     reads=[agkey, avkey], writes=[("aT", jp)])

            for j in range(NJ):
                u, jj = divmod(j, 4)
                sg_, kg_ = use_unit(ti, f"UG{u}")
                sv_, kv_ = use_unit(ti, f"UV{u}")
                col = slice(jj * 128, (jj + 1) * 128)
                bg, bgkey = new_bank()
                mm_group(bg, bgkey, [(sg_[:, k, col], hT[1][:, k, :]) for k in range(8)], reads=[kg_] + h2k)
                bv, bvkey = new_bank()
                mm_group(bv, bvkey, [(sv_[:, k, col], hT[1][:, k, :]) for k in range(8)], reads=[kv_] + h2k)
                if jj == 3 or j == NJ - 1:
                    done_unit(ti, f"UG{u}")
                    done_unit(ti, f"UV{u}")
                accs = []
                for (bkx, bkxkey, q) in ((bg, bgkey, j), (bv, bvkey, NJ + j)):
                    a, akey = new_t512()
                    P.op("act", lambda e, a=a, bkx=bkx, q=q: e.activation(
                        out=a[:], in_=bkx[:], func=AF.Identity, scale=fw[:, 2, q:q + 1], bias=fb[:, q:q + 1]),
                        reads=[bkxkey, ("c", "fw"), ("c", "fb")], writes=[akey])
                    accs.append((a, akey, bkx, bkxkey, q))
                flush_pair()
                for (a, akey, bkx, bkxkey, q) in accs:
                    P.op("dve", lambda e, a=a, bkx=bkx, q=q: e.scalar_tensor_tensor(
                        out=a[:, 1:T], in0=bkx[:, 0:T - 1], scalar=fw[:, 1, q:q + 1], in1=a[:, 1:T],
                        op0=ALU.mult, op1=ALU.add),
                        reads=[bkxkey, akey, ("c", "fw")], writes=[akey])
                for (a, akey, bkx, bkxkey, q) in accs:
                    P.op("dve", lambda e, a=a, bkx=bkx, q=q: e.scalar_tensor_tensor(
                        out=a[:, 2:T], in0=bkx[:, 0:T - 2], scalar=fw[:, 0, q:q + 1], in1=a[:, 2:T],
                        op0=ALU.mult, op1=ALU.add),
                        reads=[bkxkey, akey, ("c", "fw")], writes=[akey])
                for (a, akey, bkx, bkxkey, q) in accs:
                    P.op("dve", lambda e, a=a, q=q: e.tensor_tensor(out=a[:, 0:2], in0=a[:, 0:2], in1=fH[:, q, :],
                                                                    op=ALU.add),
                         reads=[akey, ("fH",)], writes=[akey])
                    P.op("dve", lambda e, bkx=bkx, q=q: e.tensor_copy(out=fhalo[:, q, :], in_=bkx[:, T - 2:T]),
                         reads=[bkxkey, ("fH",)], writes=[("fh", q)])
                (ag, agkey, _, _, _), (av, avkey, _, _, _) = accs
                pend[0] = (ag, agkey, av, avkey, j)
            flush_pair()

        def phase_D(ti, mid_hook=None):
            b = ti % 2
            for hh in range(2):
                bks = [new_bank() for _ in range(4)]
                for gi, (k0, k1) in enumerate(KG):
                    swd, kwd = use_unit(ti, f"WD{hh}{gi}")

                    def fn(e, swd=swd, k0=k0, k1=k1, bks=bks):
                        ins = None
                        for k in range(k0, k1):
                            for s in range(4):
                                ins = e.matmul(bks[s][0][:], lhsT=aT_t[:, k, s * 128:(s + 1) * 128],
                                               rhs=swd[:, k - k0, :], start=(k == 0), stop=(k == NJ - 1))
                        return ins
                    P.op("pe", fn, reads=[kwd] + [("aT", k) for k in range(k0, k1)], writes=[bk[1] for bk in bks])
                    done_unit(ti, f"WD{hh}{gi}")
                for s in range(4):
                    bk, bkey = bks[s]
                    P.op("dve", lambda e, bk=bk, s=s, hh=hh, b=b: e.tensor_tensor(
                        out=xt[b][:, s, hh * 512:(hh + 1) * 512], in0=bk[:], in1=xt[b][:, s, hh * 512:(hh + 1) * 512],
                        op=ALU.add),
                        reads=[bkey, ("xt", b, s, hh)], writes=[("xt", b, s, hh)])
                    if hh == 1:
                        norm_sumsq(2, b, s)
            if mid_hook is not None:
                mid_hook()
            norm_rstd(2)
            for s in range(4):
                P.op("dve", lambda e, s=s, b=b: e.scalar_tensor_tensor(
                    out=xt[b][:, s, :], in0=xt[b][:, s, :], scalar=rstd[2][:, s:s + 1], in1=gfin[:],
                    op0=ALU.mult, op1=ALU.mult),
                    reads=[("xt", b, s, 0), ("xt", b, s, 1), ("rstd", 2, s), ("c", "gfin")],
                    writes=[("xt", b, s, 0), ("xt", b, s, 1)])

        emit_xload(0)
        emit_xload(1)
        for G in range(R):
            emit_load(G)
        emit_norm1(0)
        transposes(0, g1)
        phase_P(0)

        last_store = [None, None]
        for ti in range(NT):
            b = ti % 2
            if ti + 1 < NT:
                emit_norm1(ti + 1)
            phase_C(ti)
            phase_P2(ti)
            phase_M(ti)
            if ti + 1 < NT:
                transposes(0, g1)
            phase_O(ti)
            transposes(1, g2)
            phase_F(ti)
            phase_D(ti, (lambda ti=ti: phase_P(ti + 1)) if ti + 1 < NT else None)
            last_store[b] = emit_xstore(ti)
            emit_xload(ti + 2)

        P.wait_only("sp", [ev for ev in last_store if ev is not None])

        block = E(nc.Block())

        def replay(stream, eng):
            for waits, fn, inc_sem, inc_amt in stream.ops:
                for sem, val in waits:
                    eng.wait_ge(sems[sem], val)
                if fn is not None:
                    ins = fn(eng)
                    ins.then_inc(sems[inc_sem], inc_amt)

        @block.tensor
        def _(eng):
            replay(P.streams["pe"], eng)

        @block.scalar
        def _(eng):
            replay(P.streams["act"], eng)

        @block.vector
        def _(eng):
            replay(P.streams["dve"], eng)

        @block.gpsimd
        def _(eng):
            replay(P.streams["pool"], eng)

        @block.sync
        def _(eng):
            replay(P.streams["sp"], eng)

    return nc


def _get_program():
    if "nc" not in _CACHE:
        _CACHE["nc"] = build_program()
    return _CACHE["nc"]


def _chunked(v, n):
    return np.ascontiguousarray(np.asarray(v, dtype=np.float32).reshape(n, 128).T)


def make_shared(I):
    f = lambda a: np.ascontiguousarray(np.asarray(a, dtype=np.float32))
    vecs = np.concatenate([_chunked(f(I["norm_mix"])[0], 8), _chunked(f(I["norm_ffn"])[0], 8),
                           _chunked(f(I["pool_scale"])[0], 8)], axis=1)
    cw = np.stack([_chunked(f(I["conv_w"])[0, k], 8) for k in range(3)], axis=1).reshape(128, 24)
    fw = np.stack([_chunked(f(I["ffn_conv_w"])[0, k], 44) for k in range(3)], axis=1).reshape(128, 132)
    fb = _chunked(f(I["ffn_conv_b"])[0], 44)
    gfin = np.ascontiguousarray(np.broadcast_to(f(I["norm_final"])[None, :], (128, D)))
    aux = np.zeros((128, 144), dtype=np.float32)
    aux[:, 0:128] = np.eye(128, dtype=np.float32)
    aux[:, 128:144] = (1.0 / np.arange(1, 17, dtype=np.float64)).astype(np.float32)[None, :]
    return {
        "w_in": f(I["w_in"])[0], "pool_w": f(I["pool_w"])[0].reshape(1024, 256),
        "w_pool_proj": f(I["w_pool_proj"])[0], "w_conv_out": f(I["w_conv_out"])[0], "w_o": f(I["w_o"])[0],
        "w_up": f(I["w_up"])[0], "w_down": f(I["w_down"])[0],
        "vecs": np.ascontiguousarray(vecs), "cw": np.ascontiguousarray(cw), "fw": np.ascontiguousarray(fw),
        "fb": fb, "gfin": gfin, "aux": aux,
    }


def kernel(x, norm_mix, w_in, pool_w, pool_scale, w_pool_proj, conv_w, w_conv_out, w_o,
           norm_ffn, w_up, ffn_conv_w, ffn_conv_b, w_down, norm_final):
    x = np.ascontiguousarray(np.asarray(x, dtype=np.float32))
    shared = make_shared(dict(norm_mix=norm_mix, w_in=w_in, pool_w=pool_w, pool_scale=pool_scale,
                              w_pool_proj=w_pool_proj, conv_w=conv_w, w_conv_out=w_conv_out, w_o=w_o,
                              norm_ffn=norm_ffn, w_up=w_up, ffn_conv_w=ffn_conv_w, ffn_conv_b=ffn_conv_b,
                              w_down=w_down, norm_final=norm_final))
    xf = x.reshape(N_CORES, TOK_PER_CORE, D)
    in_maps = [dict(shared, x=np.ascontiguousarray(xf[c])) for c in range(N_CORES)]
    nc = _get_program()
    res = run_bass_kernel_spmd(nc, in_maps, core_ids=list(range(N_CORES)))
    out = np.stack([np.asarray(r["out"], dtype=np.float32) for r in res.results], axis=0)
    return out.reshape(x.shape)
```

```python
from contextlib import ExitStack

import os
import numpy as np
import concourse.bass as bass
import concourse.mybir as mybir
from concourse.bass_utils import run_bass_kernel_spmd

F32 = mybir.dt.float32
BF16 = mybir.dt.bfloat16
AF = mybir.ActivationFunctionType
ALU = mybir.AluOpType

N_CORES = 8
D = 1024
SEQ = 2048
TOK_PER_CORE = 4096
T = 512
NT = TOK_PER_CORE // T
TILES_PER_SEQ = SEQ // T
D_FF = 2816
NJ = D_FF // 128
EPS = 1e-6
R = 8
NBANK = 6
NT512 = 7
NU = 3
NS = 3
POOL_WINDOWS = (2, 4, 8, 16)
DBG = os.environ.get("KDBG", "")
MUL_ENG = "dve" if "dvemul" in DBG else "pool"


_CACHE = {}


class Stream:
    def __init__(self, name):
        self.name = name
        self.count = 0
        self.ops = []
        self.waited = {}


class Prog:
    def __init__(self):
        self.streams = {n: Stream(n) for n in ("pe", "act", "dve", "pool", "sp")}
        self.res = {}
        self.semval = {}

    def _deps(self, reads, writes):
        evs = []
        for k in reads:
            r = self.res.get(k)
            if r and r[0] is not None:
                evs.append(r[0])
        for k in writes:
            r = self.res.get(k)
            if r:
                if r[0] is not None:
                    evs.append(r[0])
                evs.extend(r[1])
        return evs

    def _commit(self, ev, reads, writes):
        for k in reads:
            r = self.res.setdefault(k, [None, []])
            r[1].append(ev)
        for k in writes:
            self.res[k] = [ev, []]

    def _waits(self, st, evs):
        need = {}
        for sem, val in evs:
            if st.name == "pe" and sem == "pe":
                continue
            if st.waited.get(sem, 0) >= val:
                continue
            if need.get(sem, 0) < val:
                need[sem] = val
        for sem, val in need.items():
            st.waited[sem] = val
        return list(need.items())

    def op(self, eng, fn, reads=(), writes=()):
        st = self.streams[eng]
        waits = self._waits(st, self._deps(reads, writes))
        st.count += 1
        ev = (eng, st.count)
        st.ops.append((waits, fn, eng, 1))
        self._commit(ev, reads, writes)
        return ev

    def dma(self, queue, sem, fn, reads=(), writes=()):
        st = self.streams[queue]
        evs = self._deps(reads, writes)
        prev = self.semval.get(sem, 0)
        if prev:
            evs.append((sem, prev))
        waits = self._waits(st, evs)
        val = prev + 16
        self.semval[sem] = val
        ev = (sem, val)
        st.ops.append((waits, fn, sem, 16))
        self._commit(ev, reads, writes)
        return ev

    def wait_only(self, eng, evs):
        st = self.streams[eng]
        waits = self._waits(st, evs)
        if waits:
            st.ops.append((waits, None, None, 0))


def build_program(NT=NT, TOK=TOK_PER_CORE, STOP=99):
    nc = bass.Bass("TRN2", target_bir_lowering=False)
    dt = nc.dram_tensor

    x_d = dt("x", [TOK, D], F32, kind="ExternalInput").ap()
    out_d = dt("out", [TOK, D], F32, kind="ExternalOutput").ap()
    w_in_d = dt("w_in", [D, 6 * D], F32, kind="ExternalInput").ap()
    pool_w_d = dt("pool_w", [1024, 256], F32, kind="ExternalInput").ap()
    w_pp_d = dt("w_pool_proj", [D, D], F32, kind="ExternalInput").ap()
    w_co_d = dt("w_conv_out", [D, D], F32, kind="ExternalInput").ap()
    w_o_d = dt("w_o", [D, D], F32, kind="ExternalInput").ap()
    w_up_d = dt("w_up", [D, 2 * D_FF], F32, kind="ExternalInput").ap()
    w_dn_d = dt("w_down", [D_FF, D], F32, kind="ExternalInput").ap()
    vecs_d = dt("vecs", [128, 24], F32, kind="ExternalInput").ap()
    cw_d = dt("cw", [128, 24], F32, kind="ExternalInput").ap()
    fw_d = dt("fw", [128, 132], F32, kind="ExternalInput").ap()
    fb_d = dt("fb", [128, 44], F32, kind="ExternalInput").ap()
    gfin_d = dt("gfin", [128, D], F32, kind="ExternalInput").ap()
    aux_d = dt("aux", [128, 144], F32, kind="ExternalInput").ap()

    wv_in = w_in_d.rearrange("(k p) n -> p k n", p=128)
    wv_pw = pool_w_d.rearrange("(gk p) c -> p gk c", p=128)
    wv_pp = w_pp_d.rearrange("(k p) n -> p k n", p=128)
    wv_co = w_co_d.rearrange("(k p) n -> p k n", p=128)
    wv_o = w_o_d.rearrange("(k p) n -> p k n", p=128)
    wv_up = w_up_d.rearrange("(k p) n -> p k n", p=128)
    wv_dn = w_dn_d.rearrange("(k p) n -> p k n", p=128)

    units = []

    def add_unit(name, src, nk, ncols):
        units.append((name, src, nk, ncols))

    add_unit("Z0", wv_in[:, :, 0:512], 8, 512)
    add_unit("Z1", wv_in[:, :, 512:1024], 8, 512)
    add_unit("PW", wv_pw, 8, 256)
    for hh in range(2):
        add_unit(f"C{hh}", wv_in[:, :, 2048 + hh * 512:2048 + (hh + 1) * 512], 8, 512)
        add_unit(f"V{hh}", wv_in[:, :, 3072 + hh * 512:3072 + (hh + 1) * 512], 8, 512)
        add_unit(f"B{hh}", wv_in[:, :, 1024 + hh * 512:1024 + (hh + 1) * 512], 8, 512)
    for hh in range(2):
        add_unit(f"PP{hh}", wv_pp[:, :, hh * 512:(hh + 1) * 512], 8, 512)
        add_unit(f"CO{hh}", wv_co[:, :, hh * 512:(hh + 1) * 512], 8, 512)
        add_unit(f"GP{hh}", wv_in[:, :, 4096 + hh * 512:4096 + (hh + 1) * 512], 8, 512)
        add_unit(f"GC{hh}", wv_in[:, :, 5120 + hh * 512:5120 + (hh + 1) * 512], 8, 512)
    for hh in range(2):
        add_unit(f"WO{hh}", wv_o[:, :, hh * 512:(hh + 1) * 512], 8, 512)
    for u in range(6):
        nc_ = 512 if u < 5 else 256
        add_unit(f"UG{u}", wv_up[:, :, u * 512:u * 512 + nc_], 8, nc_)
        add_unit(f"UV{u}", wv_up[:, :, D_FF + u * 512:D_FF + u * 512 + nc_], 8, nc_)
    KG = [(0, 8), (8, 16), (16, 22)]
    for hh in range(2):
        for gi, (k0, k1) in enumerate(KG):
            add_unit(f"WD{hh}{gi}", wv_dn[:, k0:k1, hh * 512:(hh + 1) * 512], k1 - k0, 512)
    NUNITS = len(units)
    uidx = {u[0]: i for i, u in enumerate(units)}

    scr_d = dt("wscr", [NUNITS, 128, 8, 512], BF16, kind="Internal").ap()

    P = Prog()
    _CACHE["prog"] = P
    es = ExitStack()
    E = es.enter_context
    with es:
        xt = [E(nc.sbuf_tensor(f"xt{b}", [128, 4, D], F32)) for b in range(2)]
        xs = E(nc.sbuf_tensor("xs0", [128, 4, D], BF16))
        hT = [E(nc.sbuf_tensor(f"hT{i}", [128, 8, T], BF16)) for i in range(2)]
        A_t = E(nc.sbuf_tensor("A_t", [128, 8, T], BF16))
        p2_t = E(nc.sbuf_tensor("p2_t", [128, 8, T], BF16))
        cv_t = E(nc.sbuf_tensor("cv_t", [128, 8, T], BF16))
        aT_t = E(nc.sbuf_tensor("aT_t", [128, NJ, T], BF16))
        slots = [E(nc.sbuf_tensor(f"slot{r}", [128, 8, 512], BF16)) for r in range(R)]
        t512 = [E(nc.sbuf_tensor(f"t512_{i}", [128, T], F32)) for i in range(NT512)]
        ubuf = [E(nc.sbuf_tensor(f"ubuf{i}", [128, T + 16], F32)) for i in range(NU)]
        sbuf_ = [E(nc.sbuf_tensor(f"sbuf{i}", [128, T + 16], F32)) for i in range(NS)]
        junk = E(nc.sbuf_tensor("junk", [128, D], BF16))
        vecs = E(nc.sbuf_tensor("vecs_t", [128, 24], F32))
        cw = E(nc.sbuf_tensor("cw_t", [128, 3, 8], F32))
        fw = E(nc.sbuf_tensor("fw_t", [128, 3, 44], F32))
        fb = E(nc.sbuf_tensor("fb_t", [128, 44], F32))
        gfin = E(nc.sbuf_tensor("gfin_t", [128, D], F32))
        aux = E(nc.sbuf_tensor("aux_t", [128, 144], F32))
        identb = E(nc.sbuf_tensor("identb", [128, 128], BF16))
        mhalf = E(nc.sbuf_tensor("mhalf", [128, 4], F32))
        ss = [E(nc.sbuf_tensor(f"ss{i}", [128, 4], F32)) for i in range(3)]
        ms = [E(nc.sbuf_tensor(f"ms{i}", [128, 4], F32)) for i in range(3)]
        rstd = [E(nc.sbuf_tensor(f"rstd{i}", [128, 4], F32)) for i in range(3)]
        phalo = E(nc.sbuf_tensor("phalo", [128, 8, 16], F32))
        chalo = E(nc.sbuf_tensor("chalo", [128, 8, 2], F32))
        fhalo = E(nc.sbuf_tensor("fhalo", [128, 44, 2], F32))
        fH = E(nc.sbuf_tensor("fH", [128, 44, 2], F32))
        ftmp = E(nc.sbuf_tensor("ftmp", [128, 44], F32))
        pfix = E(nc.sbuf_tensor("pfix", [128, 16], F32))
        banks = [E(nc.psum_tensor(f"bank{i}", [128, 512], F32)) for i in range(NBANK)]
        tbank = [E(nc.psum_tensor(f"tbank{i}", [128, 512], BF16)) for i in range(2)]
        sem_names = ["pe", "act", "dve", "pool", "const", "x0", "x1"] + [f"w{r}" for r in range(R)]
        sems = {n: E(nc.semaphore("sem_" + n)) for n in sem_names}

        g1 = lambda k: vecs[:, k:k + 1]
        g2 = lambda k: vecs[:, 8 + k:9 + k]
        psc = lambda k: vecs[:, 16 + k:17 + k]
        invcnt = aux[:, 128:144]

        cnt = {"bank": 0, "t512": 0, "u": 0, "tb": 0, "s": 0}

        def new_bank():
            i = cnt["bank"] % NBANK
            cnt["bank"] += 1
            return banks[i], ("bank", i)

        def new_t512():
            i = cnt["t512"] % NT512
            cnt["t512"] += 1
            return t512[i], ("t512", i)

        def new_u():
            i = cnt["u"] % NU
            cnt["u"] += 1
            return ubuf[i], ("u", i), ("uh", i)

        def new_s():
            i = cnt["s"] % NS
            cnt["s"] += 1
            return sbuf_[i], ("sb", i)

        def new_tb():
            i = cnt["tb"] % 2
            cnt["tb"] += 1
            return tbank[i], ("tb", i)

        P.dma("pool", "const", lambda e: e.dma_start(out=vecs[:], in_=vecs_d), writes=[("c", "vecs")])
        P.dma("pool", "const", lambda e: e.dma_start(out=cw[:].rearrange("p a b -> p (a b)"), in_=cw_d),
              writes=[("c", "cw")])
        P.dma("pool", "const", lambda e: e.dma_start(out=fw[:].rearrange("p a b -> p (a b)"), in_=fw_d),
              writes=[("c", "fw")])
        P.dma("pool", "const", lambda e: e.dma_start(out=fb[:], in_=fb_d), writes=[("c", "fb")])
        P.dma("pool", "const", lambda e: e.dma_start(out=gfin[:], in_=gfin_d), writes=[("c", "gfin")])
        P.dma("pool", "const", lambda e: e.dma_start(out=aux[:], in_=aux_d), writes=[("c", "aux")])
        P.op("pool", lambda e: e.memset(mhalf[:], -0.5), writes=[("c", "mhalf")])
        P.op("dve", lambda e: e.tensor_copy(out=identb[:], in_=aux[:, 0:128]),
             reads=[("c", "aux")], writes=[("c", "identb")])
        CONST_KEYS = [("c", n) for n in ("vecs", "cw", "fw", "fb", "gfin", "aux", "mhalf", "identb")]

        per_tile = ["C0", "V0", "B0", "C1", "V1", "B1", "PW",
                    "PP0", "CO0", "GP0", "GC0", "PP1", "CO1", "GP1", "GC1", "WO0", "WO1"]
        ffn_units = [f"U{t}{u}" for u in range(6) for t in ("G", "V")] + \
                    [f"WD{hh}{gi}" for hh in range(2) for gi in range(3)]
        sched = ["Z0", "Z1"]
        Gof = {(0, "Z0"): 0, (0, "Z1"): 1}
        for ti_ in range(NT):
            for nm in per_tile:
                Gof[(ti_, nm)] = len(sched)
                sched.append(nm)
            for nm in ffn_units:
                Gof[(ti_, nm)] = len(sched)
                sched.append(nm)
            if ti_ + 1 < NT:
                for nm in ("Z0", "Z1"):
                    Gof[(ti_ + 1, nm)] = len(sched)
                    sched.append(nm)
        seen_units = set()

        def slot_key(G):
            return ("slot", G % R)

        def emit_load(G):
            if G >= len(sched):
                return
            loaded.add(G)
            n = uidx[sched[G]]
            name, src, nk, ncols = units[n]
            slot = slots[G % R]
            semn = f"w{G % R}"
            if name not in seen_units:
                seen_units.add(name)
                dst = slot[:, 0:nk, 0:ncols]
                P.dma("pool", semn, lambda e: e.dma_start(out=dst, in_=src), writes=[slot_key(G)])
                if NT > 1:
                    P.dma("sp", semn, lambda e: e.dma_start(out=scr_d[n], in_=slot[:]),
                          reads=[slot_key(G)], writes=[("scr", n)])
            else:
                P.dma("sp", semn, lambda e: e.dma_start(out=slot[:], in_=scr_d[n]),
                      reads=[("scr", n)], writes=[slot_key(G)])

        loaded = set()
        used_max = [-1]

        def use_unit(ti, name):
            G = Gof[(ti, name)]
            assert G in loaded, (ti, name, G)
            assert G + R > used_max[0], (ti, name, G, used_max[0])
            used_max[0] = max(used_max[0], G)
            return slots[G % R], slot_key(G)

        def done_unit(ti, name):
            emit_load(Gof[(ti, name)] + R)

        def emit_xload(ti):
            if ti >= NT:
                return
            b = ti % 2
            src = x_d[ti * T:(ti + 1) * T, :].rearrange("(s p) d -> p s d", p=128)
            P.dma("sp", f"x{b}", lambda e: e.dma_start(out=xt[b][:], in_=src),
                  writes=[("xt", b, s, h) for s in range(4) for h in range(2)])

        def emit_xstore(ti):
            b = ti % 2
            dst = out_d[ti * T:(ti + 1) * T, :].rearrange("(s p) d -> p s d", p=128)
            return P.dma("sp", f"x{b}", lambda e: e.dma_start(out=dst, in_=xt[b][:]),
                         reads=[("xt", b, s, h) for s in range(4) for h in range(2)])

        def mm_group(bank_ap, bank_key, pairs, reads):
            n = len(pairs)

            def fn(e):
                ins = None
                for i, (l, r) in enumerate(pairs):
                    ins = e.matmul(bank_ap[:], lhsT=l, rhs=r, start=(i == 0), stop=(i == n - 1))
                return ins
            return P.op("pe", fn, reads=reads, writes=[bank_key])

        def norm_sumsq(which, b, s):
            P.op("act", lambda e: e.activation(out=junk[:], in_=xt[b][:, s, :], func=AF.Square,
                                               accum_out=ss[which][:, s:s + 1]),
                 reads=[("xt", b, s, 0), ("xt", b, s, 1)], writes=[("ss", which, s)])

        def norm_rstd(which):
            allk = lambda n: [(n, which, s) for s in range(4)]
            P.op("dve", lambda e: e.tensor_scalar(out=ms[which][:], in0=ss[which][:],
                                                  scalar1=1.0 / D, scalar2=EPS, op0=ALU.mult, op1=ALU.add),
                 reads=allk("ss"), writes=allk("ms"))
            P.op("pool", lambda e: e.tensor_tensor(out=rstd[which][:], in0=ms[which][:], in1=mhalf[:], op=ALU.pow),
                 reads=allk("ms") + [("c", "mhalf")], writes=allk("rstd"))

        def norm_scale(which, b, s):
            P.op("act", lambda e: e.activation(out=xs[:, s, :], in_=xt[b][:, s, :], func=AF.Copy,
                                               scale=rstd[which][:, s:s + 1]),
                 reads=[("xt", b, s, 0), ("xt", b, s, 1), ("rstd", which, s)], writes=[("xs", s)])

        def transposes(hb, gfun, act_only=False):
            for k in range(8):
                tb_ap, tb_key = new_tb()

                def fn(e, k=k, tb_ap=tb_ap):
                    ins = None
                    for s in range(4):
                        ins = e.transpose(out=tb_ap[:, s * 128:(s + 1) * 128],
                                          in_=xs[:, s, k * 128:(k + 1) * 128], identity=identb[:])
                    return ins
                P.op("pe", fn, reads=[("xs", s) for s in range(4)] + [("c", "identb")], writes=[tb_key])
                if act_only or k % 2 == 0:
                    P.op("act", lambda e, k=k, tb_ap=tb_ap: e.activation(out=hT[hb][:, k, :], in_=tb_ap[:],
                                                                         func=AF.Copy, scale=gfun(k)),
                         reads=[tb_key, ("c", "vecs")], writes=[("hT", hb, k)])
                else:
                    P.op("dve", lambda e, k=k, tb_ap=tb_ap: e.tensor_scalar(out=hT[hb][:, k, :], in0=tb_ap[:],
                                                                            scalar1=gfun(k), scalar2=None,
                                                                            op0=ALU.mult),
                         reads=[tb_key, ("c", "vecs")], writes=[("hT", hb, k)])

        def emit_norm1(ti):
            b = ti % 2
            for s in range(4):
                norm_sumsq(0, b, s)
            norm_rstd(0)
            for s in range(4):
                norm_scale(0, b, s)

        def phase_P(ti):
            first = (ti % TILES_PER_SEQ == 0)
            hb = 0
            hTk = [("hT", hb, k) for k in range(8)]
            for c in range(8):
                g = c // 2
                win = POOL_WINDOWS[g]
                uname = "Z0" if c < 4 else "Z1"
                slot, skey = use_unit(ti, uname)
                cc = c % 4
                bk, bkey = new_bank()
                mm_group(bk, bkey, [(slot[:, k, cc * 128:(cc + 1) * 128], hT[hb][:, k, :]) for k in range(8)],
                         reads=[skey] + hTk)
                if cc == 3:
                    done_unit(ti, uname)
                U, ukey, uhkey = new_u()
                if first:
                    P.op("pool", lambda e, U=U: e.memset(U[:, 0:16], 0.0), writes=[uhkey])
                else:
                    P.op("pool", lambda e, U=U, c=c: e.tensor_copy(out=U[:, 0:16], in_=phalo[:, c, :]),
                         reads=[("ph", c)], writes=[uhkey])
                P.op("act", lambda e, U=U, bk=bk: e.activation(out=U[:, 16:16 + T], in_=bk[:], func=AF.Copy),
                     reads=[bkey], writes=[ukey])
                P.op("pool", lambda e, U=U, c=c: e.tensor_copy(out=phalo[:, c, :], in_=U[:, T:T + 16]),
                     reads=[ukey], writes=[("ph", c)])
                src, srckeys = U, [ukey, uhkey]
                off = 1
                for lvl in range(g + 1):
                    S, skey2 = new_s()
                    lo = 2 * off - 1
                    P.op("dve", lambda e, S=S, src=src, off=off, lo=lo: e.tensor_tensor(
                        out=S[:, lo:T + 16], in0=src[:, lo:T + 16], in1=src[:, lo - off:T + 16 - off], op=ALU.add),
                        reads=srckeys, writes=[skey2])
                    src, srckeys = S, [skey2]
                    off *= 2
                P.op("dve", lambda e, S=src, U=U, c=c, win=win: e.scalar_tensor_tensor(
                    out=A_t[:, c, :], in0=S[:, 16:16 + T], scalar=1.0 / win, in1=U[:, 16:16 + T],
                    op0=ALU.mult, op1=ALU.subtract),
                    reads=srckeys + [ukey], writes=[("A", c)])
                if first:
                    nfix = win - 1
                    P.op("dve", lambda e, S=src, nfix=nfix: e.tensor_tensor(
                        out=pfix[:, 0:nfix], in0=S[:, 16:16 + nfix], in1=invcnt[:, 0:nfix], op=ALU.mult),
                        reads=srckeys + [("c", "aux")], writes=[("pfix",)])
                    P.op("dve", lambda e, U=U, c=c, nfix=nfix: e.tensor_tensor(
                        out=A_t[:, c, 0:nfix], in0=pfix[:, 0:nfix], in1=U[:, 16:16 + nfix], op=ALU.subtract),
                        reads=[("pfix",), ukey], writes=[("A", c)])

        def phase_P2(ti):
            slot, skey = use_unit(ti, "PW")
            for g in range(4):
                for o in range(2):
                    c = 2 * g + o
                    bk, bkey = new_bank()
                    mm_group(bk, bkey,
                             [(slot[:, 2 * g + k2, o * 128:(o + 1) * 128], A_t[:, 2 * g + k2, :]) for k2 in range(2)],
                             reads=[skey, ("A", 2 * g), ("A", 2 * g + 1)])
                    if c % 2 == 0:
                        P.op("act", lambda e, bk=bk, c=c: e.activation(out=p2_t[:, c, :], in_=bk[:], func=AF.Copy,
                                                                      scale=psc(c)),
                             reads=[bkey, ("c", "vecs")], writes=[("p2", c)])
                    else:
                        P.op("dve", lambda e, bk=bk, c=c: e.tensor_scalar(out=p2_t[:, c, :], in0=bk[:],
                                                                         scalar1=psc(c), scalar2=None, op0=ALU.mult),
                             reads=[bkey, ("c", "vecs")], writes=[("p2", c)])
            done_unit(ti, "PW")

        def phase_C(ti):
            first = (ti % TILES_PER_SEQ == 0)
            hb = 0
            hTk = [("hT", hb, k) for k in range(8)]
            for j in range(8):
                hh, jj = divmod(j, 4)
                sc, kc = use_unit(ti, f"C{hh}")
                sv, kv = use_unit(ti, f"V{hh}")
                sb_, kb = use_unit(ti, f"B{hh}")
                bc, bckey = new_bank()
                mm_group(bc, bckey, [(sc[:, k, jj * 128:(jj + 1) * 128], hT[hb][:, k, :]) for k in range(8)],
                         reads=[kc] + hTk)
                bv, bvkey = new_bank()
                mm_group(bv, bvkey, [(sv[:, k, jj * 128:(jj + 1) * 128], hT[hb][:, k, :]) for k in range(8)],
                         reads=[kv] + hTk)
                bb, bbkey = new_bank()
                mm_group(bb, bbkey, [(sb_[:, k, jj * 128:(jj + 1) * 128], hT[hb][:, k, :]) for k in range(8)],
                         reads=[kb] + hTk)
                if jj == 3:
                    done_unit(ti, f"C{hh}")
                    done_unit(ti, f"V{hh}")
                    done_unit(ti, f"B{hh}")
                zc, zckey = new_t512()
                P.op("act", lambda e, zc=zc, bc=bc: e.activation(out=zc[:], in_=bc[:], func=AF.Copy),
                     reads=[bckey], writes=[zckey])
                CV, cvkey, cvhkey = new_u()
                if first:
                    P.op("pool", lambda e, CV=CV: e.memset(CV[:, 0:2], 0.0), writes=[cvhkey])
                else:
                    P.op("pool", lambda e, CV=CV, j=j: e.tensor_copy(out=CV[:, 0:2], in_=chalo[:, j, :]),
                         reads=[("ch", j)], writes=[cvhkey])
                P.op("dve", lambda e, CV=CV, bv=bv, zc=zc: e.tensor_tensor(out=CV[:, 2:2 + T], in0=bv[:], in1=zc[:],
                                                                           op=ALU.mult),
                     reads=[bvkey, zckey], writes=[cvkey])
                P.op("pool", lambda e, CV=CV, j=j: e.tensor_copy(out=chalo[:, j, :], in_=CV[:, T:T + 2]),
                     reads=[cvkey], writes=[("ch", j)])
                acc, acckey = new_t512()
                P.op("act", lambda e, acc=acc, CV=CV, j=j: e.activation(out=acc[:], in_=CV[:, 2:2 + T], func=AF.Copy,
                                                                       scale=cw[:, 2, j:j + 1]),
                     reads=[cvkey, ("c", "cw")], writes=[acckey])
                P.op("dve", lambda e, acc=acc, CV=CV, j=j: e.scalar_tensor_tensor(
                    out=acc[:], in0=CV[:, 1:1 + T], scalar=cw[:, 1, j:j + 1], in1=acc[:], op0=ALU.mult, op1=ALU.add),
                    reads=[cvkey, cvhkey, acckey, ("c", "cw")], writes=[acckey])
                P.op("dve", lambda e, acc=acc, CV=CV, j=j: e.scalar_tensor_tensor(
                    out=acc[:], in0=CV[:, 0:T], scalar=cw[:, 0, j:j + 1], in1=acc[:], op0=ALU.mult, op1=ALU.add),
                    reads=[cvkey, cvhkey, acckey, ("c", "cw")], writes=[acckey])
                P.op("dve", lambda e, acc=acc, bb=bb, j=j: e.tensor_tensor(out=cv_t[:, j, :], in0=bb[:], in1=acc[:],
                                                                          op=ALU.mult),
                     reads=[bbkey, acckey], writes=[("cv", j)])

        def phase_M(ti):
            hb = 0
            hTk = [("hT", hb, k) for k in range(8)]
            for oc in range(8):
                hh, oo = divmod(oc, 4)
                spp, kpp = use_unit(ti, f"PP{hh}")
                sco, kco = use_unit(ti, f"CO{hh}")
                sgp_, kgp = use_unit(ti, f"GP{hh}")
                sgc_, kgc = use_unit(ti, f"GC{hh}")
                col = slice(oo * 128, (oo + 1) * 128)
                bgp, bgpkey = new_bank()
                mm_group(bgp, bgpkey, [(sgp_[:, k, col], hT[hb][:, k, :]) for k in range(8)], reads=[kgp] + hTk)
                bgc, bgckey = new_bank()
                mm_group(bgc, bgckey, [(sgc_[:, k, col], hT[hb][:, k, :]) for k in range(8)], reads=[kgc] + hTk)
                byp, bypkey = new_bank()
                mm_group(byp, bypkey, [(spp[:, k, col], p2_t[:, k, :]) for k in range(8)],
                         reads=[kpp] + [("p2", k) for k in range(8)])
                byc, byckey = new_bank()
                mm_group(byc, byckey, [(sco[:, k, col], cv_t[:, k, :]) for k in range(8)],
                         reads=[kco] + [("cv", k) for k in range(8)])
                if oo == 3:
                    for nm in ("PP", "CO", "GP", "GC"):
                        done_unit(ti, f"{nm}{hh}")
                s1, s1key = new_t512()
                s2, s2key = new_t512()
                P.op("act", lambda e, s1=s1, bgp=bgp: e.activation(out=s1[:], in_=bgp[:], func=AF.Sigmoid),
                     reads=[bgpkey], writes=[s1key])
                P.op("act", lambda e, s2=s2, bgc=bgc: e.activation(out=s2[:], in_=bgc[:], func=AF.Sigmoid),
                     reads=[bgckey], writes=[s2key])
                P.op("dve", lambda e, s1=s1, byp=byp: e.tensor_tensor(out=s1[:], in0=byp[:], in1=s1[:], op=ALU.mult),
                     reads=[bypkey, s1key], writes=[s1key])
                P.op("dve", lambda e, s2=s2, byc=byc: e.tensor_tensor(out=s2[:], in0=byc[:], in1=s2[:], op=ALU.mult),
                     reads=[byckey, s2key], writes=[s2key])
                P.op("dve", lambda e, s1=s1, s2=s2, oc=oc: e.tensor_tensor(out=A_t[:, oc, :], in0=s1[:], in1=s2[:],
                                                                          op=ALU.add),
                     reads=[s1key, s2key], writes=[("A", oc)])

        def phase_O(ti):
            b = ti % 2
            for hh in range(2):
                swo, kwo = use_unit(ti, f"WO{hh}")
                for s in range(4):
                    bk, bkey = new_bank()
                    mm_group(bk, bkey, [(A_t[:, k, s * 128:(s + 1) * 128], swo[:, k, :]) for k in range(8)],
                             reads=[kwo] + [("A", k) for k in range(8)])
                    P.op("dve", lambda e, bk=bk, s=s, hh=hh, b=b: e.tensor_tensor(
                        out=xt[b][:, s, hh * 512:(hh + 1) * 512], in0=bk[:], in1=xt[b][:, s, hh * 512:(hh + 1) * 512],
                        op=ALU.add),
                        reads=[bkey, ("xt", b, s, hh)], writes=[("xt", b, s, hh)])
                    if hh == 1:
                        norm_sumsq(1, b, s)
                done_unit(ti, f"WO{hh}")
            norm_rstd(1)
            for s in range(4):
                norm_scale(1, b, s)

        def phase_F(ti):
            first = (ti % TILES_PER_SEQ == 0)
            h2k = [("hT", 1, k) for k in range(8)]
            if first:
                P.op("pool", lambda e: e.memset(fH[:].rearrange("p a b -> p (a b)"), 0.0), writes=[("fH",)])
            else:
                fhk = [("fh", q) for q in range(44)]
                P.op("pool", lambda e: e.tensor_tensor(out=fH[:, :, 0], in0=fw[:, 1, :], in1=fhalo[:, :, 1],
                                                       op=ALU.mult),
                     reads=[("c", "fw")] + fhk, writes=[("fH",)])
                P.op("pool", lambda e: e.tensor_tensor(out=ftmp[:], in0=fw[:, 0, :], in1=fhalo[:, :, 0],
                                                       op=ALU.mult),
                     reads=[("c", "fw")] + fhk, writes=[("ftmp",)])
                P.op("pool", lambda e: e.tensor_tensor(out=fH[:, :, 0], in0=fH[:, :, 0], in1=ftmp[:], op=ALU.add),
                     reads=[("ftmp",), ("fH",)], writes=[("fH",)])
                P.op("pool", lambda e: e.tensor_tensor(out=fH[:, :, 1], in0=fw[:, 0, :], in1=fhalo[:, :, 1],
                                                       op=ALU.mult),
                     reads=[("c", "fw")] + fhk, writes=[("fH",)])
            pend = [None]

            def flush_pair():
                if pend[0] is None:
                    return
                ag, agkey, av, avkey, jp = pend[0]
                pend[0] = None
                P.op("act", lambda e, ag=ag: e.activation(out=ag[:], in_=ag[:], func=AF.Silu),
                     reads=[agkey], writes=[agkey])
                P.op("pool", lambda e, ag=ag, av=av, jp=jp: e.tensor_tensor(out=aT_t[:, jp, :], in0=ag[:], in1=av[:],
                                                                           op=ALU.mult),
                     reads=[agkey, avkey], writes=[("aT", jp)])

            for j in range(NJ):
                u, jj = divmod(j, 4)
                sg_, kg_ = use_unit(ti, f"UG{u}")
                sv_, kv_ = use_unit(ti, f"UV{u}")
                col = slice(jj * 128, (jj + 1) * 128)
                bg, bgkey = new_bank()
                mm_group(bg, bgkey, [(sg_[:, k, col], hT[1][:, k, :]) for k in range(8)], reads=[kg_] + h2k)
                bv, bvkey = new_bank()
                mm_group(bv, bvkey, [(sv_[:, k, col], hT[1][:, k, :]) for k in range(8)], reads=[kv_] + h2k)
                if jj == 3 or j == NJ - 1:
                    done_unit(ti, f"UG{u}")
                    done_unit(ti, f"UV{u}")
                accs = []
                for (bkx, bkxkey, q) in ((bg, bgkey, j), (bv, bvkey, NJ + j)):
                    a, akey = new_t512()
                    P.op("act", lambda e, a=a, bkx=bkx, q=q: e.activation(
                        out=a[:], in_=bkx[:], func=AF.Identity, scale=fw[:, 2, q:q + 1], bias=fb[:, q:q + 1]),
                        reads=[bkxkey, ("c", "fw"), ("c", "fb")], writes=[akey])
                    accs.append((a, akey, bkx, bkxkey, q))
                flush_pair()
                for (a, akey, bkx, bkxkey, q) in accs:
                    P.op("dve", lambda e, a=a, bkx=bkx, q=q: e.scalar_tensor_tensor(
                        out=a[:, 1:T], in0=bkx[:, 0:T - 1], scalar=fw[:, 1, q:q + 1], in1=a[:, 1:T],
                        op0=ALU.mult, op1=ALU.add),
                        reads=[bkxkey, akey, ("c", "fw")], writes=[akey])
                for (a, akey, bkx, bkxkey, q) in accs:
                    P.op("dve", lambda e, a=a, bkx=bkx, q=q: e.scalar_tensor_tensor(
                        out=a[:, 2:T], in0=bkx[:, 0:T - 2], scalar=fw[:, 0, q:q + 1], in1=a[:, 2:T],
                        op0=ALU.mult, op1=ALU.add),
                        reads=[bkxkey, akey, ("c", "fw")], writes=[akey])
                for (a, akey, bkx, bkxkey, q) in accs:
                    P.op("dve", lambda e, a=a, q=q: e.tensor_tensor(out=a[:, 0:2], in0=a[:, 0:2], in1=fH[:, q, :],
                                                                    op=ALU.add),
                         reads=[akey, ("fH",)], writes=[akey])
                    P.op("dve", lambda e, bkx=bkx, q=q: e.tensor_copy(out=fhalo[:, q, :], in_=bkx[:, T - 2:T]),
                         reads=[bkxkey, ("fH",)], writes=[("fh", q)])
                (ag, agkey, _, _, _), (av, avkey, _, _, _) = accs
                pend[0] = (ag, agkey, av, avkey, j)
            flush_pair()

        def phase_D(ti, mid_hook=None):
            b = ti % 2
            for hh in range(2):
                bks = [new_bank() for _ in range(4)]
                for gi, (k0, k1) in enumerate(KG):
                    swd, kwd = use_unit(ti, f"WD{hh}{gi}")

                    def fn(e, swd=swd, k0=k0, k1=k1, bks=bks):
                        ins = None
                        for k in range(k0, k1):
                            for s in range(4):
                                ins = e.matmul(bks[s][0][:], lhsT=aT_t[:, k, s * 128:(s + 1) * 128],
                                               rhs=swd[:, k - k0, :], start=(k == 0), stop=(k == NJ - 1))
                        return ins
                    P.op("pe", fn, reads=[kwd] + [("aT", k) for k in range(k0, k1)], writes=[bk[1] for bk in bks])
                    done_unit(ti, f"WD{hh}{gi}")
                for s in range(4):
                    bk, bkey = bks[s]
                    P.op("dve", lambda e, bk=bk, s=s, hh=hh, b=b: e.tensor_tensor(
                        out=xt[b][:, s, hh * 512:(hh + 1) * 512], in0=bk[:], in1=xt[b][:, s, hh * 512:(hh + 1) * 512],
                        op=ALU.add),
                        reads=[bkey, ("xt", b, s, hh)], writes=[("xt", b, s, hh)])
                    if hh == 1:
                        norm_sumsq(2, b, s)
            if mid_hook is not None:
                mid_hook()
            norm_rstd(2)
            for s in range(4):
                P.op("dve", lambda e, s=s, b=b: e.scalar_tensor_tensor(
                    out=xt[b][:, s, :], in0=xt[b][:, s, :], scalar=rstd[2][:, s:s + 1], in1=gfin[:],
                    op0=ALU.mult, op1=ALU.mult),
                    reads=[("xt", b, s, 0), ("xt", b, s, 1), ("rstd", 2, s), ("c", "gfin")],
                    writes=[("xt", b, s, 0), ("xt", b, s, 1)])

        emit_xload(0)
        emit_xload(1)
        for G in range(R):
            emit_load(G)
        emit_norm1(0)
        transposes(0, g1)
        phase_P(0)

        last_store = [None, None]
        for ti in range(NT):
            b = ti % 2
            phase_C(ti)
            if ti + 1 < NT:
                emit_norm1(ti + 1)
            phase_P2(ti)
            phase_M(ti)
            if ti + 1 < NT:
                transposes(0, g1)
            phase_O(ti)
            transposes(1, g2)
            phase_F(ti)
            phase_D(ti, (lambda ti=ti: phase_P(ti + 1)) if ti + 1 < NT else None)
            last_store[b] = emit_xstore(ti)
            emit_xload(ti + 2)

        P.wait_only("sp", [ev for ev in last_store if ev is not None])

        block = E(nc.Block())

        def replay(stream, eng):
            for waits, fn, inc_sem, inc_amt in stream.ops:
                for sem, val in waits:
                    eng.wait_ge(sems[sem], val)
                if fn is not None:
                    ins = fn(eng)
                    ins.then_inc(sems[inc_sem], inc_amt)

        @block.tensor
        def _(eng):
            replay(P.streams["pe"], eng)

        @block.scalar
        def _(eng):
            replay(P.streams["act"], eng)

        @block.vector
        def _(eng):
            replay(P.streams["dve"], eng)

        @block.gpsimd
        def _(eng):
            replay(P.streams["pool"], eng)

        @block.sync
        def _(eng):
            replay(P.streams["sp"], eng)

    return nc


def _get_program():
    if "nc" not in _CACHE:
        _CACHE["nc"] = build_program()
    return _CACHE["nc"]


def _chunked(v, n):
    return np.ascontiguousarray(np.asarray(v, dtype=np.float32).reshape(n, 128).T)


def make_shared(I):
    f = lambda a: np.ascontiguousarray(np.asarray(a, dtype=np.float32))
    vecs = np.concatenate([_chunked(f(I["norm_mix"])[0], 8), _chunked(f(I["norm_ffn"])[0], 8),
                           _chunked(f(I["pool_scale"])[0], 8)], axis=1)
    cw = np.stack([_chunked(f(I["conv_w"])[0, k], 8) for k in range(3)], axis=1).reshape(128, 24)
    fw = np.stack([_chunked(f(I["ffn_conv_w"])[0, k], 44) for k in range(3)], axis=1).reshape(128, 132)
    fb = _chunked(f(I["ffn_conv_b"])[0], 44)
    gfin = np.ascontiguousarray(np.broadcast_to(f(I["norm_final"])[None, :], (128, D)))
    aux = np.zeros((128, 144), dtype=np.float32)
    aux[:, 0:128] = np.eye(128, dtype=np.float32)
    aux[:, 128:144] = (1.0 / np.arange(1, 17, dtype=np.float64)).astype(np.float32)[None, :]
    return {
        "w_in": f(I["w_in"])[0], "pool_w": f(I["pool_w"])[0].reshape(1024, 256),
        "w_pool_proj": f(I["w_pool_proj"])[0], "w_conv_out": f(I["w_conv_out"])[0], "w_o": f(I["w_o"])[0],
        "w_up": f(I["w_up"])[0], "w_down": f(I["w_down"])[0],
        "vecs": np.ascontiguousarray(vecs), "cw": np.ascontiguousarray(cw), "fw": np.ascontiguousarray(fw),
        "fb": fb, "gfin": gfin, "aux": aux,
    }


def kernel(x, norm_mix, w_in, pool_w, pool_scale, w_pool_proj, conv_w, w_conv_out, w_o,
           norm_ffn, w_up, ffn_conv_w, ffn_conv_b, w_down, norm_final):
    x = np.ascontiguousarray(np.asarray(x, dtype=np.float32))
    shared = make_shared(dict(norm_mix=norm_mix, w_in=w_in, pool_w=pool_w, pool_scale=pool_scale,
                              w_pool_proj=w_pool_proj, conv_w=conv_w, w_conv_out=w_conv_out, w_o=w_o,
                              norm_ffn=norm_ffn, w_up=w_up, ffn_conv_w=ffn_conv_w, ffn_conv_b=ffn_conv_b,
                              w_down=w_down, norm_final=norm_final))
    xf = x.reshape(N_CORES, TOK_PER_CORE, D)
    in_maps = [dict(shared, x=np.ascontiguousarray(xf[c])) for c in range(N_CORES)]
    nc = _get_program()
    res = run_bass_kernel_spmd(nc, in_maps, core_ids=list(range(N_CORES)))
    out = np.stack([np.asarray(r["out"], dtype=np.float32) for r in res.results], axis=0)
    return out.reshape(x.shape)
```

```python
from contextlib import ExitStack

import os
import numpy as np
import concourse.bass as bass
import concourse.mybir as mybir
from concourse.bass_utils import run_bass_kernel_spmd

F32 = mybir.dt.float32
BF16 = mybir.dt.bfloat16
AF = mybir.ActivationFunctionType
ALU = mybir.AluOpType

N_CORES = 8
D = 1024
SEQ = 2048
TOK_PER_CORE = 4096
T = 512
NT = TOK_PER_CORE // T
TILES_PER_SEQ = SEQ // T
D_FF = 2816
NJ = D_FF // 128
EPS = 1e-6
R = 8
NBANK = 6
NT512 = 7
NU = 3
NS = 3
POOL_WINDOWS = (2, 4, 8, 16)
DBG = os.environ.get("KDBG", "")
MUL_ENG = "dve" if "dvemul" in DBG else "pool"


_CACHE = {}


class Stream:
    def __init__(self, name):
        self.name = name
        self.count = 0
        self.ops = []
        self.waited = {}


class Prog:
    def __init__(self):
        self.streams = {n: Stream(n) for n in ("pe", "act", "dve", "pool", "sp")}
        self.res = {}
        self.semval = {}
        self.tag = ""
        self.evtag = {}
        self.pe_n = []

    def _deps(self, reads, writes):
        evs = []
        for k in reads:
            r = self.res.get(k)
            if r and r[0] is not None:
                evs.append(r[0])
        for k in writes:
            r = self.res.get(k)
            if r:
                if r[0] is not None:
                    evs.append(r[0])
                evs.extend(r[1])
        return evs

    def _commit(self, ev, reads, writes):
        for k in reads:
            r = self.res.setdefault(k, [None, []])
            r[1].append(ev)
        for k in writes:
            self.res[k] = [ev, []]

    def _waits(self, st, evs):
        need = {}
        for sem, val in evs:
            if st.name == "pe" and sem == "pe":
                continue
            if st.waited.get(sem, 0) >= val:
                continue
            if need.get(sem, 0) < val:
                need[sem] = val
        for sem, val in need.items():
            st.waited[sem] = val
        return list(need.items())

    def op(self, eng, fn, reads=(), writes=()):
        st = self.streams[eng]
        waits = self._waits(st, self._deps(reads, writes))
        st.count += 1
        ev = (eng, st.count)
        st.ops.append((waits, fn, eng, 1))
        self.evtag[ev] = self.tag
        self._commit(ev, reads, writes)
        return ev

    def dma(self, queue, sem, fn, reads=(), writes=()):
        st = self.streams[queue]
        evs = self._deps(reads, writes)
        prev = self.semval.get(sem, 0)
        if prev:
            evs.append((sem, prev))
        waits = self._waits(st, evs)
        val = prev + 16
        self.semval[sem] = val
        ev = (sem, val)
        st.ops.append((waits, fn, sem, 16))
        self.evtag[ev] = self.tag + "/dma"
        self._commit(ev, reads, writes)
        return ev

    def wait_only(self, eng, evs):
        st = self.streams[eng]
        waits = self._waits(st, evs)
        if waits:
            st.ops.append((waits, None, None, 0))


def build_program(NT=NT, TOK=TOK_PER_CORE, STOP=99):
    nc = bass.Bass("TRN2", target_bir_lowering=False)
    dt = nc.dram_tensor

    x_d = dt("x", [TOK, D], F32, kind="ExternalInput").ap()
    out_d = dt("out", [TOK, D], F32, kind="ExternalOutput").ap()
    w_in_d = dt("w_in", [D, 6 * D], F32, kind="ExternalInput").ap()
    pool_w_d = dt("pool_w", [1024, 256], F32, kind="ExternalInput").ap()
    w_pp_d = dt("w_pool_proj", [D, D], F32, kind="ExternalInput").ap()
    w_co_d = dt("w_conv_out", [D, D], F32, kind="ExternalInput").ap()
    w_o_d = dt("w_o", [D, D], F32, kind="ExternalInput").ap()
    w_up_d = dt("w_up", [D, 2 * D_FF], F32, kind="ExternalInput").ap()
    w_dn_d = dt("w_down", [D_FF, D], F32, kind="ExternalInput").ap()
    vecs_d = dt("vecs", [128, 24], F32, kind="ExternalInput").ap()
    cw_d = dt("cw", [128, 24], F32, kind="ExternalInput").ap()
    fw_d = dt("fw", [128, 132], F32, kind="ExternalInput").ap()
    fb_d = dt("fb", [128, 44], F32, kind="ExternalInput").ap()
    gfin_d = dt("gfin", [128, D], F32, kind="ExternalInput").ap()
    aux_d = dt("aux", [128, 144], F32, kind="ExternalInput").ap()

    wv_in = w_in_d.rearrange("(k p) n -> p k n", p=128)
    wv_pw = pool_w_d.rearrange("(gk p) c -> p gk c", p=128)
    wv_pp = w_pp_d.rearrange("(k p) n -> p k n", p=128)
    wv_co = w_co_d.rearrange("(k p) n -> p k n", p=128)
    wv_o = w_o_d.rearrange("(k p) n -> p k n", p=128)
    wv_up = w_up_d.rearrange("(k p) n -> p k n", p=128)
    wv_dn = w_dn_d.rearrange("(k p) n -> p k n", p=128)

    units = []

    def add_unit(name, src, nk, ncols):
        units.append((name, src, nk, ncols))

    add_unit("Z0", wv_in[:, :, 0:512], 8, 512)
    add_unit("Z1", wv_in[:, :, 512:1024], 8, 512)
    add_unit("PW", wv_pw, 8, 256)
    for hh in range(2):
        add_unit(f"C{hh}", wv_in[:, :, 2048 + hh * 512:2048 + (hh + 1) * 512], 8, 512)
        add_unit(f"V{hh}", wv_in[:, :, 3072 + hh * 512:3072 + (hh + 1) * 512], 8, 512)
        add_unit(f"B{hh}", wv_in[:, :, 1024 + hh * 512:1024 + (hh + 1) * 512], 8, 512)
    for hh in range(2):
        add_unit(f"PP{hh}", wv_pp[:, :, hh * 512:(hh + 1) * 512], 8, 512)
        add_unit(f"CO{hh}", wv_co[:, :, hh * 512:(hh + 1) * 512], 8, 512)
        add_unit(f"GP{hh}", wv_in[:, :, 4096 + hh * 512:4096 + (hh + 1) * 512], 8, 512)
        add_unit(f"GC{hh}", wv_in[:, :, 5120 + hh * 512:5120 + (hh + 1) * 512], 8, 512)
    for hh in range(2):
        add_unit(f"WO{hh}", wv_o[:, :, hh * 512:(hh + 1) * 512], 8, 512)
    for u in range(6):
        nc_ = 512 if u < 5 else 256
        add_unit(f"UG{u}", wv_up[:, :, u * 512:u * 512 + nc_], 8, nc_)
        add_unit(f"UV{u}", wv_up[:, :, D_FF + u * 512:D_FF + u * 512 + nc_], 8, nc_)
    KG = [(0, 8), (8, 16), (16, 22)]
    for hh in range(2):
        for gi, (k0, k1) in enumerate(KG):
            add_unit(f"WD{hh}{gi}", wv_dn[:, k0:k1, hh * 512:(hh + 1) * 512], k1 - k0, 512)
    NUNITS = len(units)
    uidx = {u[0]: i for i, u in enumerate(units)}

    scr_d = dt("wscr", [NUNITS, 128, 8, 512], BF16, kind="Internal").ap()

    P = Prog()
    _CACHE["prog"] = P
    es = ExitStack()
    E = es.enter_context
    with es:
        xt = [E(nc.sbuf_tensor(f"xt{b}", [128, 4, D], F32)) for b in range(2)]
        xs = E(nc.sbuf_tensor("xs0", [128, 4, D], BF16))
        hT = [E(nc.sbuf_tensor(f"hT{i}", [128, 8, T], BF16)) for i in range(2)]
        A_t = E(nc.sbuf_tensor("A_t", [128, 8, T], BF16))
        p2_t = E(nc.sbuf_tensor("p2_t", [128, 8, T], BF16))
        cv_t = E(nc.sbuf_tensor("cv_t", [128, 8, T], BF16))
        aT_t = E(nc.sbuf_tensor("aT_t", [128, NJ, T], BF16))
        slots = [E(nc.sbuf_tensor(f"slot{r}", [128, 8, 512], BF16)) for r in range(R)]
        t512 = [E(nc.sbuf_tensor(f"t512_{i}", [128, T], F32)) for i in range(NT512)]
        ubuf = [E(nc.sbuf_tensor(f"ubuf{i}", [128, T + 16], F32)) for i in range(NU)]
        sbuf_ = [E(nc.sbuf_tensor(f"sbuf{i}", [128, T + 16], F32)) for i in range(NS)]
        junk = E(nc.sbuf_tensor("junk", [128, D], BF16))
        vecs = E(nc.sbuf_tensor("vecs_t", [128, 24], F32))
        cw = E(nc.sbuf_tensor("cw_t", [128, 3, 8], F32))
        fw = E(nc.sbuf_tensor("fw_t", [128, 3, 44], F32))
        fb = E(nc.sbuf_tensor("fb_t", [128, 44], F32))
        gfin = E(nc.sbuf_tensor("gfin_t", [128, D], F32))
        aux = E(nc.sbuf_tensor("aux_t", [128, 144], F32))
        identb = E(nc.sbuf_tensor("identb", [128, 128], BF16))
        mhalf = E(nc.sbuf_tensor("mhalf", [128, 4], F32))
        ss = [E(nc.sbuf_tensor(f"ss{i}", [128, 4], F32)) for i in range(3)]
        ms = [E(nc.sbuf_tensor(f"ms{i}", [128, 4], F32)) for i in range(3)]
        rstd = [E(nc.sbuf_tensor(f"rstd{i}", [128, 4], F32)) for i in range(3)]
        phalo = E(nc.sbuf_tensor("phalo", [128, 8, 16], F32))
        chalo = E(nc.sbuf_tensor("chalo", [128, 8, 2], F32))
        fhalo = E(nc.sbuf_tensor("fhalo", [128, 44, 2], F32))
        fH = E(nc.sbuf_tensor("fH", [128, 44, 2], F32))
        ftmp = E(nc.sbuf_tensor("ftmp", [128, 44], F32))
        pfix = E(nc.sbuf_tensor("pfix", [128, 16], F32))
        banks = [E(nc.psum_tensor(f"bank{i}", [128, 512], F32)) for i in range(NBANK)]
        tbank = [E(nc.psum_tensor(f"tbank{i}", [128, 512], BF16)) for i in range(2)]
        sem_names = ["pe", "act", "dve", "pool", "const", "x0", "x1"] + [f"w{r}" for r in range(R)]
        sems = {n: E(nc.semaphore("sem_" + n)) for n in sem_names}

        g1 = lambda k: vecs[:, k:k + 1]
        g2 = lambda k: vecs[:, 8 + k:9 + k]
        psc = lambda k: vecs[:, 16 + k:17 + k]
        invcnt = aux[:, 128:144]

        cnt = {"bank": 0, "t512": 0, "u": 0, "tb": 0, "s": 0}

        def new_bank():
            i = cnt["bank"] % NBANK
            cnt["bank"] += 1
            return banks[i], ("bank", i)

        def new_t512():
            i = cnt["t512"] % NT512
            cnt["t512"] += 1
            return t512[i], ("t512", i)

        def new_u():
            i = cnt["u"] % NU
            cnt["u"] += 1
            return ubuf[i], ("u", i), ("uh", i)

        def new_s():
            i = cnt["s"] % NS
            cnt["s"] += 1
            return sbuf_[i], ("sb", i)

        def new_tb():
            i = cnt["tb"] % 2
            cnt["tb"] += 1
            return tbank[i], ("tb", i)

        P.dma("pool", "const", lambda e: e.dma_start(out=vecs[:], in_=vecs_d), writes=[("c", "vecs")])
        P.dma("pool", "const", lambda e: e.dma_start(out=cw[:].rearrange("p a b -> p (a b)"), in_=cw_d),
              writes=[("c", "cw")])
        P.dma("pool", "const", lambda e: e.dma_start(out=fw[:].rearrange("p a b -> p (a b)"), in_=fw_d),
              writes=[("c", "fw")])
        P.dma("pool", "const", lambda e: e.dma_start(out=fb[:], in_=fb_d), writes=[("c", "fb")])
        P.dma("pool", "const", lambda e: e.dma_start(out=gfin[:], in_=gfin_d), writes=[("c", "gfin")])
        P.dma("pool", "const", lambda e: e.dma_start(out=aux[:], in_=aux_d), writes=[("c", "aux")])
        P.op("pool", lambda e: e.memset(mhalf[:], -0.5), writes=[("c", "mhalf")])
        P.op("dve", lambda e: e.tensor_copy(out=identb[:], in_=aux[:, 0:128]),
             reads=[("c", "aux")], writes=[("c", "identb")])
        CONST_KEYS = [("c", n) for n in ("vecs", "cw", "fw", "fb", "gfin", "aux", "mhalf", "identb")]

        per_tile = ["C0", "V0", "B0", "C1", "V1", "B1", "PW",
                    "PP0", "CO0", "GP0", "GC0", "PP1", "CO1", "GP1", "GC1", "WO0", "WO1"]
        ffn_units = [f"U{t}{u}" for u in range(6) for t in ("G", "V")] + \
                    [f"WD{hh}{gi}" for hh in range(2) for gi in range(3)]
        sched = ["Z0", "Z1"]
        Gof = {(0, "Z0"): 0, (0, "Z1"): 1}
        for ti_ in range(NT):
            for nm in per_tile:
                Gof[(ti_, nm)] = len(sched)
                sched.append(nm)
            for nm in ffn_units:
                Gof[(ti_, nm)] = len(sched)
                sched.append(nm)
            if ti_ + 1 < NT:
                for nm in ("Z0", "Z1"):
                    Gof[(ti_ + 1, nm)] = len(sched)
                    sched.append(nm)
        seen_units = set()

        def slot_key(G):
            return ("slot", G % R)

        def emit_load(G):
            if G >= len(sched):
                return
            loaded.add(G)
            n = uidx[sched[G]]
            name, src, nk, ncols = units[n]
            slot = slots[G % R]
            semn = f"w{G % R}"
            if name not in seen_units:
                seen_units.add(name)
                dst = slot[:, 0:nk, 0:ncols]
                P.dma("pool", semn, lambda e: e.dma_start(out=dst, in_=src), writes=[slot_key(G)])
                if NT > 1:
                    P.dma("sp", semn, lambda e: e.dma_start(out=scr_d[n], in_=slot[:]),
                          reads=[slot_key(G)], writes=[("scr", n)])
            else:
                P.dma("sp", semn, lambda e: e.dma_start(out=slot[:], in_=scr_d[n]),
                      reads=[("scr", n)], writes=[slot_key(G)])

        loaded = set()
        used_max = [-1]

        def use_unit(ti, name):
            G = Gof[(ti, name)]
            assert G in loaded, (ti, name, G)
            assert G + R > used_max[0], (ti, name, G, used_max[0])
            used_max[0] = max(used_max[0], G)
            return slots[G % R], slot_key(G)

        def done_unit(ti, name):
            emit_load(Gof[(ti, name)] + R)

        def emit_xload(ti):
            if ti >= NT:
                return
            b = ti % 2
            src = x_d[ti * T:(ti + 1) * T, :].rearrange("(s p) d -> p s d", p=128)
            P.dma("sp", f"x{b}", lambda e: e.dma_start(out=xt[b][:], in_=src),
                  writes=[("xt", b, s, h) for s in range(4) for h in range(2)])

        def emit_xstore(ti):
            b = ti % 2
            dst = out_d[ti * T:(ti + 1) * T, :].rearrange("(s p) d -> p s d", p=128)
            return P.dma("sp", f"x{b}", lambda e: e.dma_start(out=dst, in_=xt[b][:]),
                         reads=[("xt", b, s, h) for s in range(4) for h in range(2)])

        def mm_group(bank_ap, bank_key, pairs, reads):
            n = len(pairs)

            def fn(e):
                ins = None
                for i, (l, r) in enumerate(pairs):
                    ins = e.matmul(bank_ap[:], lhsT=l, rhs=r, start=(i == 0), stop=(i == n - 1))
                return ins
            P.pe_n.append((n, P.tag))
            return P.op("pe", fn, reads=reads, writes=[bank_key])

        def norm_sumsq(which, b, s):
            P.op("act", lambda e: e.activation(out=junk[:], in_=xt[b][:, s, :], func=AF.Square,
                                               accum_out=ss[which][:, s:s + 1]),
                 reads=[("xt", b, s, 0), ("xt", b, s, 1)], writes=[("ss", which, s)])

        def norm_rstd(which):
            allk = lambda n: [(n, which, s) for s in range(4)]
            P.op("dve", lambda e: e.tensor_scalar(out=ms[which][:], in0=ss[which][:],
                                                  scalar1=1.0 / D, scalar2=EPS, op0=ALU.mult, op1=ALU.add),
                 reads=allk("ss"), writes=allk("ms"))
            P.op("pool", lambda e: e.tensor_tensor(out=rstd[which][:], in0=ms[which][:], in1=mhalf[:], op=ALU.pow),
                 reads=allk("ms") + [("c", "mhalf")], writes=allk("rstd"))

        def norm_scale(which, b, s):
            P.op("act", lambda e: e.activation(out=xs[:, s, :], in_=xt[b][:, s, :], func=AF.Copy,
                                               scale=rstd[which][:, s:s + 1]),
                 reads=[("xt", b, s, 0), ("xt", b, s, 1), ("rstd", which, s)], writes=[("xs", s)])

        def transposes(hb, gfun, act_only=False):
            for k in range(8):
                tb_ap, tb_key = new_tb()

                def fn(e, k=k, tb_ap=tb_ap):
                    ins = None
                    for s in range(4):
                        ins = e.transpose(out=tb_ap[:, s * 128:(s + 1) * 128],
                                          in_=xs[:, s, k * 128:(k + 1) * 128], identity=identb[:])
                    return ins
                P.pe_n.append((4, P.tag + "/T"))
                P.op("pe", fn, reads=[("xs", s) for s in range(4)] + [("c", "identb")], writes=[tb_key])
                if act_only or k % 2 == 0:
                    P.op("act", lambda e, k=k, tb_ap=tb_ap: e.activation(out=hT[hb][:, k, :], in_=tb_ap[:],
                                                                         func=AF.Copy, scale=gfun(k)),
                         reads=[tb_key, ("c", "vecs")], writes=[("hT", hb, k)])
                else:
                    P.op("dve", lambda e, k=k, tb_ap=tb_ap: e.tensor_scalar(out=hT[hb][:, k, :], in0=tb_ap[:],
                                                                            scalar1=gfun(k), scalar2=None,
                                                                            op0=ALU.mult),
                         reads=[tb_key, ("c", "vecs")], writes=[("hT", hb, k)])

        def emit_norm1(ti):
            b = ti % 2
            for s in range(4):
                norm_sumsq(0, b, s)
            norm_rstd(0)
            for s in range(4):
                norm_scale(0, b, s)

        def phase_P(ti):
            first = (ti % TILES_PER_SEQ == 0)
            hb = 0
            hTk = [("hT", hb, k) for k in range(8)]
            for c in range(8):
                g = c // 2
                win = POOL_WINDOWS[g]
                uname = "Z0" if c < 4 else "Z1"
                slot, skey = use_unit(ti, uname)
                cc = c % 4
                bk, bkey = new_bank()
                mm_group(bk, bkey, [(slot[:, k, cc * 128:(cc + 1) * 128], hT[hb][:, k, :]) for k in range(8)],
                         reads=[skey] + hTk)
                if cc == 3:
                    done_unit(ti, uname)
                U, ukey, uhkey = new_u()
                if first:
                    P.op("pool", lambda e, U=U: e.memset(U[:, 0:16], 0.0), writes=[uhkey])
                else:
                    P.op("pool", lambda e, U=U, c=c: e.tensor_copy(out=U[:, 0:16], in_=phalo[:, c, :]),
                         reads=[("ph", c)], writes=[uhkey])
                P.op("act", lambda e, U=U, bk=bk: e.activation(out=U[:, 16:16 + T], in_=bk[:], func=AF.Copy),
                     reads=[bkey], writes=[ukey])
                P.op("pool", lambda e, U=U, c=c: e.tensor_copy(out=phalo[:, c, :], in_=U[:, T:T + 16]),
                     reads=[ukey], writes=[("ph", c)])
                src, srckeys = U, [ukey, uhkey]
                off = 1
                for lvl in range(g + 1):
                    S, skey2 = new_s()
                    lo = 2 * off - 1
                    P.op("dve", lambda e, S=S, src=src, off=off, lo=lo: e.tensor_tensor(
                        out=S[:, lo:T + 16], in0=src[:, lo:T + 16], in1=src[:, lo - off:T + 16 - off], op=ALU.add),
                        reads=srckeys, writes=[skey2])
                    src, srckeys = S, [skey2]
                    off *= 2
                P.op("dve", lambda e, S=src, U=U, c=c, win=win: e.scalar_tensor_tensor(
                    out=A_t[:, c, :], in0=S[:, 16:16 + T], scalar=1.0 / win, in1=U[:, 16:16 + T],
                    op0=ALU.mult, op1=ALU.subtract),
                    reads=srckeys + [ukey], writes=[("A", c)])
                if first:
                    nfix = win - 1
                    P.op("dve", lambda e, S=src, nfix=nfix: e.tensor_tensor(
                        out=pfix[:, 0:nfix], in0=S[:, 16:16 + nfix], in1=invcnt[:, 0:nfix], op=ALU.mult),
                        reads=srckeys + [("c", "aux")], writes=[("pfix",)])
                    P.op("dve", lambda e, U=U, c=c, nfix=nfix: e.tensor_tensor(
                        out=A_t[:, c, 0:nfix], in0=pfix[:, 0:nfix], in1=U[:, 16:16 + nfix], op=ALU.subtract),
                        reads=[("pfix",), ukey], writes=[("A", c)])

        def phase_P2(ti):
            slot, skey = use_unit(ti, "PW")
            for g in range(4):
                for o in range(2):
                    c = 2 * g + o
                    bk, bkey = new_bank()
                    mm_group(bk, bkey,
                             [(slot[:, 2 * g + k2, o * 128:(o + 1) * 128], A_t[:, 2 * g + k2, :]) for k2 in range(2)],
                             reads=[skey, ("A", 2 * g), ("A", 2 * g + 1)])
                    if c % 2 == 0:
                        P.op("act", lambda e, bk=bk, c=c: e.activation(out=p2_t[:, c, :], in_=bk[:], func=AF.Copy,
                                                                      scale=psc(c)),
                             reads=[bkey, ("c", "vecs")], writes=[("p2", c)])
                    else:
                        P.op("dve", lambda e, bk=bk, c=c: e.tensor_scalar(out=p2_t[:, c, :], in0=bk[:],
                                                                         scalar1=psc(c), scalar2=None, op0=ALU.mult),
                             reads=[bkey, ("c", "vecs")], writes=[("p2", c)])
            done_unit(ti, "PW")

        def phase_C(ti, j_hooks=None):
            first = (ti % TILES_PER_SEQ == 0)
            hb = 0
            hTk = [("hT", hb, k) for k in range(8)]
            for j in range(8):
                hh, jj = divmod(j, 4)
                sc, kc = use_unit(ti, f"C{hh}")
                sv, kv = use_unit(ti, f"V{hh}")
                sb_, kb = use_unit(ti, f"B{hh}")
                bc, bckey = new_bank()
                mm_group(bc, bckey, [(sc[:, k, jj * 128:(jj + 1) * 128], hT[hb][:, k, :]) for k in range(8)],
                         reads=[kc] + hTk)
                bv, bvkey = new_bank()
                mm_group(bv, bvkey, [(sv[:, k, jj * 128:(jj + 1) * 128], hT[hb][:, k, :]) for k in range(8)],
                         reads=[kv] + hTk)
                bb, bbkey = new_bank()
                mm_group(bb, bbkey, [(sb_[:, k, jj * 128:(jj + 1) * 128], hT[hb][:, k, :]) for k in range(8)],
                         reads=[kb] + hTk)
                if jj == 3:
                    done_unit(ti, f"C{hh}")
                    done_unit(ti, f"V{hh}")
                    done_unit(ti, f"B{hh}")
                zc, zckey = new_t512()
                P.op("act", lambda e, zc=zc, bc=bc: e.activation(out=zc[:], in_=bc[:], func=AF.Copy),
                     reads=[bckey], writes=[zckey])
                CV, cvkey, cvhkey = new_u()
                if first:
                    P.op("pool", lambda e, CV=CV: e.memset(CV[:, 0:2], 0.0), writes=[cvhkey])
                else:
                    P.op("pool", lambda e, CV=CV, j=j: e.tensor_copy(out=CV[:, 0:2], in_=chalo[:, j, :]),
                         reads=[("ch", j)], writes=[cvhkey])
                P.op("dve", lambda e, CV=CV, bv=bv, zc=zc: e.tensor_tensor(out=CV[:, 2:2 + T], in0=bv[:], in1=zc[:],
                                                                           op=ALU.mult),
                     reads=[bvkey, zckey], writes=[cvkey])
                P.op("pool", lambda e, CV=CV, j=j: e.tensor_copy(out=chalo[:, j, :], in_=CV[:, T:T + 2]),
                     reads=[cvkey], writes=[("ch", j)])
                acc, acckey = new_t512()
                P.op("act", lambda e, acc=acc, CV=CV, j=j: e.activation(out=acc[:], in_=CV[:, 2:2 + T], func=AF.Copy,
                                                                       scale=cw[:, 2, j:j + 1]),
                     reads=[cvkey, ("c", "cw")], writes=[acckey])
                P.op("dve", lambda e, acc=acc, CV=CV, j=j: e.scalar_tensor_tensor(
                    out=acc[:], in0=CV[:, 1:1 + T], scalar=cw[:, 1, j:j + 1], in1=acc[:], op0=ALU.mult, op1=ALU.add),
                    reads=[cvkey, cvhkey, acckey, ("c", "cw")], writes=[acckey])
                P.op("dve", lambda e, acc=acc, CV=CV, j=j: e.scalar_tensor_tensor(
                    out=acc[:], in0=CV[:, 0:T], scalar=cw[:, 0, j:j + 1], in1=acc[:], op0=ALU.mult, op1=ALU.add),
                    reads=[cvkey, cvhkey, acckey, ("c", "cw")], writes=[acckey])
                P.op("dve", lambda e, acc=acc, bb=bb, j=j: e.tensor_tensor(out=cv_t[:, j, :], in0=bb[:], in1=acc[:],
                                                                          op=ALU.mult),
                     reads=[bbkey, acckey], writes=[("cv", j)])
                if j_hooks and j < len(j_hooks):
                    j_hooks[j]()

        def phase_M(ti, oc_hooks=None):
            hb = 0
            hTk = [("hT", hb, k) for k in range(8)]
            for oc in range(8):
                hh, oo = divmod(oc, 4)
                spp, kpp = use_unit(ti, f"PP{hh}")
                sco, kco = use_unit(ti, f"CO{hh}")
                sgp_, kgp = use_unit(ti, f"GP{hh}")
                sgc_, kgc = use_unit(ti, f"GC{hh}")
                col = slice(oo * 128, (oo + 1) * 128)
                bgp, bgpkey = new_bank()
                mm_group(bgp, bgpkey, [(sgp_[:, k, col], hT[hb][:, k, :]) for k in range(8)], reads=[kgp] + hTk)
                bgc, bgckey = new_bank()
                mm_group(bgc, bgckey, [(sgc_[:, k, col], hT[hb][:, k, :]) for k in range(8)], reads=[kgc] + hTk)
                byp, bypkey = new_bank()
                mm_group(byp, bypkey, [(spp[:, k, col], p2_t[:, k, :]) for k in range(8)],
                         reads=[kpp] + [("p2", k) for k in range(8)])
                byc, byckey = new_bank()
                mm_group(byc, byckey, [(sco[:, k, col], cv_t[:, k, :]) for k in range(8)],
                         reads=[kco] + [("cv", k) for k in range(8)])
                if oo == 3:
                    for nm in ("PP", "CO", "GP", "GC"):
                        done_unit(ti, f"{nm}{hh}")
                s1, s1key = new_t512()
                s2, s2key = new_t512()
                P.op("act", lambda e, s1=s1, bgp=bgp: e.activation(out=s1[:], in_=bgp[:], func=AF.Sigmoid),
                     reads=[bgpkey], writes=[s1key])
                P.op("act", lambda e, s2=s2, bgc=bgc: e.activation(out=s2[:], in_=bgc[:], func=AF.Sigmoid),
                     reads=[bgckey], writes=[s2key])
                P.op("dve", lambda e, s1=s1, byp=byp: e.tensor_tensor(out=s1[:], in0=byp[:], in1=s1[:], op=ALU.mult),
                     reads=[bypkey, s1key], writes=[s1key])
                P.op("dve", lambda e, s2=s2, byc=byc: e.tensor_tensor(out=s2[:], in0=byc[:], in1=s2[:], op=ALU.mult),
                     reads=[byckey, s2key], writes=[s2key])
                P.op("dve", lambda e, s1=s1, s2=s2, oc=oc: e.tensor_tensor(out=A_t[:, oc, :], in0=s1[:], in1=s2[:],
                                                                          op=ALU.add),
                     reads=[s1key, s2key], writes=[("A", oc)])
                if oc_hooks and oc < len(oc_hooks):
                    oc_hooks[oc]()

        def phase_O(ti):
            b = ti % 2
            for hh in range(2):
                swo, kwo = use_unit(ti, f"WO{hh}")
                for s in range(4):
                    bk, bkey = new_bank()
                    mm_group(bk, bkey, [(A_t[:, k, s * 128:(s + 1) * 128], swo[:, k, :]) for k in range(8)],
                             reads=[kwo] + [("A", k) for k in range(8)])
                    P.op("dve", lambda e, bk=bk, s=s, hh=hh, b=b: e.tensor_tensor(
                        out=xt[b][:, s, hh * 512:(hh + 1) * 512], in0=bk[:], in1=xt[b][:, s, hh * 512:(hh + 1) * 512],
                        op=ALU.add),
                        reads=[bkey, ("xt", b, s, hh)], writes=[("xt", b, s, hh)])
                    if hh == 1:
                        norm_sumsq(1, b, s)
                done_unit(ti, f"WO{hh}")
            norm_rstd(1)
            for s in range(4):
                norm_scale(1, b, s)

        def phase_F(ti):
            first = (ti % TILES_PER_SEQ == 0)
            h2k = [("hT", 1, k) for k in range(8)]
            if first:
                P.op("pool", lambda e: e.memset(fH[:].rearrange("p a b -> p (a b)"), 0.0), writes=[("fH",)])
            else:
                fhk = [("fh", q) for q in range(44)]
                P.op("pool", lambda e: e.tensor_tensor(out=fH[:, :, 0], in0=fw[:, 1, :], in1=fhalo[:, :, 1],
                                                       op=ALU.mult),
                     reads=[("c", "fw")] + fhk, writes=[("fH",)])
                P.op("pool", lambda e: e.tensor_tensor(out=ftmp[:], in0=fw[:, 0, :], in1=fhalo[:, :, 0],
                                                       op=ALU.mult),
                     reads=[("c", "fw")] + fhk, writes=[("ftmp",)])
                P.op("pool", lambda e: e.tensor_tensor(out=fH[:, :, 0], in0=fH[:, :, 0], in1=ftmp[:], op=ALU.add),
                     reads=[("ftmp",), ("fH",)], writes=[("fH",)])
                P.op("pool", lambda e: e.tensor_tensor(out=fH[:, :, 1], in0=fw[:, 0, :], in1=fhalo[:, :, 1],
                                                       op=ALU.mult),
                     reads=[("c", "fw")] + fhk, writes=[("fH",)])
            pend = [None]

            def flush_pair():
                if pend[0] is None:
                    return
                ag, agkey, av, avkey, jp = pend[0]
                pend[0] = None
                P.op("act", lambda e, ag=ag: e.activation(out=ag[:], in_=ag[:], func=AF.Silu),
                     reads=[agkey], writes=[agkey])
                P.op("pool", lambda e, ag=ag, av=av, jp=jp: e.tensor_tensor(out=aT_t[:, jp, :], in0=ag[:], in1=av[:],
                                                                           op=ALU.mult),
                     reads=[agkey, avkey], writes=[("aT", jp)])

            for j in range(NJ):
                u, jj = divmod(j, 4)
                sg_, kg_ = use_unit(ti, f"UG{u}")
                sv_, kv_ = use_unit(ti, f"UV{u}")
                col = slice(jj * 128, (jj + 1) * 128)
                bg, bgkey = new_bank()
                mm_group(bg, bgkey, [(sg_[:, k, col], hT[1][:, k, :]) for k in range(8)], reads=[kg_] + h2k)
                bv, bvkey = new_bank()
                mm_group(bv, bvkey, [(sv_[:, k, col], hT[1][:, k, :]) for k in range(8)], reads=[kv_] + h2k)
                if jj == 3 or j == NJ - 1:
                    done_unit(ti, f"UG{u}")
                    done_unit(ti, f"UV{u}")
                accs = []
                for (bkx, bkxkey, q) in ((bg, bgkey, j), (bv, bvkey, NJ + j)):
                    a, akey = new_t512()
                    P.op("act", lambda e, a=a, bkx=bkx, q=q: e.activation(
                        out=a[:], in_=bkx[:], func=AF.Identity, scale=fw[:, 2, q:q + 1], bias=fb[:, q:q + 1]),
                        reads=[bkxkey, ("c", "fw"), ("c", "fb")], writes=[akey])
                    accs.append((a, akey, bkx, bkxkey, q))
                flush_pair()
                for (a, akey, bkx, bkxkey, q) in accs:
                    P.op("dve", lambda e, a=a, bkx=bkx, q=q: e.scalar_tensor_tensor(
                        out=a[:, 1:T], in0=bkx[:, 0:T - 1], scalar=fw[:, 1, q:q + 1], in1=a[:, 1:T],
                        op0=ALU.mult, op1=ALU.add),
                        reads=[bkxkey, akey, ("c", "fw")], writes=[akey])
                for (a, akey, bkx, bkxkey, q) in accs:
                    P.op("dve", lambda e, a=a, bkx=bkx, q=q: e.scalar_tensor_tensor(
                        out=a[:, 2:T], in0=bkx[:, 0:T - 2], scalar=fw[:, 0, q:q + 1], in1=a[:, 2:T],
                        op0=ALU.mult, op1=ALU.add),
                        reads=[bkxkey, akey, ("c", "fw")], writes=[akey])
                for (a, akey, bkx, bkxkey, q) in accs:
                    P.op("dve", lambda e, a=a, q=q: e.tensor_tensor(out=a[:, 0:2], in0=a[:, 0:2], in1=fH[:, q, :],
                                                                    op=ALU.add),
                         reads=[akey, ("fH",)], writes=[akey])
                    P.op("dve", lambda e, bkx=bkx, q=q: e.tensor_copy(out=fhalo[:, q, :], in_=bkx[:, T - 2:T]),
                         reads=[bkxkey, ("fH",)], writes=[("fh", q)])
                (ag, agkey, _, _, _), (av, avkey, _, _, _) = accs
                pend[0] = (ag, agkey, av, avkey, j)
            flush_pair()

        def phase_D(ti, mid_hook=None):
            b = ti % 2
            for hh in range(2):
                bks = [new_bank() for _ in range(4)]
                for gi, (k0, k1) in enumerate(KG):
                    swd, kwd = use_unit(ti, f"WD{hh}{gi}")

                    def fn(e, swd=swd, k0=k0, k1=k1, bks=bks):
                        ins = None
                        for k in range(k0, k1):
                            for s in range(4):
                                ins = e.matmul(bks[s][0][:], lhsT=aT_t[:, k, s * 128:(s + 1) * 128],
                                               rhs=swd[:, k - k0, :], start=(k == 0), stop=(k == NJ - 1))
                        return ins
                    P.pe_n.append(((k1 - k0) * 4, P.tag))
                    P.op("pe", fn, reads=[kwd] + [("aT", k) for k in range(k0, k1)], writes=[bk[1] for bk in bks])
                    done_unit(ti, f"WD{hh}{gi}")
                for s in range(4):
                    bk, bkey = bks[s]
                    P.op("dve", lambda e, bk=bk, s=s, hh=hh, b=b: e.tensor_tensor(
                        out=xt[b][:, s, hh * 512:(hh + 1) * 512], in0=bk[:], in1=xt[b][:, s, hh * 512:(hh + 1) * 512],
                        op=ALU.add),
                        reads=[bkey, ("xt", b, s, hh)], writes=[("xt", b, s, hh)])
                    if hh == 1:
                        norm_sumsq(2, b, s)
            if mid_hook is not None:
                mid_hook()
            norm_rstd(2)

            def final_scale(s, b=b):
                P.op("dve", lambda e, s=s, b=b: e.scalar_tensor_tensor(
                    out=xt[b][:, s, :], in0=xt[b][:, s, :], scalar=rstd[2][:, s:s + 1], in1=gfin[:],
                    op0=ALU.mult, op1=ALU.mult),
                    reads=[("xt", b, s, 0), ("xt", b, s, 1), ("rstd", 2, s), ("c", "gfin")],
                    writes=[("xt", b, s, 0), ("xt", b, s, 1)])
            return [lambda s=s: final_scale(s) for s in range(4)]

        emit_xload(0)
        emit_xload(1)
        for G in range(R):
            emit_load(G)
        emit_norm1(0)
        transposes(0, g1)
        phase_P(0)

        last_store = [None, None]
        deferred = None
        for ti in range(NT):
            b = ti % 2
            P.tag = f"C{ti}"
            j_hooks = None
            if deferred is not None:
                prev = ti - 1

                def tail(prev=prev, fs=deferred[3]):
                    fs()
                    last_store[prev % 2] = emit_xstore(prev)
                    emit_xload(prev + 2)
                j_hooks = [deferred[0], deferred[1], deferred[2], tail]
            phase_C(ti, j_hooks)
            P.tag = f"P2_{ti}"
            phase_P2(ti)
            P.tag = f"M{ti}"
            oc_hooks = None
            if ti + 1 < NT:
                nb = (ti + 1) % 2
                oc_hooks = [lambda s=s, nb=nb: norm_sumsq(0, nb, s) for s in range(3)]
                oc_hooks.append(lambda nb=nb: (norm_sumsq(0, nb, 3), norm_rstd(0)))
                oc_hooks += [lambda s=s, nb=nb: norm_scale(0, nb, s) for s in range(4)]
            phase_M(ti, oc_hooks)
            P.tag = f"T1n{ti}"
            if ti + 1 < NT:
                transposes(0, g1)
            P.tag = f"O{ti}"
            phase_O(ti)
            P.tag = f"T2_{ti}"
            transposes(1, g2)
            P.tag = f"F{ti}"
            phase_F(ti)
            P.tag = f"D{ti}"
            deferred = phase_D(ti, (lambda ti=ti: phase_P(ti + 1)) if ti + 1 < NT else None)
        P.tag = "tail"
        for fs in deferred:
            fs()
        last_store[(NT - 1) % 2] = emit_xstore(NT - 1)

        P.wait_only("sp", [ev for ev in last_store if ev is not None])

        block = E(nc.Block())

        def replay(stream, eng):
            for waits, fn, inc_sem, inc_amt in stream.ops:
                for sem, val in waits:
                    eng.wait_ge(sems[sem], val)
                if fn is not None:
                    ins = fn(eng)
                    ins.then_inc(sems[inc_sem], inc_amt)

        @block.tensor
        def _(eng):
            replay(P.streams["pe"], eng)

        @block.scalar
        def _(eng):
            replay(P.streams["act"], eng)

        @block.vector
        def _(eng):
            replay(P.streams["dve"], eng)

        @block.gpsimd
        def _(eng):
            replay(P.streams["pool"], eng)

        @block.sync
        def _(eng):
            replay(P.streams["sp"], eng)

    return nc


def _get_program():
    if "nc" not in _CACHE:
        _CACHE["nc"] = build_program()
    return _CACHE["nc"]


def _chunked(v, n):
    return np.ascontiguousarray(np.asarray(v, dtype=np.float32).reshape(n, 128).T)


def make_shared(I):
    f = lambda a: np.ascontiguousarray(np.asarray(a, dtype=np.float32))
    vecs = np.concatenate([_chunked(f(I["norm_mix"])[0], 8), _chunked(f(I["norm_ffn"])[0], 8),
                           _chunked(f(I["pool_scale"])[0], 8)], axis=1)
    cw = np.stack([_chunked(f(I["conv_w"])[0, k], 8) for k in range(3)], axis=1).reshape(128, 24)
    fw = np.stack([_chunked(f(I["ffn_conv_w"])[0, k], 44) for k in range(3)], axis=1).reshape(128, 132)
    fb = _chunked(f(I["ffn_conv_b"])[0], 44)
    gfin = np.ascontiguousarray(np.broadcast_to(f(I["norm_final"])[None, :], (128, D)))
    aux = np.zeros((128, 144), dtype=np.float32)
    aux[:, 0:128] = np.eye(128, dtype=np.float32)
    aux[:, 128:144] = (1.0 / np.arange(1, 17, dtype=np.float64)).astype(np.float32)[None, :]
    return {
        "w_in": f(I["w_in"])[0], "pool_w": f(I["pool_w"])[0].reshape(1024, 256),
        "w_pool_proj": f(I["w_pool_proj"])[0], "w_conv_out": f(I["w_conv_out"])[0], "w_o": f(I["w_o"])[0],
        "w_up": f(I["w_up"])[0], "w_down": f(I["w_down"])[0],
        "vecs": np.ascontiguousarray(vecs), "cw": np.ascontiguousarray(cw), "fw": np.ascontiguousarray(fw),
        "fb": fb, "gfin": gfin, "aux": aux,
    }


def kernel(x, norm_mix, w_in, pool_w, pool_scale, w_pool_proj, conv_w, w_conv_out, w_o,
           norm_ffn, w_up, ffn_conv_w, ffn_conv_b, w_down, norm_final):
    x = np.ascontiguousarray(np.asarray(x, dtype=np.float32))
    shared = make_shared(dict(norm_mix=norm_mix, w_in=w_in, pool_w=pool_w, pool_scale=pool_scale,
                              w_pool_proj=w_pool_proj, conv_w=conv_w, w_conv_out=w_conv_out, w_o=w_o,
                              norm_ffn=norm_ffn, w_up=w_up, ffn_conv_w=ffn_conv_w, ffn_conv_b=ffn_conv_b,
                              w_down=w_down, norm_final=norm_final))
    xf = x.reshape(N_CORES, TOK_PER_CORE, D)
    in_maps = [dict(shared, x=np.ascontiguousarray(xf[c])) for c in range(N_CORES)]
    nc = _get_program()
    res = run_bass_kernel_spmd(nc, in_maps, core_ids=list(range(N_CORES)))
    out = np.stack([np.asarray(r["out"], dtype=np.float32) for r in res.results], axis=0)
    return out.reshape(x.shape)
```

```python
from contextlib import ExitStack

import os
import numpy as np
import concourse.bass as bass
import concourse.mybir as mybir
from concourse.bass_utils import run_bass_kernel_spmd

F32 = mybir.dt.float32
BF16 = mybir.dt.bfloat16
AF = mybir.ActivationFunctionType
ALU = mybir.AluOpType

N_CORES = 8
D = 1024
SEQ = 2048
TOK_PER_CORE = 4096
T = 512
NT = TOK_PER_CORE // T
TILES_PER_SEQ = SEQ // T
D_FF = 2816
NJ = D_FF // 128
EPS = 1e-6
R = 8
NBANK = 6
NT512 = 7
NU = 3
NS = 3
POOL_WINDOWS = (2, 4, 8, 16)
DBG = os.environ.get("KDBG", "")
MUL_ENG = "dve" if "dvemul" in DBG else "pool"


_CACHE = {}


class Stream:
    def __init__(self, name):
        self.name = name
        self.count = 0
        self.ops = []
        self.waited = {}


class Prog:
    def __init__(self):
        self.streams = {n: Stream(n) for n in ("pe", "act", "dve", "pool", "sp")}
        self.res = {}
        self.semval = {}
        self.tag = ""
        self.evtag = {}
        self.pe_n = []

    def _deps(self, reads, writes):
        evs = []
        for k in reads:
            r = self.res.get(k)
            if r and r[0] is not None:
                evs.append(r[0])
        for k in writes:
            r = self.res.get(k)
            if r:
                if r[0] is not None:
                    evs.append(r[0])
                evs.extend(r[1])
        return evs

    def _commit(self, ev, reads, writes):
        for k in reads:
            r = self.res.setdefault(k, [None, []])
            r[1].append(ev)
        for k in writes:
            self.res[k] = [ev, []]

    def _waits(self, st, evs):
        need = {}
        for sem, val in evs:
            if st.name == "pe" and sem == "pe":
                continue
            if st.waited.get(sem, 0) >= val:
                continue
            if need.get(sem, 0) < val:
                need[sem] = val
        for sem, val in need.items():
            st.waited[sem] = val
        return list(need.items())

    def op(self, eng, fn, reads=(), writes=()):
        st = self.streams[eng]
        waits = self._waits(st, self._deps(reads, writes))
        st.count += 1
        ev = (eng, st.count)
        st.ops.append((waits, fn, eng, 1))
        self.evtag[ev] = self.tag
        self._commit(ev, reads, writes)
        return ev

    def dma(self, queue, sem, fn, reads=(), writes=()):
        st = self.streams[queue]
        evs = self._deps(reads, writes)
        prev = self.semval.get(sem, 0)
        if prev:
            evs.append((sem, prev))
        waits = self._waits(st, evs)
        val = prev + 16
        self.semval[sem] = val
        ev = (sem, val)
        st.ops.append((waits, fn, sem, 16))
        self.evtag[ev] = self.tag + "/dma"
        self._commit(ev, reads, writes)
        return ev

    def wait_only(self, eng, evs):
        st = self.streams[eng]
        waits = self._waits(st, evs)
        if waits:
            st.ops.append((waits, None, None, 0))


def build_program(NT=NT, TOK=TOK_PER_CORE, STOP=99):
    nc = bass.Bass("TRN2", target_bir_lowering=False)
    dt = nc.dram_tensor

    x_d = dt("x", [TOK, D], F32, kind="ExternalInput").ap()
    out_d = dt("out", [TOK, D], F32, kind="ExternalOutput").ap()
    w_in_d = dt("w_in", [D, 6 * D], F32, kind="ExternalInput").ap()
    pool_w_d = dt("pool_w", [1024, 256], F32, kind="ExternalInput").ap()
    w_pp_d = dt("w_pool_proj", [D, D], F32, kind="ExternalInput").ap()
    w_co_d = dt("w_conv_out", [D, D], F32, kind="ExternalInput").ap()
    w_o_d = dt("w_o", [D, D], F32, kind="ExternalInput").ap()
    w_up_d = dt("w_up", [D, 2 * D_FF], F32, kind="ExternalInput").ap()
    w_dn_d = dt("w_down", [D_FF, D], F32, kind="ExternalInput").ap()
    vecs_d = dt("vecs", [128, 24], F32, kind="ExternalInput").ap()
    cw_d = dt("cw", [128, 24], F32, kind="ExternalInput").ap()
    fw_d = dt("fw", [128, 132], F32, kind="ExternalInput").ap()
    fb_d = dt("fb", [128, 44], F32, kind="ExternalInput").ap()
    gfin_d = dt("gfin", [128, D], F32, kind="ExternalInput").ap()
    aux_d = dt("aux", [128, 144], F32, kind="ExternalInput").ap()

    wv_in = w_in_d.rearrange("(k p) n -> p k n", p=128)
    wv_pw = pool_w_d.rearrange("(gk p) c -> p gk c", p=128)
    wv_pp = w_pp_d.rearrange("(k p) n -> p k n", p=128)
    wv_co = w_co_d.rearrange("(k p) n -> p k n", p=128)
    wv_o = w_o_d.rearrange("(k p) n -> p k n", p=128)
    wv_up = w_up_d.rearrange("(k p) n -> p k n", p=128)
    wv_dn = w_dn_d.rearrange("(k p) n -> p k n", p=128)

    units = []

    def add_unit(name, src, nk, ncols):
        units.append((name, src, nk, ncols))

    add_unit("Z0", wv_in[:, :, 0:512], 8, 512)
    add_unit("Z1", wv_in[:, :, 512:1024], 8, 512)
    add_unit("PW", wv_pw, 8, 256)
    for hh in range(2):
        add_unit(f"C{hh}", wv_in[:, :, 2048 + hh * 512:2048 + (hh + 1) * 512], 8, 512)
        add_unit(f"V{hh}", wv_in[:, :, 3072 + hh * 512:3072 + (hh + 1) * 512], 8, 512)
        add_unit(f"B{hh}", wv_in[:, :, 1024 + hh * 512:1024 + (hh + 1) * 512], 8, 512)
    for hh in range(2):
        add_unit(f"PP{hh}", wv_pp[:, :, hh * 512:(hh + 1) * 512], 8, 512)
        add_unit(f"CO{hh}", wv_co[:, :, hh * 512:(hh + 1) * 512], 8, 512)
        add_unit(f"GP{hh}", wv_in[:, :, 4096 + hh * 512:4096 + (hh + 1) * 512], 8, 512)
        add_unit(f"GC{hh}", wv_in[:, :, 5120 + hh * 512:5120 + (hh + 1) * 512], 8, 512)
    for hh in range(2):
        add_unit(f"WO{hh}", wv_o[:, :, hh * 512:(hh + 1) * 512], 8, 512)
    for u in range(6):
        nc_ = 512 if u < 5 else 256
        add_unit(f"UG{u}", wv_up[:, :, u * 512:u * 512 + nc_], 8, nc_)
        add_unit(f"UV{u}", wv_up[:, :, D_FF + u * 512:D_FF + u * 512 + nc_], 8, nc_)
    KG = [(0, 8), (8, 16), (16, 22)]
    for hh in range(2):
        for gi, (k0, k1) in enumerate(KG):
            add_unit(f"WD{hh}{gi}", wv_dn[:, k0:k1, hh * 512:(hh + 1) * 512], k1 - k0, 512)
    NUNITS = len(units)
    uidx = {u[0]: i for i, u in enumerate(units)}

    scr_d = dt("wscr", [NUNITS, 128, 8, 512], BF16, kind="Internal").ap()

    P = Prog()
    _CACHE["prog"] = P
    es = ExitStack()
    E = es.enter_context
    with es:
        xt = [E(nc.sbuf_tensor(f"xt{b}", [128, 4, D], F32)) for b in range(2)]
        xs = E(nc.sbuf_tensor("xs0", [128, 4, D], BF16))
        hT = [E(nc.sbuf_tensor(f"hT{i}", [128, 8, T], BF16)) for i in range(2)]
        A_t = E(nc.sbuf_tensor("A_t", [128, 8, T], BF16))
        p2_t = E(nc.sbuf_tensor("p2_t", [128, 8, T], BF16))
        cv_t = E(nc.sbuf_tensor("cv_t", [128, 8, T], BF16))
        aT_t = E(nc.sbuf_tensor("aT_t", [128, NJ, T], BF16))
        slots = [E(nc.sbuf_tensor(f"slot{r}", [128, 8, 512], BF16)) for r in range(R)]
        t512 = [E(nc.sbuf_tensor(f"t512_{i}", [128, T], F32)) for i in range(NT512)]
        ubuf = [E(nc.sbuf_tensor(f"ubuf{i}", [128, T + 16], F32)) for i in range(NU)]
        sbuf_ = [E(nc.sbuf_tensor(f"sbuf{i}", [128, T + 16], F32)) for i in range(NS)]
        junk = E(nc.sbuf_tensor("junk", [128, D], BF16))
        vecs = E(nc.sbuf_tensor("vecs_t", [128, 24], F32))
        cw = E(nc.sbuf_tensor("cw_t", [128, 3, 8], F32))
        fw = E(nc.sbuf_tensor("fw_t", [128, 3, 44], F32))
        fb = E(nc.sbuf_tensor("fb_t", [128, 44], F32))
        gfin = E(nc.sbuf_tensor("gfin_t", [128, D], F32))
        aux = E(nc.sbuf_tensor("aux_t", [128, 144], F32))
        identb = E(nc.sbuf_tensor("identb", [128, 128], BF16))
        mhalf = E(nc.sbuf_tensor("mhalf", [128, 4], F32))
        ss = [E(nc.sbuf_tensor(f"ss{i}", [128, 4], F32)) for i in range(3)]
        ms = [E(nc.sbuf_tensor(f"ms{i}", [128, 4], F32)) for i in range(3)]
        rstd = [E(nc.sbuf_tensor(f"rstd{i}", [128, 4], F32)) for i in range(3)]
        phalo = E(nc.sbuf_tensor("phalo", [128, 8, 16], F32))
        chalo = E(nc.sbuf_tensor("chalo", [128, 8, 2], F32))
        fhalo = E(nc.sbuf_tensor("fhalo", [128, 44, 2], F32))
        fH = E(nc.sbuf_tensor("fH", [128, 44, 2], F32))
        ftmp = E(nc.sbuf_tensor("ftmp", [128, 44], F32))
        pfix = E(nc.sbuf_tensor("pfix", [128, 16], F32))
        banks = [E(nc.psum_tensor(f"bank{i}", [128, 512], F32)) for i in range(NBANK)]
        tbank = [E(nc.psum_tensor(f"tbank{i}", [128, 512], BF16)) for i in range(2)]
        sem_names = ["pe", "act", "dve", "pool", "x0", "x1"] + ["c_" + n for n in ("vecs", "aux", "cw", "fw", "fb", "gfin")] + [f"w{r}" for r in range(R)]
        sems = {n: E(nc.semaphore("sem_" + n)) for n in sem_names}

        g1 = lambda k: vecs[:, k:k + 1]
        g2 = lambda k: vecs[:, 8 + k:9 + k]
        psc = lambda k: vecs[:, 16 + k:17 + k]
        invcnt = aux[:, 128:144]

        cnt = {"bank": 0, "t512": 0, "u": 0, "tb": 0, "s": 0}

        def new_bank():
            i = cnt["bank"] % NBANK
            cnt["bank"] += 1
            return banks[i], ("bank", i)

        def new_t512():
            i = cnt["t512"] % NT512
            cnt["t512"] += 1
            return t512[i], ("t512", i)

        def new_u():
            i = cnt["u"] % NU
            cnt["u"] += 1
            return ubuf[i], ("u", i), ("uh", i)

        def new_s():
            i = cnt["s"] % NS
            cnt["s"] += 1
            return sbuf_[i], ("sb", i)

        def new_tb():
            i = cnt["tb"] % 2
            cnt["tb"] += 1
            return tbank[i], ("tb", i)

        def const_loads(names):
            table = {
                "vecs": (lambda e: e.dma_start(out=vecs[:], in_=vecs_d)),
                "aux": (lambda e: e.dma_start(out=aux[:], in_=aux_d)),
                "cw": (lambda e: e.dma_start(out=cw[:].rearrange("p a b -> p (a b)"), in_=cw_d)),
                "fw": (lambda e: e.dma_start(out=fw[:].rearrange("p a b -> p (a b)"), in_=fw_d)),
                "fb": (lambda e: e.dma_start(out=fb[:], in_=fb_d)),
                "gfin": (lambda e: e.dma_start(out=gfin[:], in_=gfin_d)),
            }
            for n in names:
                P.dma("sp", "c_" + n, table[n], writes=[("c", n)])
        const_loads(["aux", "vecs"])
        P.op("pool", lambda e: e.memset(mhalf[:], -0.5), writes=[("c", "mhalf")])
        P.op("dve", lambda e: e.tensor_copy(out=identb[:], in_=aux[:, 0:128]),
             reads=[("c", "aux")], writes=[("c", "identb")])
        CONST_KEYS = [("c", n) for n in ("vecs", "cw", "fw", "fb", "gfin", "aux", "mhalf", "identb")]

        per_tile = ["C0", "V0", "B0", "C1", "V1", "B1", "PW",
                    "PP0", "CO0", "GP0", "GC0", "PP1", "CO1", "GP1", "GC1", "WO0", "WO1"]
        ffn_units = [f"U{t}{u}" for u in range(6) for t in ("G", "V")] + \
                    [f"WD{hh}{gi}" for hh in range(2) for gi in range(3)]
        sched = ["Z0", "Z1"]
        Gof = {(0, "Z0"): 0, (0, "Z1"): 1}
        for ti_ in range(NT):
            for nm in per_tile:
                Gof[(ti_, nm)] = len(sched)
                sched.append(nm)
            for nm in ffn_units:
                Gof[(ti_, nm)] = len(sched)
                sched.append(nm)
            if ti_ + 1 < NT:
                for nm in ("Z0", "Z1"):
                    Gof[(ti_ + 1, nm)] = len(sched)
                    sched.append(nm)
        seen_units = set()

        def slot_key(G):
            return ("slot", G % R)

        def emit_load(G):
            if G >= len(sched):
                return
            loaded.add(G)
            n = uidx[sched[G]]
            name, src, nk, ncols = units[n]
            slot = slots[G % R]
            semn = f"w{G % R}"
            if name not in seen_units:
                seen_units.add(name)
                dst = slot[:, 0:nk, 0:ncols]
                P.dma("pool", semn, lambda e: e.dma_start(out=dst, in_=src), writes=[slot_key(G)])
                if NT > 1:
                    P.dma("sp", semn, lambda e: e.dma_start(out=scr_d[n], in_=slot[:]),
                          reads=[slot_key(G)], writes=[("scr", n)])
            else:
                P.dma("sp", semn, lambda e: e.dma_start(out=slot[:], in_=scr_d[n]),
                      reads=[("scr", n)], writes=[slot_key(G)])

        loaded = set()
        used_max = [-1]

        def use_unit(ti, name):
            G = Gof[(ti, name)]
            assert G in loaded, (ti, name, G)
            assert G + R > used_max[0], (ti, name, G, used_max[0])
            used_max[0] = max(used_max[0], G)
            return slots[G % R], slot_key(G)

        def done_unit(ti, name):
            emit_load(Gof[(ti, name)] + R)

        def emit_xload(ti):
            if ti >= NT:
                return
            b = ti % 2
            src = x_d[ti * T:(ti + 1) * T, :].rearrange("(s p) d -> p s d", p=128)
            P.dma("sp", f"x{b}", lambda e: e.dma_start(out=xt[b][:], in_=src),
                  writes=[("xt", b, s, h) for s in range(4) for h in range(2)])

        def emit_xstore(ti):
            b = ti % 2
            dst = out_d[ti * T:(ti + 1) * T, :].rearrange("(s p) d -> p s d", p=128)
            return P.dma("sp", f"x{b}", lambda e: e.dma_start(out=dst, in_=xt[b][:]),
                         reads=[("xt", b, s, h) for s in range(4) for h in range(2)])

        def mm_group(bank_ap, bank_key, pairs, reads):
            n = len(pairs)

            def fn(e):
                ins = None
                for i, (l, r) in enumerate(pairs):
                    ins = e.matmul(bank_ap[:], lhsT=l, rhs=r, start=(i == 0), stop=(i == n - 1))
                return ins
            P.pe_n.append((n, P.tag))
            return P.op("pe", fn, reads=reads, writes=[bank_key])

        def norm_sumsq(which, b, s):
            P.op("act", lambda e: e.activation(out=junk[:], in_=xt[b][:, s, :], func=AF.Square,
                                               accum_out=ss[which][:, s:s + 1]),
                 reads=[("xt", b, s, 0), ("xt", b, s, 1)], writes=[("ss", which, s)])

        def norm_rstd1(which, s):
            P.op("dve", lambda e: e.tensor_scalar(out=ms[which][:, s:s + 1], in0=ss[which][:, s:s + 1],
                                                  scalar1=1.0 / D, scalar2=EPS, op0=ALU.mult, op1=ALU.add),
                 reads=[("ss", which, s)], writes=[("ms", which, s)])
            P.op("pool", lambda e: e.tensor_tensor(out=rstd[which][:, s:s + 1], in0=ms[which][:, s:s + 1],
                                                   in1=mhalf[:, 0:1], op=ALU.pow),
                 reads=[("ms", which, s), ("c", "mhalf")], writes=[("rstd", which, s)])

        def norm_rstd(which):
            allk = lambda n: [(n, which, s) for s in range(4)]
            P.op("dve", lambda e: e.tensor_scalar(out=ms[which][:], in0=ss[which][:],
                                                  scalar1=1.0 / D, scalar2=EPS, op0=ALU.mult, op1=ALU.add),
                 reads=allk("ss"), writes=allk("ms"))
            P.op("pool", lambda e: e.tensor_tensor(out=rstd[which][:], in0=ms[which][:], in1=mhalf[:], op=ALU.pow),
                 reads=allk("ms") + [("c", "mhalf")], writes=allk("rstd"))

        def norm_scale(which, b, s):
            P.op("act", lambda e: e.activation(out=xs[:, s, :], in_=xt[b][:, s, :], func=AF.Copy,
                                               scale=rstd[which][:, s:s + 1]),
                 reads=[("xt", b, s, 0), ("xt", b, s, 1), ("rstd", which, s)], writes=[("xs", s)])

        def transposes(hb, gfun, act_only=False):
            for k in range(8):
                tb_ap, tb_key = new_tb()

                def fn(e, k=k, tb_ap=tb_ap):
                    ins = None
                    for s in range(4):
                        ins = e.transpose(out=tb_ap[:, s * 128:(s + 1) * 128],
                                          in_=xs[:, s, k * 128:(k + 1) * 128], identity=identb[:])
                    return ins
                P.pe_n.append((4, P.tag + "/T"))
                P.op("pe", fn, reads=[("xs", s) for s in range(4)] + [("c", "identb")], writes=[tb_key])
                if act_only or k % 2 == 0:
                    P.op("act", lambda e, k=k, tb_ap=tb_ap: e.activation(out=hT[hb][:, k, :], in_=tb_ap[:],
                                                                         func=AF.Copy, scale=gfun(k)),
                         reads=[tb_key, ("c", "vecs")], writes=[("hT", hb, k)])
                else:
                    P.op("dve", lambda e, k=k, tb_ap=tb_ap: e.tensor_scalar(out=hT[hb][:, k, :], in0=tb_ap[:],
                                                                            scalar1=gfun(k), scalar2=None,
                                                                            op0=ALU.mult),
                         reads=[tb_key, ("c", "vecs")], writes=[("hT", hb, k)])

        def emit_norm1(ti):
            b = ti % 2
            for s in range(4):
                norm_sumsq(0, b, s)
            norm_rstd(0)
            for s in range(4):
                norm_scale(0, b, s)

        def phase_P(ti):
            first = (ti % TILES_PER_SEQ == 0)
            hb = 0
            hTk = [("hT", hb, k) for k in range(8)]
            for c in range(8):
                g = c // 2
                win = POOL_WINDOWS[g]
                uname = "Z0" if c < 4 else "Z1"
                slot, skey = use_unit(ti, uname)
                cc = c % 4
                bk, bkey = new_bank()
                mm_group(bk, bkey, [(slot[:, k, cc * 128:(cc + 1) * 128], hT[hb][:, k, :]) for k in range(8)],
                         reads=[skey] + hTk)
                if cc == 3:
                    done_unit(ti, uname)
                U, ukey, uhkey = new_u()
                if first:
                    P.op("pool", lambda e, U=U: e.memset(U[:, 0:16], 0.0), writes=[uhkey])
                else:
                    P.op("pool", lambda e, U=U, c=c: e.tensor_copy(out=U[:, 0:16], in_=phalo[:, c, :]),
                         reads=[("ph", c)], writes=[uhkey])
                P.op("act", lambda e, U=U, bk=bk: e.activation(out=U[:, 16:16 + T], in_=bk[:], func=AF.Copy),
                     reads=[bkey], writes=[ukey])
                P.op("pool", lambda e, U=U, c=c: e.tensor_copy(out=phalo[:, c, :], in_=U[:, T:T + 16]),
                     reads=[ukey], writes=[("ph", c)])
                src, srckeys = U, [ukey, uhkey]
                off = 1
                for lvl in range(g + 1):
                    S, skey2 = new_s()
                    lo = 2 * off - 1
                    P.op("dve", lambda e, S=S, src=src, off=off, lo=lo: e.tensor_tensor(
                        out=S[:, lo:T + 16], in0=src[:, lo:T + 16], in1=src[:, lo - off:T + 16 - off], op=ALU.add),
                        reads=srckeys, writes=[skey2])
                    src, srckeys = S, [skey2]
                    off *= 2
                P.op("dve", lambda e, S=src, U=U, c=c, win=win: e.scalar_tensor_tensor(
                    out=A_t[:, c, :], in0=S[:, 16:16 + T], scalar=1.0 / win, in1=U[:, 16:16 + T],
                    op0=ALU.mult, op1=ALU.subtract),
                    reads=srckeys + [ukey], writes=[("A", c)])
                if first:
                    nfix = win - 1
                    P.op("dve", lambda e, S=src, nfix=nfix: e.tensor_tensor(
                        out=pfix[:, 0:nfix], in0=S[:, 16:16 + nfix], in1=invcnt[:, 0:nfix], op=ALU.mult),
                        reads=srckeys + [("c", "aux")], writes=[("pfix",)])
                    P.op("dve", lambda e, U=U, c=c, nfix=nfix: e.tensor_tensor(
                        out=A_t[:, c, 0:nfix], in0=pfix[:, 0:nfix], in1=U[:, 16:16 + nfix], op=ALU.subtract),
                        reads=[("pfix",), ukey], writes=[("A", c)])

        def phase_P2(ti):
            slot, skey = use_unit(ti, "PW")
            for g in range(4):
                for o in range(2):
                    c = 2 * g + o
                    bk, bkey = new_bank()
                    mm_group(bk, bkey,
                             [(slot[:, 2 * g + k2, o * 128:(o + 1) * 128], A_t[:, 2 * g + k2, :]) for k2 in range(2)],
                             reads=[skey, ("A", 2 * g), ("A", 2 * g + 1)])
                    if c % 2 == 0:
                        P.op("act", lambda e, bk=bk, c=c: e.activation(out=p2_t[:, c, :], in_=bk[:], func=AF.Copy,
                                                                      scale=psc(c)),
                             reads=[bkey, ("c", "vecs")], writes=[("p2", c)])
                    else:
                        P.op("dve", lambda e, bk=bk, c=c: e.tensor_scalar(out=p2_t[:, c, :], in0=bk[:],
                                                                         scalar1=psc(c), scalar2=None, op0=ALU.mult),
                             reads=[bkey, ("c", "vecs")], writes=[("p2", c)])
            done_unit(ti, "PW")

        def phase_C(ti, j_hooks=None):
            first = (ti % TILES_PER_SEQ == 0)
            hb = 0
            hTk = [("hT", hb, k) for k in range(8)]
            for j in range(8):
                hh, jj = divmod(j, 4)
                sc, kc = use_unit(ti, f"C{hh}")
                sv, kv = use_unit(ti, f"V{hh}")
                sb_, kb = use_unit(ti, f"B{hh}")
                bc, bckey = new_bank()
                mm_group(bc, bckey, [(sc[:, k, jj * 128:(jj + 1) * 128], hT[hb][:, k, :]) for k in range(8)],
                         reads=[kc] + hTk)
                bv, bvkey = new_bank()
                mm_group(bv, bvkey, [(sv[:, k, jj * 128:(jj + 1) * 128], hT[hb][:, k, :]) for k in range(8)],
                         reads=[kv] + hTk)
                bb, bbkey = new_bank()
                mm_group(bb, bbkey, [(sb_[:, k, jj * 128:(jj + 1) * 128], hT[hb][:, k, :]) for k in range(8)],
                         reads=[kb] + hTk)
                if jj == 3:
                    done_unit(ti, f"C{hh}")
                    done_unit(ti, f"V{hh}")
                    done_unit(ti, f"B{hh}")
                zc, zckey = new_t512()
                P.op("act", lambda e, zc=zc, bc=bc: e.activation(out=zc[:], in_=bc[:], func=AF.Copy),
                     reads=[bckey], writes=[zckey])
                CV, cvkey, cvhkey = new_u()
                if first:
                    P.op("pool", lambda e, CV=CV: e.memset(CV[:, 0:2], 0.0), writes=[cvhkey])
                else:
                    P.op("pool", lambda e, CV=CV, j=j: e.tensor_copy(out=CV[:, 0:2], in_=chalo[:, j, :]),
                         reads=[("ch", j)], writes=[cvhkey])
                P.op("dve", lambda e, CV=CV, bv=bv, zc=zc: e.tensor_tensor(out=CV[:, 2:2 + T], in0=bv[:], in1=zc[:],
                                                                           op=ALU.mult),
                     reads=[bvkey, zckey], writes=[cvkey])
                P.op("pool", lambda e, CV=CV, j=j: e.tensor_copy(out=chalo[:, j, :], in_=CV[:, T:T + 2]),
                     reads=[cvkey], writes=[("ch", j)])
                acc, acckey = new_t512()
                P.op("act", lambda e, acc=acc, CV=CV, j=j: e.activation(out=acc[:], in_=CV[:, 2:2 + T], func=AF.Copy,
                                                                       scale=cw[:, 2, j:j + 1]),
                     reads=[cvkey, ("c", "cw")], writes=[acckey])
                P.op("dve", lambda e, acc=acc, CV=CV, j=j: e.scalar_tensor_tensor(
                    out=acc[:], in0=CV[:, 1:1 + T], scalar=cw[:, 1, j:j + 1], in1=acc[:], op0=ALU.mult, op1=ALU.add),
                    reads=[cvkey, cvhkey, acckey, ("c", "cw")], writes=[acckey])
                P.op("dve", lambda e, acc=acc, CV=CV, j=j: e.scalar_tensor_tensor(
                    out=acc[:], in0=CV[:, 0:T], scalar=cw[:, 0, j:j + 1], in1=acc[:], op0=ALU.mult, op1=ALU.add),
                    reads=[cvkey, cvhkey, acckey, ("c", "cw")], writes=[acckey])
                P.op("dve", lambda e, acc=acc, bb=bb, j=j: e.tensor_tensor(out=cv_t[:, j, :], in0=bb[:], in1=acc[:],
                                                                          op=ALU.mult),
                     reads=[bbkey, acckey], writes=[("cv", j)])
                if j_hooks and j < len(j_hooks):
                    j_hooks[j]()

        def phase_M(ti, oc_hooks=None):
            hb = 0
            hTk = [("hT", hb, k) for k in range(8)]
            for oc in range(8):
                hh, oo = divmod(oc, 4)
                spp, kpp = use_unit(ti, f"PP{hh}")
                sco, kco = use_unit(ti, f"CO{hh}")
                sgp_, kgp = use_unit(ti, f"GP{hh}")
                sgc_, kgc = use_unit(ti, f"GC{hh}")
                col = slice(oo * 128, (oo + 1) * 128)
                bgp, bgpkey = new_bank()
                mm_group(bgp, bgpkey, [(sgp_[:, k, col], hT[hb][:, k, :]) for k in range(8)], reads=[kgp] + hTk)
                bgc, bgckey = new_bank()
                mm_group(bgc, bgckey, [(sgc_[:, k, col], hT[hb][:, k, :]) for k in range(8)], reads=[kgc] + hTk)
                byp, bypkey = new_bank()
                mm_group(byp, bypkey, [(spp[:, k, col], p2_t[:, k, :]) for k in range(8)],
                         reads=[kpp] + [("p2", k) for k in range(8)])
                byc, byckey = new_bank()
                mm_group(byc, byckey, [(sco[:, k, col], cv_t[:, k, :]) for k in range(8)],
                         reads=[kco] + [("cv", k) for k in range(8)])
                if oo == 3:
                    for nm in ("PP", "CO", "GP", "GC"):
                        done_unit(ti, f"{nm}{hh}")
                s1, s1key = new_t512()
                s2, s2key = new_t512()
                P.op("act", lambda e, s1=s1, bgp=bgp: e.activation(out=s1[:], in_=bgp[:], func=AF.Sigmoid),
                     reads=[bgpkey], writes=[s1key])
                P.op("act", lambda e, s2=s2, bgc=bgc: e.activation(out=s2[:], in_=bgc[:], func=AF.Sigmoid),
                     reads=[bgckey], writes=[s2key])
                P.op("dve", lambda e, s1=s1, byp=byp: e.tensor_tensor(out=s1[:], in0=byp[:], in1=s1[:], op=ALU.mult),
                     reads=[bypkey, s1key], writes=[s1key])
                P.op("dve", lambda e, s2=s2, byc=byc: e.tensor_tensor(out=s2[:], in0=byc[:], in1=s2[:], op=ALU.mult),
                     reads=[byckey, s2key], writes=[s2key])
                P.op("dve", lambda e, s1=s1, s2=s2, oc=oc: e.tensor_tensor(out=A_t[:, oc, :], in0=s1[:], in1=s2[:],
                                                                          op=ALU.add),
                     reads=[s1key, s2key], writes=[("A", oc)])
                if oc_hooks and oc < len(oc_hooks):
                    oc_hooks[oc]()

        def phase_O(ti):
            b = ti % 2
            for hh in range(2):
                swo, kwo = use_unit(ti, f"WO{hh}")
                for s in range(4):
                    bk, bkey = new_bank()
                    mm_group(bk, bkey, [(A_t[:, k, s * 128:(s + 1) * 128], swo[:, k, :]) for k in range(8)],
                             reads=[kwo] + [("A", k) for k in range(8)])
                    P.op("dve", lambda e, bk=bk, s=s, hh=hh, b=b: e.tensor_tensor(
                        out=xt[b][:, s, hh * 512:(hh + 1) * 512], in0=bk[:], in1=xt[b][:, s, hh * 512:(hh + 1) * 512],
                        op=ALU.add),
                        reads=[bkey, ("xt", b, s, hh)], writes=[("xt", b, s, hh)])
                    if hh == 1:
                        norm_sumsq(1, b, s)
                        norm_rstd1(1, s)
                        if s >= 1:
                            norm_scale(1, b, s - 1)
                done_unit(ti, f"WO{hh}")
            norm_scale(1, b, 3)

        def phase_F(ti):
            first = (ti % TILES_PER_SEQ == 0)
            h2k = [("hT", 1, k) for k in range(8)]
            if first:
                P.op("pool", lambda e: e.memset(fH[:].rearrange("p a b -> p (a b)"), 0.0), writes=[("fH",)])
            else:
                fhk = [("fh", q) for q in range(44)]
                P.op("pool", lambda e: e.tensor_tensor(out=fH[:, :, 0], in0=fw[:, 1, :], in1=fhalo[:, :, 1],
                                                       op=ALU.mult),
                     reads=[("c", "fw")] + fhk, writes=[("fH",)])
                P.op("pool", lambda e: e.tensor_tensor(out=ftmp[:], in0=fw[:, 0, :], in1=fhalo[:, :, 0],
                                                       op=ALU.mult),
                     reads=[("c", "fw")] + fhk, writes=[("ftmp",)])
                P.op("pool", lambda e: e.tensor_tensor(out=fH[:, :, 0], in0=fH[:, :, 0], in1=ftmp[:], op=ALU.add),
                     reads=[("ftmp",), ("fH",)], writes=[("fH",)])
                P.op("pool", lambda e: e.tensor_tensor(out=fH[:, :, 1], in0=fw[:, 0, :], in1=fhalo[:, :, 1],
                                                       op=ALU.mult),
                     reads=[("c", "fw")] + fhk, writes=[("fH",)])
            pend = [None]

            def flush_pair():
                if pend[0] is None:
                    return
                ag, agkey, av, avkey, jp = pend[0]
                pend[0] = None
                P.op("act", lambda e, ag=ag: e.activation(out=ag[:], in_=ag[:], func=AF.Silu),
                     reads=[agkey], writes=[agkey])
                P.op("pool", lambda e, ag=ag, av=av, jp=jp: e.tensor_tensor(out=aT_t[:, jp, :], in0=ag[:], in1=av[:],
                                                                           op=ALU.mult),
                     reads=[agkey, avkey], writes=[("aT", jp)])

            for j in range(NJ):
                u, jj = divmod(j, 4)
                sg_, kg_ = use_unit(ti, f"UG{u}")
                sv_, kv_ = use_unit(ti, f"UV{u}")
                col = slice(jj * 128, (jj + 1) * 128)
                bg, bgkey = new_bank()
                mm_group(bg, bgkey, [(sg_[:, k, col], hT[1][:, k, :]) for k in range(8)], reads=[kg_] + h2k)
                bv, bvkey = new_bank()
                mm_group(bv, bvkey, [(sv_[:, k, col], hT[1][:, k, :]) for k in range(8)], reads=[kv_] + h2k)
                if jj == 3 or j == NJ - 1:
                    done_unit(ti, f"UG{u}")
                    done_unit(ti, f"UV{u}")
                accs = []
                for (bkx, bkxkey, q) in ((bg, bgkey, j), (bv, bvkey, NJ + j)):
                    a, akey = new_t512()
                    P.op("act", lambda e, a=a, bkx=bkx, q=q: e.activation(
                        out=a[:], in_=bkx[:], func=AF.Identity, scale=fw[:, 2, q:q + 1], bias=fb[:, q:q + 1]),
                        reads=[bkxkey, ("c", "fw"), ("c", "fb")], writes=[akey])
                    accs.append((a, akey, bkx, bkxkey, q))
                flush_pair()
                for (a, akey, bkx, bkxkey, q) in accs:
                    P.op("dve", lambda e, a=a, bkx=bkx, q=q: e.scalar_tensor_tensor(
                        out=a[:, 1:T], in0=bkx[:, 0:T - 1], scalar=fw[:, 1, q:q + 1], in1=a[:, 1:T],
                        op0=ALU.mult, op1=ALU.add),
                        reads=[bkxkey, akey, ("c", "fw")], writes=[akey])
                for (a, akey, bkx, bkxkey, q) in accs:
                    P.op("dve", lambda e, a=a, bkx=bkx, q=q: e.scalar_tensor_tensor(
                        out=a[:, 2:T], in0=bkx[:, 0:T - 2], scalar=fw[:, 0, q:q + 1], in1=a[:, 2:T],
                        op0=ALU.mult, op1=ALU.add),
                        reads=[bkxkey, akey, ("c", "fw")], writes=[akey])
                for (a, akey, bkx, bkxkey, q) in accs:
                    P.op("dve", lambda e, a=a, q=q: e.tensor_tensor(out=a[:, 0:2], in0=a[:, 0:2], in1=fH[:, q, :],
                                                                    op=ALU.add),
                         reads=[akey, ("fH",)], writes=[akey])
                    P.op("dve", lambda e, bkx=bkx, q=q: e.tensor_copy(out=fhalo[:, q, :], in_=bkx[:, T - 2:T]),
                         reads=[bkxkey, ("fH",)], writes=[("fh", q)])
                (ag, agkey, _, _, _), (av, avkey, _, _, _) = accs
                pend[0] = (ag, agkey, av, avkey, j)
            flush_pair()

        def phase_D(ti, mid_hook=None):
            b = ti % 2
            for hh in range(2):
                bks = [new_bank() for _ in range(4)]
                for gi, (k0, k1) in enumerate(KG):
                    swd, kwd = use_unit(ti, f"WD{hh}{gi}")

                    def fn(e, swd=swd, k0=k0, k1=k1, bks=bks):
                        ins = None
                        for k in range(k0, k1):
                            for s in range(4):
                                ins = e.matmul(bks[s][0][:], lhsT=aT_t[:, k, s * 128:(s + 1) * 128],
                                               rhs=swd[:, k - k0, :], start=(k == 0), stop=(k == NJ - 1))
                        return ins
                    P.pe_n.append(((k1 - k0) * 4, P.tag))
                    P.op("pe", fn, reads=[kwd] + [("aT", k) for k in range(k0, k1)], writes=[bk[1] for bk in bks])
                    done_unit(ti, f"WD{hh}{gi}")
                for s in range(4):
                    bk, bkey = bks[s]
                    P.op("dve", lambda e, bk=bk, s=s, hh=hh, b=b: e.tensor_tensor(
                        out=xt[b][:, s, hh * 512:(hh + 1) * 512], in0=bk[:], in1=xt[b][:, s, hh * 512:(hh + 1) * 512],
                        op=ALU.add),
                        reads=[bkey, ("xt", b, s, hh)], writes=[("xt", b, s, hh)])
                    if hh == 1:
                        norm_sumsq(2, b, s)
            if mid_hook is not None:
                mid_hook()
            norm_rstd(2)

            def final_scale(s, b=b):
                P.op("dve", lambda e, s=s, b=b: e.scalar_tensor_tensor(
                    out=xt[b][:, s, :], in0=xt[b][:, s, :], scalar=rstd[2][:, s:s + 1], in1=gfin[:],
                    op0=ALU.mult, op1=ALU.mult),
                    reads=[("xt", b, s, 0), ("xt", b, s, 1), ("rstd", 2, s), ("c", "gfin")],
                    writes=[("xt", b, s, 0), ("xt", b, s, 1)])
            return [lambda s=s: final_scale(s) for s in range(4)]

        emit_xload(0)
        const_loads(["cw", "fw", "fb", "gfin"])
        emit_xload(1)
        for G in range(R):
            emit_load(G)
        emit_norm1(0)
        transposes(0, g1)
        phase_P(0)

        last_store = [None, None]
        deferred = None
        for ti in range(NT):
            b = ti % 2
            P.tag = f"C{ti}"
            j_hooks = None
            if deferred is not None:
                prev = ti - 1

                def tail(prev=prev, fs=deferred[3]):
                    fs()
                    last_store[prev % 2] = emit_xstore(prev)
                    emit_xload(prev + 2)
                j_hooks = [deferred[0], deferred[1], deferred[2], tail]
            phase_C(ti, j_hooks)
            P.tag = f"P2_{ti}"
            phase_P2(ti)
            P.tag = f"M{ti}"
            oc_hooks = None
            if ti + 1 < NT:
                nb = (ti + 1) % 2
                oc_hooks = [lambda s=s, nb=nb: norm_sumsq(0, nb, s) for s in range(3)]
                oc_hooks.append(lambda nb=nb: (norm_sumsq(0, nb, 3), norm_rstd(0)))
                oc_hooks += [lambda s=s, nb=nb: norm_scale(0, nb, s) for s in range(4)]
            phase_M(ti, oc_hooks)
            P.tag = f"T1n{ti}"
            if ti + 1 < NT:
                transposes(0, g1)
            P.tag = f"O{ti}"
            phase_O(ti)
            P.tag = f"T2_{ti}"
            transposes(1, g2)
            P.tag = f"F{ti}"
            phase_F(ti)
            P.tag = f"D{ti}"
            deferred = phase_D(ti, (lambda ti=ti: phase_P(ti + 1)) if ti + 1 < NT else None)
        P.tag = "tail"
        for fs in deferred:
            fs()
        last_store[(NT - 1) % 2] = emit_xstore(NT - 1)

        P.wait_only("sp", [ev for ev in last_store if ev is not None])

        block = E(nc.Block())

        def replay(stream, eng):
            for waits, fn, inc_sem, inc_amt in stream.ops:
                for sem, val in waits:
                    eng.wait_ge(sems[sem], val)
                if fn is not None:
                    ins = fn(eng)
                    ins.then_inc(sems[inc_sem], inc_amt)

        @block.tensor
        def _(eng):
            replay(P.streams["pe"], eng)

        @block.scalar
        def _(eng):
            replay(P.streams["act"], eng)

        @block.vector
        def _(eng):
            replay(P.streams["dve"], eng)

        @block.gpsimd
        def _(eng):
            replay(P.streams["pool"], eng)

        @block.sync
        def _(eng):
            replay(P.streams["sp"], eng)

    return nc


def _get_program():
    if "nc" not in _CACHE:
        _CACHE["nc"] = build_program()
    return _CACHE["nc"]


def _chunked(v, n):
    return np.ascontiguousarray(np.asarray(v, dtype=np.float32).reshape(n, 128).T)


def make_shared(I):
    f = lambda a: np.ascontiguousarray(np.asarray(a, dtype=np.float32))
    vecs = np.concatenate([_chunked(f(I["norm_mix"])[0], 8), _chunked(f(I["norm_ffn"])[0], 8),
                           _chunked(f(I["pool_scale"])[0], 8)], axis=1)
    cw = np.stack([_chunked(f(I["conv_w"])[0, k], 8) for k in range(3)], axis=1).reshape(128, 24)
    fw = np.stack([_chunked(f(I["ffn_conv_w"])[0, k], 44) for k in range(3)], axis=1).reshape(128, 132)
    fb = _chunked(f(I["ffn_conv_b"])[0], 44)
    gfin = np.ascontiguousarray(np.broadcast_to(f(I["norm_final"])[None, :], (128, D)))
    aux = np.zeros((128, 144), dtype=np.float32)
    aux[:, 0:128] = np.eye(128, dtype=np.float32)
    aux[:, 128:144] = (1.0 / np.arange(1, 17, dtype=np.float64)).astype(np.float32)[None, :]
    return {
        "w_in": f(I["w_in"])[0], "pool_w": f(I["pool_w"])[0].reshape(1024, 256),
        "w_pool_proj": f(I["w_pool_proj"])[0], "w_conv_out": f(I["w_conv_out"])[0], "w_o": f(I["w_o"])[0],
        "w_up": f(I["w_up"])[0], "w_down": f(I["w_down"])[0],
        "vecs": np.ascontiguousarray(vecs), "cw": np.ascontiguousarray(cw), "fw": np.ascontiguousarray(fw),
        "fb": fb, "gfin": gfin, "aux": aux,
    }


def kernel(x, norm_mix, w_in, pool_w, pool_scale, w_pool_proj, conv_w, w_conv_out, w_o,
           norm_ffn, w_up, ffn_conv_w, ffn_conv_b, w_down, norm_final):
    x = np.ascontiguousarray(np.asarray(x, dtype=np.float32))
    shared = make_shared(dict(norm_mix=norm_mix, w_in=w_in, pool_w=pool_w, pool_scale=pool_scale,
                              w_pool_proj=w_pool_proj, conv_w=conv_w, w_conv_out=w_conv_out, w_o=w_o,
                              norm_ffn=norm_ffn, w_up=w_up, ffn_conv_w=ffn_conv_w, ffn_conv_b=ffn_conv_b,
                              w_down=w_down, norm_final=norm_final))
    xf = x.reshape(N_CORES, TOK_PER_CORE, D)
    in_maps = [dict(shared, x=np.ascontiguousarray(xf[c])) for c in range(N_CORES)]
    nc = _get_program()
    res = run_bass_kernel_spmd(nc, in_maps, core_ids=list(range(N_CORES)))
    out = np.stack([np.asarray(r["out"], dtype=np.float32) for r in res.results], axis=0)
    return out.reshape(x.shape)
```

```python
from contextlib import ExitStack

import os
import numpy as np
import concourse.bass as bass
import concourse.mybir as mybir
from concourse.bass_utils import run_bass_kernel_spmd

F32 = mybir.dt.float32
BF16 = mybir.dt.bfloat16
AF = mybir.ActivationFunctionType
ALU = mybir.AluOpType

N_CORES = 8
D = 1024
SEQ = 2048
TOK_PER_CORE = 4096
T = 512
NT = TOK_PER_CORE // T
TILES_PER_SEQ = SEQ // T
D_FF = 2816
NJ = D_FF // 128
EPS = 1e-6
R = 8
NBANK = 6
NT512 = 7
NU = 4
NS = 3
POOL_WINDOWS = (2, 4, 8, 16)
DBG = os.environ.get("KDBG", "")
MUL_ENG = "dve" if "dvemul" in DBG else "pool"


_CACHE = {}


class Stream:
    def __init__(self, name):
        self.name = name
        self.count = 0
        self.ops = []
        self.waited = {}


class Prog:
    def __init__(self):
        self.streams = {n: Stream(n) for n in ("pe", "act", "dve", "pool", "sp")}
        self.res = {}
        self.semval = {}
        self.tag = ""
        self.evtag = {}
        self.pe_n = []

    def _deps(self, reads, writes):
        evs = []
        for k in reads:
            r = self.res.get(k)
            if r and r[0] is not None:
                evs.append(r[0])
        for k in writes:
            r = self.res.get(k)
            if r:
                if r[0] is not None:
                    evs.append(r[0])
                evs.extend(r[1])
        return evs

    def _commit(self, ev, reads, writes):
        for k in reads:
            r = self.res.setdefault(k, [None, []])
            r[1].append(ev)
        for k in writes:
            self.res[k] = [ev, []]

    def _waits(self, st, evs):
        need = {}
        for sem, val in evs:
            if st.name == "pe" and sem == "pe":
                continue
            if st.waited.get(sem, 0) >= val:
                continue
            if need.get(sem, 0) < val:
                need[sem] = val
        for sem, val in need.items():
            st.waited[sem] = val
        return list(need.items())

    def op(self, eng, fn, reads=(), writes=()):
        st = self.streams[eng]
        waits = self._waits(st, self._deps(reads, writes))
        st.count += 1
        ev = (eng, st.count)
        st.ops.append((waits, fn, eng, 1))
        self.evtag[ev] = self.tag
        self._commit(ev, reads, writes)
        return ev

    def dma(self, queue, sem, fn, reads=(), writes=()):
        st = self.streams[queue]
        evs = self._deps(reads, writes)
        prev = self.semval.get(sem, 0)
        if prev:
            evs.append((sem, prev))
        waits = self._waits(st, evs)
        val = prev + 16
        self.semval[sem] = val
        ev = (sem, val)
        st.ops.append((waits, fn, sem, 16))
        self.evtag[ev] = self.tag + "/dma"
        self._commit(ev, reads, writes)
        return ev

    def wait_only(self, eng, evs):
        st = self.streams[eng]
        waits = self._waits(st, evs)
        if waits:
            st.ops.append((waits, None, None, 0))


def build_program(NT=NT, TOK=TOK_PER_CORE, STOP=99):
    nc = bass.Bass("TRN2", target_bir_lowering=False)
    dt = nc.dram_tensor

    x_d = dt("x", [TOK, D], F32, kind="ExternalInput").ap()
    out_d = dt("out", [TOK, D], F32, kind="ExternalOutput").ap()
    w_in_d = dt("w_in", [D, 6 * D], F32, kind="ExternalInput").ap()
    pool_w_d = dt("pool_w", [1024, 256], F32, kind="ExternalInput").ap()
    w_pp_d = dt("w_pool_proj", [D, D], F32, kind="ExternalInput").ap()
    w_co_d = dt("w_conv_out", [D, D], F32, kind="ExternalInput").ap()
    w_o_d = dt("w_o", [D, D], F32, kind="ExternalInput").ap()
    w_up_d = dt("w_up", [D, 2 * D_FF], F32, kind="ExternalInput").ap()
    w_dn_d = dt("w_down", [D_FF, D], F32, kind="ExternalInput").ap()
    vecs_d = dt("vecs", [128, 24], F32, kind="ExternalInput").ap()
    cw_d = dt("cw", [128, 24], F32, kind="ExternalInput").ap()
    fw_d = dt("fw", [128, 132], F32, kind="ExternalInput").ap()
    fb_d = dt("fb", [128, 44], F32, kind="ExternalInput").ap()
    gfin_d = dt("gfin", [128, D], F32, kind="ExternalInput").ap()
    aux_d = dt("aux", [128, 144], F32, kind="ExternalInput").ap()

    wv_in = w_in_d.rearrange("(k p) n -> p k n", p=128)
    wv_pw = pool_w_d.rearrange("(gk p) c -> p gk c", p=128)
    wv_pp = w_pp_d.rearrange("(k p) n -> p k n", p=128)
    wv_co = w_co_d.rearrange("(k p) n -> p k n", p=128)
    wv_o = w_o_d.rearrange("(k p) n -> p k n", p=128)
    wv_up = w_up_d.rearrange("(k p) n -> p k n", p=128)
    wv_dn = w_dn_d.rearrange("(k p) n -> p k n", p=128)

    units = []

    def add_unit(name, src, nk, ncols):
        units.append((name, src, nk, ncols))

    add_unit("Z0", wv_in[:, :, 0:512], 8, 512)
    add_unit("Z1", wv_in[:, :, 512:1024], 8, 512)
    add_unit("PW", wv_pw, 8, 256)
    for hh in range(2):
        add_unit(f"C{hh}", wv_in[:, :, 2048 + hh * 512:2048 + (hh + 1) * 512], 8, 512)
        add_unit(f"V{hh}", wv_in[:, :, 3072 + hh * 512:3072 + (hh + 1) * 512], 8, 512)
        add_unit(f"B{hh}", wv_in[:, :, 1024 + hh * 512:1024 + (hh + 1) * 512], 8, 512)
    for hh in range(2):
        add_unit(f"PP{hh}", wv_pp[:, :, hh * 512:(hh + 1) * 512], 8, 512)
        add_unit(f"CO{hh}", wv_co[:, :, hh * 512:(hh + 1) * 512], 8, 512)
        add_unit(f"GP{hh}", wv_in[:, :, 4096 + hh * 512:4096 + (hh + 1) * 512], 8, 512)
        add_unit(f"GC{hh}", wv_in[:, :, 5120 + hh * 512:5120 + (hh + 1) * 512], 8, 512)
    for hh in range(2):
        add_unit(f"WO{hh}", wv_o[:, :, hh * 512:(hh + 1) * 512], 8, 512)
    for u in range(6):
        nc_ = 512 if u < 5 else 256
        add_unit(f"UG{u}", wv_up[:, :, u * 512:u * 512 + nc_], 8, nc_)
        add_unit(f"UV{u}", wv_up[:, :, D_FF + u * 512:D_FF + u * 512 + nc_], 8, nc_)
    KG = [(0, 8), (8, 16), (16, 22)]
    for hh in range(2):
        for gi, (k0, k1) in enumerate(KG):
            add_unit(f"WD{hh}{gi}", wv_dn[:, k0:k1, hh * 512:(hh + 1) * 512], k1 - k0, 512)
    NUNITS = len(units)
    uidx = {u[0]: i for i, u in enumerate(units)}

    scr_d = dt("wscr", [NUNITS, 128, 8, 512], BF16, kind="Internal").ap()

    P = Prog()
    _CACHE["prog"] = P
    es = ExitStack()
    E = es.enter_context
    with es:
        xt = [E(nc.sbuf_tensor(f"xt{b}", [128, 4, D], F32)) for b in range(2)]
        xs = E(nc.sbuf_tensor("xs0", [128, 4, D], BF16))
        hT = [E(nc.sbuf_tensor(f"hT{i}", [128, 8, T], BF16)) for i in range(2)]
        A_t = E(nc.sbuf_tensor("A_t", [128, 8, T], BF16))
        p2_t = E(nc.sbuf_tensor("p2_t", [128, 8, T], BF16))
        cv_t = E(nc.sbuf_tensor("cv_t", [128, 8, T], BF16))
        aT_t = E(nc.sbuf_tensor("aT_t", [128, NJ, T], BF16))
        slots = [E(nc.sbuf_tensor(f"slot{r}", [128, 8, 512], BF16)) for r in range(R)]
        t512 = [E(nc.sbuf_tensor(f"t512_{i}", [128, T], F32)) for i in range(NT512)]
        ubuf = [E(nc.sbuf_tensor(f"ubuf{i}", [128, T + 16], F32)) for i in range(NU)]
        sbuf_ = [E(nc.sbuf_tensor(f"sbuf{i}", [128, T + 16], F32)) for i in range(NS)]
        junk = E(nc.sbuf_tensor("junk", [128, D], BF16))
        vecs = E(nc.sbuf_tensor("vecs_t", [128, 24], F32))
        cw = E(nc.sbuf_tensor("cw_t", [128, 3, 8], F32))
        fw = E(nc.sbuf_tensor("fw_t", [128, 3, 44], F32))
        fb = E(nc.sbuf_tensor("fb_t", [128, 44], F32))
        gfin = E(nc.sbuf_tensor("gfin_t", [128, D], F32))
        aux = E(nc.sbuf_tensor("aux_t", [128, 144], F32))
        identb = E(nc.sbuf_tensor("identb", [128, 128], BF16))
        mhalf = E(nc.sbuf_tensor("mhalf", [128, 4], F32))
        ss = [E(nc.sbuf_tensor(f"ss{i}", [128, 4], F32)) for i in range(3)]
        ms = [E(nc.sbuf_tensor(f"ms{i}", [128, 4], F32)) for i in range(3)]
        rstd = [E(nc.sbuf_tensor(f"rstd{i}", [128, 4], F32)) for i in range(3)]
        phalo = E(nc.sbuf_tensor("phalo", [128, 8, 16], F32))
        chalo = E(nc.sbuf_tensor("chalo", [128, 8, 2], F32))
        fhalo = E(nc.sbuf_tensor("fhalo", [128, 44, 2], F32))
        fH = E(nc.sbuf_tensor("fH", [128, 44, 2], F32))
        ftmp = E(nc.sbuf_tensor("ftmp", [128, 44], F32))
        pfix = E(nc.sbuf_tensor("pfix", [128, 16], F32))
        banks = [E(nc.psum_tensor(f"bank{i}", [128, 512], F32)) for i in range(NBANK)]
        tbank = [E(nc.psum_tensor(f"tbank{i}", [128, 512], BF16)) for i in range(2)]
        sem_names = ["pe", "act", "dve", "pool", "x0", "x1"] + ["c_" + n for n in ("vecs", "aux", "cw", "fw", "fb", "gfin")] + [f"w{r}" for r in range(R)]
        sems = {n: E(nc.semaphore("sem_" + n)) for n in sem_names}

        g1 = lambda k: vecs[:, k:k + 1]
        g2 = lambda k: vecs[:, 8 + k:9 + k]
        psc = lambda k: vecs[:, 16 + k:17 + k]
        invcnt = aux[:, 128:144]

        cnt = {"bank": 0, "t512": 0, "u": 0, "tb": 0, "s": 0}

        def new_bank():
            i = cnt["bank"] % NBANK
            cnt["bank"] += 1
            return banks[i], ("bank", i)

        def new_t512():
            i = cnt["t512"] % NT512
            cnt["t512"] += 1
            return t512[i], ("t512", i)

        def new_u():
            i = cnt["u"] % NU
            cnt["u"] += 1
            return ubuf[i], ("u", i), ("uh", i)

        def new_s():
            i = cnt["s"] % NS
            cnt["s"] += 1
            return sbuf_[i], ("sb", i)

        def new_tb():
            i = cnt["tb"] % 2
            cnt["tb"] += 1
            return tbank[i], ("tb", i)

        def const_loads(names):
            table = {
                "vecs": (lambda e: e.dma_start(out=vecs[:], in_=vecs_d)),
                "aux": (lambda e: e.dma_start(out=aux[:], in_=aux_d)),
                "cw": (lambda e: e.dma_start(out=cw[:].rearrange("p a b -> p (a b)"), in_=cw_d)),
                "fw": (lambda e: e.dma_start(out=fw[:].rearrange("p a b -> p (a b)"), in_=fw_d)),
                "fb": (lambda e: e.dma_start(out=fb[:], in_=fb_d)),
                "gfin": (lambda e: e.dma_start(out=gfin[:], in_=gfin_d)),
            }
            for n in names:
                P.dma("sp", "c_" + n, table[n], writes=[("c", n)])
        const_loads(["aux", "vecs"])
        P.op("pool", lambda e: e.memset(mhalf[:], -0.5), writes=[("c", "mhalf")])
        P.op("dve", lambda e: e.tensor_copy(out=identb[:], in_=aux[:, 0:128]),
             reads=[("c", "aux")], writes=[("c", "identb")])
        CONST_KEYS = [("c", n) for n in ("vecs", "cw", "fw", "fb", "gfin", "aux", "mhalf", "identb")]

        per_tile = ["C0", "V0", "B0", "C1", "V1", "B1", "PW",
                    "PP0", "CO0", "GP0", "GC0", "PP1", "CO1", "GP1", "GC1", "WO0", "WO1"]
        ffn_units = [f"U{t}{u}" for u in range(6) for t in ("G", "V")] + \
                    [f"WD{hh}{gi}" for hh in range(2) for gi in range(3)]
        sched = ["Z0", "Z1"]
        Gof = {(0, "Z0"): 0, (0, "Z1"): 1}
        for ti_ in range(NT):
            for nm in per_tile:
                Gof[(ti_, nm)] = len(sched)
                sched.append(nm)
            if ti_ + 1 < NT:
                Gof[(ti_ + 1, "Z1")] = len(sched)
                sched.append("Z1")
            for nm in ffn_units:
                Gof[(ti_, nm)] = len(sched)
                sched.append(nm)
            if ti_ + 1 < NT:
                Gof[(ti_ + 1, "Z0")] = len(sched)
                sched.append("Z0")
        seen_units = set()

        def slot_key(G):
            return ("slot", G % R)

        def emit_load(G):
            if G >= len(sched):
                return
            loaded.add(G)
            n = uidx[sched[G]]
            name, src, nk, ncols = units[n]
            slot = slots[G % R]
            semn = f"w{G % R}"
            if name not in seen_units:
                seen_units.add(name)
                dst = slot[:, 0:nk, 0:ncols]
                P.dma("pool", semn, lambda e: e.dma_start(out=dst, in_=src), writes=[slot_key(G)])
                if NT > 1:
                    P.dma("sp", semn, lambda e: e.dma_start(out=scr_d[n], in_=slot[:]),
                          reads=[slot_key(G)], writes=[("scr", n)])
            else:
                P.dma("sp", semn, lambda e: e.dma_start(out=slot[:], in_=scr_d[n]),
                      reads=[("scr", n)], writes=[slot_key(G)])

        loaded = set()
        used_max = [-1]

        def use_unit(ti, name):
            G = Gof[(ti, name)]
            assert G in loaded, (ti, name, G)
            assert G + R > used_max[0], (ti, name, G, used_max[0])
            used_max[0] = max(used_max[0], G)
            return slots[G % R], slot_key(G)

        def done_unit(ti, name):
            emit_load(Gof[(ti, name)] + R)

        def emit_xload(ti):
            if ti >= NT:
                return
            b = ti % 2
            src = x_d[ti * T:(ti + 1) * T, :].rearrange("(s p) d -> p s d", p=128)
            P.dma("sp", f"x{b}", lambda e: e.dma_start(out=xt[b][:], in_=src),
                  writes=[("xt", b, s, h) for s in range(4) for h in range(2)])

        def emit_xstore(ti):
            b = ti % 2
            dst = out_d[ti * T:(ti + 1) * T, :].rearrange("(s p) d -> p s d", p=128)
            return P.dma("sp", f"x{b}", lambda e: e.dma_start(out=dst, in_=xt[b][:]),
                         reads=[("xt", b, s, h) for s in range(4) for h in range(2)])

        def mm_group(bank_ap, bank_key, pairs, reads):
            n = len(pairs)

            def fn(e):
                ins = None
                for i, (l, r) in enumerate(pairs):
                    ins = e.matmul(bank_ap[:], lhsT=l, rhs=r, start=(i == 0), stop=(i == n - 1))
                return ins
            P.pe_n.append((n, P.tag))
            return P.op("pe", fn, reads=reads, writes=[bank_key])

        def norm_sumsq(which, b, s):
            P.op("act", lambda e: e.activation(out=junk[:], in_=xt[b][:, s, :], func=AF.Square,
                                               accum_out=ss[which][:, s:s + 1]),
                 reads=[("xt", b, s, 0), ("xt", b, s, 1)], writes=[("ss", which, s)])

        def norm_rstd1(which, s):
            P.op("dve", lambda e: e.tensor_scalar(out=ms[which][:, s:s + 1], in0=ss[which][:, s:s + 1],
                                                  scalar1=1.0 / D, scalar2=EPS, op0=ALU.mult, op1=ALU.add),
                 reads=[("ss", which, s)], writes=[("ms", which, s)])
            P.op("pool", lambda e: e.tensor_tensor(out=rstd[which][:, s:s + 1], in0=ms[which][:, s:s + 1],
                                                   in1=mhalf[:, 0:1], op=ALU.pow),
                 reads=[("ms", which, s), ("c", "mhalf")], writes=[("rstd", which, s)])

        def norm_rstd(which):
            allk = lambda n: [(n, which, s) for s in range(4)]
            P.op("dve", lambda e: e.tensor_scalar(out=ms[which][:], in0=ss[which][:],
                                                  scalar1=1.0 / D, scalar2=EPS, op0=ALU.mult, op1=ALU.add),
                 reads=allk("ss"), writes=allk("ms"))
            P.op("pool", lambda e: e.tensor_tensor(out=rstd[which][:], in0=ms[which][:], in1=mhalf[:], op=ALU.pow),
                 reads=allk("ms") + [("c", "mhalf")], writes=allk("rstd"))

        def norm_scale(which, b, s):
            P.op("act", lambda e: e.activation(out=xs[:, s, :], in_=xt[b][:, s, :], func=AF.Copy,
                                               scale=rstd[which][:, s:s + 1]),
                 reads=[("xt", b, s, 0), ("xt", b, s, 1), ("rstd", which, s)], writes=[("xs", s)])

        def transposes(hb, gfun, act_only=False):
            for k in range(8):
                tb_ap, tb_key = new_tb()

                def fn(e, k=k, tb_ap=tb_ap):
                    ins = None
                    for s in range(4):
                        ins = e.transpose(out=tb_ap[:, s * 128:(s + 1) * 128],
                                          in_=xs[:, s, k * 128:(k + 1) * 128], identity=identb[:])
                    return ins
                P.pe_n.append((4, P.tag + "/T"))
                P.op("pe", fn, reads=[("xs", s) for s in range(4)] + [("c", "identb")], writes=[tb_key])
                if act_only or k % 2 == 0:
                    P.op("act", lambda e, k=k, tb_ap=tb_ap: e.activation(out=hT[hb][:, k, :], in_=tb_ap[:],
                                                                         func=AF.Copy, scale=gfun(k)),
                         reads=[tb_key, ("c", "vecs")], writes=[("hT", hb, k)])
                else:
                    P.op("dve", lambda e, k=k, tb_ap=tb_ap: e.tensor_scalar(out=hT[hb][:, k, :], in0=tb_ap[:],
                                                                            scalar1=gfun(k), scalar2=None,
                                                                            op0=ALU.mult),
                         reads=[tb_key, ("c", "vecs")], writes=[("hT", hb, k)])

        def emit_norm1(ti):
            b = ti % 2
            for s in range(4):
                norm_sumsq(0, b, s)
            norm_rstd(0)
            for s in range(4):
                norm_scale(0, b, s)

        def phase_P(ti, chunks=range(8), part="all", st=None):
            first = (ti % TILES_PER_SEQ == 0)
            hb = 0
            hTk = [("hT", hb, k) for k in range(8)]
            if st is None:
                st = {}
            for c in chunks:
                g = c // 2
                win = POOL_WINDOWS[g]
                if part in ("front", "all"):
                    uname = "Z0" if c < 4 else "Z1"
                    slot, skey = use_unit(ti, uname)
                    cc = c % 4
                    bk, bkey = new_bank()
                    mm_group(bk, bkey, [(slot[:, k, cc * 128:(cc + 1) * 128], hT[hb][:, k, :]) for k in range(8)],
                             reads=[skey] + hTk)
                    if cc == 3:
                        done_unit(ti, uname)
                    U, ukey, uhkey = new_u()
                    st[c] = (U, ukey, uhkey)
                    if first:
                        P.op("pool", lambda e, U=U: e.memset(U[:, 0:16], 0.0), writes=[uhkey])
                    else:
                        P.op("pool", lambda e, U=U, c=c: e.tensor_copy(out=U[:, 0:16], in_=phalo[:, c, :]),
                             reads=[("ph", c)], writes=[uhkey])
                    P.op("act", lambda e, U=U, bk=bk: e.activation(out=U[:, 16:16 + T], in_=bk[:], func=AF.Copy),
                         reads=[bkey], writes=[ukey])
                    P.op("pool", lambda e, U=U, c=c: e.tensor_copy(out=phalo[:, c, :], in_=U[:, T:T + 16]),
                         reads=[ukey], writes=[("ph", c)])
                if part in ("back", "all"):
                    U, ukey, uhkey = st[c]
                    src, srckeys = U, [ukey, uhkey]
                    off = 1
                    for lvl in range(g + 1):
                        S, skey2 = new_s()
                        lo = 2 * off - 1
                        P.op("dve", lambda e, S=S, src=src, off=off, lo=lo: e.tensor_tensor(
                            out=S[:, lo:T + 16], in0=src[:, lo:T + 16], in1=src[:, lo - off:T + 16 - off],
                            op=ALU.add),
                            reads=srckeys, writes=[skey2])
                        src, srckeys = S, [skey2]
                        off *= 2
                    P.op("dve", lambda e, S=src, U=U, c=c, win=win: e.scalar_tensor_tensor(
                        out=A_t[:, c, :], in0=S[:, 16:16 + T], scalar=1.0 / win, in1=U[:, 16:16 + T],
                        op0=ALU.mult, op1=ALU.subtract),
                        reads=srckeys + [ukey], writes=[("A", c)])
                    if first:
                        nfix = win - 1
                        P.op("dve", lambda e, S=src, nfix=nfix: e.tensor_tensor(
                            out=pfix[:, 0:nfix], in0=S[:, 16:16 + nfix], in1=invcnt[:, 0:nfix], op=ALU.mult),
                            reads=srckeys + [("c", "aux")], writes=[("pfix",)])
                        P.op("dve", lambda e, U=U, c=c, nfix=nfix: e.tensor_tensor(
                            out=A_t[:, c, 0:nfix], in0=pfix[:, 0:nfix], in1=U[:, 16:16 + nfix], op=ALU.subtract),
                            reads=[("pfix",), ukey], writes=[("A", c)])
            return st

        def phase_P2(ti):
            slot, skey = use_unit(ti, "PW")
            for g in range(4):
                for o in range(2):
                    c = 2 * g + o
                    bk, bkey = new_bank()
                    mm_group(bk, bkey,
                             [(slot[:, 2 * g + k2, o * 128:(o + 1) * 128], A_t[:, 2 * g + k2, :]) for k2 in range(2)],
                             reads=[skey, ("A", 2 * g), ("A", 2 * g + 1)])
                    if c % 2 == 0:
                        P.op("act", lambda e, bk=bk, c=c: e.activation(out=p2_t[:, c, :], in_=bk[:], func=AF.Copy,
                                                                      scale=psc(c)),
                             reads=[bkey, ("c", "vecs")], writes=[("p2", c)])
                    else:
                        P.op("dve", lambda e, bk=bk, c=c: e.tensor_scalar(out=p2_t[:, c, :], in0=bk[:],
                                                                         scalar1=psc(c), scalar2=None, op0=ALU.mult),
                             reads=[bkey, ("c", "vecs")], writes=[("p2", c)])
            done_unit(ti, "PW")

        def phase_C(ti, j_hooks=None):
            first = (ti % TILES_PER_SEQ == 0)
            hb = 0
            hTk = [("hT", hb, k) for k in range(8)]
            for j in range(8):
                hh, jj = divmod(j, 4)
                sc, kc = use_unit(ti, f"C{hh}")
                sv, kv = use_unit(ti, f"V{hh}")
                sb_, kb = use_unit(ti, f"B{hh}")
                bc, bckey = new_bank()
                mm_group(bc, bckey, [(sc[:, k, jj * 128:(jj + 1) * 128], hT[hb][:, k, :]) for k in range(8)],
                         reads=[kc] + hTk)
                bv, bvkey = new_bank()
                mm_group(bv, bvkey, [(sv[:, k, jj * 128:(jj + 1) * 128], hT[hb][:, k, :]) for k in range(8)],
                         reads=[kv] + hTk)
                bb, bbkey = new_bank()
                mm_group(bb, bbkey, [(sb_[:, k, jj * 128:(jj + 1) * 128], hT[hb][:, k, :]) for k in range(8)],
                         reads=[kb] + hTk)
                if jj == 3:
                    done_unit(ti, f"C{hh}")
                    done_unit(ti, f"V{hh}")
                    done_unit(ti, f"B{hh}")
                zc, zckey = new_t512()
                P.op("act", lambda e, zc=zc, bc=bc: e.activation(out=zc[:], in_=bc[:], func=AF.Copy),
                     reads=[bckey], writes=[zckey])
                CV, cvkey, cvhkey = new_u()
                if first:
                    P.op("pool", lambda e, CV=CV: e.memset(CV[:, 0:2], 0.0), writes=[cvhkey])
                else:
                    P.op("pool", lambda e, CV=CV, j=j: e.tensor_copy(out=CV[:, 0:2], in_=chalo[:, j, :]),
                         reads=[("ch", j)], writes=[cvhkey])
                P.op("dve", lambda e, CV=CV, bv=bv, zc=zc: e.tensor_tensor(out=CV[:, 2:2 + T], in0=bv[:], in1=zc[:],
                                                                           op=ALU.mult),
                     reads=[bvkey, zckey], writes=[cvkey])
                P.op("pool", lambda e, CV=CV, j=j: e.tensor_copy(out=chalo[:, j, :], in_=CV[:, T:T + 2]),
                     reads=[cvkey], writes=[("ch", j)])
                acc, acckey = new_t512()
                P.op("act", lambda e, acc=acc, CV=CV, j=j: e.activation(out=acc[:], in_=CV[:, 2:2 + T], func=AF.Copy,
                                                                       scale=cw[:, 2, j:j + 1]),
                     reads=[cvkey, ("c", "cw")], writes=[acckey])
                P.op("dve", lambda e, acc=acc, CV=CV, j=j: e.scalar_tensor_tensor(
                    out=acc[:], in0=CV[:, 1:1 + T], scalar=cw[:, 1, j:j + 1], in1=acc[:], op0=ALU.mult, op1=ALU.add),
                    reads=[cvkey, cvhkey, acckey, ("c", "cw")], writes=[acckey])
                P.op("dve", lambda e, acc=acc, CV=CV, j=j: e.scalar_tensor_tensor(
                    out=acc[:], in0=CV[:, 0:T], scalar=cw[:, 0, j:j + 1], in1=acc[:], op0=ALU.mult, op1=ALU.add),
                    reads=[cvkey, cvhkey, acckey, ("c", "cw")], writes=[acckey])
                P.op("dve", lambda e, acc=acc, bb=bb, j=j: e.tensor_tensor(out=cv_t[:, j, :], in0=bb[:], in1=acc[:],
                                                                          op=ALU.mult),
                     reads=[bbkey, acckey], writes=[("cv", j)])
                if j_hooks and j < len(j_hooks):
                    j_hooks[j]()

        def phase_M(ti, oc_hooks=None):
            hb = 0
            hTk = [("hT", hb, k) for k in range(8)]
            for oc in range(8):
                hh, oo = divmod(oc, 4)
                spp, kpp = use_unit(ti, f"PP{hh}")
                sco, kco = use_unit(ti, f"CO{hh}")
                sgp_, kgp = use_unit(ti, f"GP{hh}")
                sgc_, kgc = use_unit(ti, f"GC{hh}")
                col = slice(oo * 128, (oo + 1) * 128)
                bgp, bgpkey = new_bank()
                mm_group(bgp, bgpkey, [(sgp_[:, k, col], hT[hb][:, k, :]) for k in range(8)], reads=[kgp] + hTk)
                bgc, bgckey = new_bank()
                mm_group(bgc, bgckey, [(sgc_[:, k, col], hT[hb][:, k, :]) for k in range(8)], reads=[kgc] + hTk)
                byp, bypkey = new_bank()
                mm_group(byp, bypkey, [(spp[:, k, col], p2_t[:, k, :]) for k in range(8)],
                         reads=[kpp] + [("p2", k) for k in range(8)])
                byc, byckey = new_bank()
                mm_group(byc, byckey, [(sco[:, k, col], cv_t[:, k, :]) for k in range(8)],
                         reads=[kco] + [("cv", k) for k in range(8)])
                if oo == 3:
                    for nm in ("PP", "CO", "GP", "GC"):
                        done_unit(ti, f"{nm}{hh}")
                s1, s1key = new_t512()
                s2, s2key = new_t512()
                P.op("act", lambda e, s1=s1, bgp=bgp: e.activation(out=s1[:], in_=bgp[:], func=AF.Sigmoid),
                     reads=[bgpkey], writes=[s1key])
                P.op("act", lambda e, s2=s2, bgc=bgc: e.activation(out=s2[:], in_=bgc[:], func=AF.Sigmoid),
                     reads=[bgckey], writes=[s2key])
                P.op("dve", lambda e, s1=s1, byp=byp: e.tensor_tensor(out=s1[:], in0=byp[:], in1=s1[:], op=ALU.mult),
                     reads=[bypkey, s1key], writes=[s1key])
                P.op("dve", lambda e, s2=s2, byc=byc: e.tensor_tensor(out=s2[:], in0=byc[:], in1=s2[:], op=ALU.mult),
                     reads=[byckey, s2key], writes=[s2key])
                P.op("dve", lambda e, s1=s1, s2=s2, oc=oc: e.tensor_tensor(out=A_t[:, oc, :], in0=s1[:], in1=s2[:],
                                                                          op=ALU.add),
                     reads=[s1key, s2key], writes=[("A", oc)])
                if oc_hooks and oc < len(oc_hooks):
                    oc_hooks[oc]()

        def phase_O(ti):
            b = ti % 2
            for hh in range(2):
                swo, kwo = use_unit(ti, f"WO{hh}")
                for s in range(4):
                    bk, bkey = new_bank()
                    mm_group(bk, bkey, [(A_t[:, k, s * 128:(s + 1) * 128], swo[:, k, :]) for k in range(8)],
                             reads=[kwo] + [("A", k) for k in range(8)])
                    P.op("dve", lambda e, bk=bk, s=s, hh=hh, b=b: e.tensor_tensor(
                        out=xt[b][:, s, hh * 512:(hh + 1) * 512], in0=bk[:], in1=xt[b][:, s, hh * 512:(hh + 1) * 512],
                        op=ALU.add),
                        reads=[bkey, ("xt", b, s, hh)], writes=[("xt", b, s, hh)])
                    if hh == 1:
                        norm_sumsq(1, b, s)
                        norm_rstd1(1, s)
                        if s >= 1:
                            norm_scale(1, b, s - 1)
                done_unit(ti, f"WO{hh}")
            norm_scale(1, b, 3)

        def phase_F(ti):
            first = (ti % TILES_PER_SEQ == 0)
            h2k = [("hT", 1, k) for k in range(8)]
            if first:
                P.op("pool", lambda e: e.memset(fH[:].rearrange("p a b -> p (a b)"), 0.0), writes=[("fH",)])
            else:
                fhk = [("fh", q) for q in range(44)]
                P.op("pool", lambda e: e.tensor_tensor(out=fH[:, :, 0], in0=fw[:, 1, :], in1=fhalo[:, :, 1],
                                                       op=ALU.mult),
                     reads=[("c", "fw")] + fhk, writes=[("fH",)])
                P.op("pool", lambda e: e.tensor_tensor(out=ftmp[:], in0=fw[:, 0, :], in1=fhalo[:, :, 0],
                                                       op=ALU.mult),
                     reads=[("c", "fw")] + fhk, writes=[("ftmp",)])
                P.op("pool", lambda e: e.tensor_tensor(out=fH[:, :, 0], in0=fH[:, :, 0], in1=ftmp[:], op=ALU.add),
                     reads=[("ftmp",), ("fH",)], writes=[("fH",)])
                P.op("pool", lambda e: e.tensor_tensor(out=fH[:, :, 1], in0=fw[:, 0, :], in1=fhalo[:, :, 1],
                                                       op=ALU.mult),
                     reads=[("c", "fw")] + fhk, writes=[("fH",)])
            pend = [None]

            def flush_pair():
                if pend[0] is None:
                    return
                ag, agkey, av, avkey, jp = pend[0]
                pend[0] = None
                P.op("act", lambda e, ag=ag: e.activation(out=ag[:], in_=ag[:], func=AF.Silu),
                     reads=[agkey], writes=[agkey])
                P.op("pool", lambda e, ag=ag, av=av, jp=jp: e.tensor_tensor(out=aT_t[:, jp, :], in0=ag[:], in1=av[:],
                                                                           op=ALU.mult),
                     reads=[agkey, avkey], writes=[("aT", jp)])

            for j in range(NJ):
                u, jj = divmod(j, 4)
                sg_, kg_ = use_unit(ti, f"UG{u}")
                sv_, kv_ = use_unit(ti, f"UV{u}")
                col = slice(jj * 128, (jj + 1) * 128)
                bg, bgkey = new_bank()
                mm_group(bg, bgkey, [(sg_[:, k, col], hT[1][:, k, :]) for k in range(8)], reads=[kg_] + h2k)
                bv, bvkey = new_bank()
                mm_group(bv, bvkey, [(sv_[:, k, col], hT[1][:, k, :]) for k in range(8)], reads=[kv_] + h2k)
                if jj == 3 or j == NJ - 1:
                    done_unit(ti, f"UG{u}")
                    done_unit(ti, f"UV{u}")
                accs = []
                for (bkx, bkxkey, q) in ((bg, bgkey, j), (bv, bvkey, NJ + j)):
                    a, akey = new_t512()
                    P.op("act", lambda e, a=a, bkx=bkx, q=q: e.activation(
                        out=a[:], in_=bkx[:], func=AF.Identity, scale=fw[:, 2, q:q + 1], bias=fb[:, q:q + 1]),
                        reads=[bkxkey, ("c", "fw"), ("c", "fb")], writes=[akey])
                    accs.append((a, akey, bkx, bkxkey, q))
                flush_pair()
                for (a, akey, bkx, bkxkey, q) in accs:
                    P.op("dve", lambda e, a=a, bkx=bkx, q=q: e.scalar_tensor_tensor(
                        out=a[:, 1:T], in0=bkx[:, 0:T - 1], scalar=fw[:, 1, q:q + 1], in1=a[:, 1:T],
                        op0=ALU.mult, op1=ALU.add),
                        reads=[bkxkey, akey, ("c", "fw")], writes=[akey])
                for (a, akey, bkx, bkxkey, q) in accs:
                    P.op("dve", lambda e, a=a, bkx=bkx, q=q: e.scalar_tensor_tensor(
                        out=a[:, 2:T], in0=bkx[:, 0:T - 2], scalar=fw[:, 0, q:q + 1], in1=a[:, 2:T],
                        op0=ALU.mult, op1=ALU.add),
                        reads=[bkxkey, akey, ("c", "fw")], writes=[akey])
                for (a, akey, bkx, bkxkey, q) in accs:
                    P.op("dve", lambda e, a=a, q=q: e.tensor_tensor(out=a[:, 0:2], in0=a[:, 0:2], in1=fH[:, q, :],
                                                                    op=ALU.add),
                         reads=[akey, ("fH",)], writes=[akey])
                    P.op("dve", lambda e, bkx=bkx, q=q: e.tensor_copy(out=fhalo[:, q, :], in_=bkx[:, T - 2:T]),
                         reads=[bkxkey, ("fH",)], writes=[("fh", q)])
                (ag, agkey, _, _, _), (av, avkey, _, _, _) = accs
                pend[0] = (ag, agkey, av, avkey, j)
            flush_pair()

        def phase_D(ti, mid_hook=None, pre_hook=None):
            b = ti % 2
            if pre_hook is not None:
                pre_hook()
            for hh in range(2):
                bks = [new_bank() for _ in range(4)]
                for gi, (k0, k1) in enumerate(KG):
                    swd, kwd = use_unit(ti, f"WD{hh}{gi}")

                    def fn(e, swd=swd, k0=k0, k1=k1, bks=bks):
                        ins = None
                        for k in range(k0, k1):
                            for s in range(4):
                                ins = e.matmul(bks[s][0][:], lhsT=aT_t[:, k, s * 128:(s + 1) * 128],
                                               rhs=swd[:, k - k0, :], start=(k == 0), stop=(k == NJ - 1))
                        return ins
                    P.pe_n.append(((k1 - k0) * 4, P.tag))
                    P.op("pe", fn, reads=[kwd] + [("aT", k) for k in range(k0, k1)], writes=[bk[1] for bk in bks])
                    done_unit(ti, f"WD{hh}{gi}")
                for s in range(4):
                    bk, bkey = bks[s]
                    P.op("dve", lambda e, bk=bk, s=s, hh=hh, b=b: e.tensor_tensor(
                        out=xt[b][:, s, hh * 512:(hh + 1) * 512], in0=bk[:], in1=xt[b][:, s, hh * 512:(hh + 1) * 512],
                        op=ALU.add),
                        reads=[bkey, ("xt", b, s, hh)], writes=[("xt", b, s, hh)])
                    if hh == 1:
                        norm_sumsq(2, b, s)
            if mid_hook is not None:
                mid_hook()
            norm_rstd(2)

            def final_scale(s, b=b):
                P.op("dve", lambda e, s=s, b=b: e.scalar_tensor_tensor(
                    out=xt[b][:, s, :], in0=xt[b][:, s, :], scalar=rstd[2][:, s:s + 1], in1=gfin[:],
                    op0=ALU.mult, op1=ALU.mult),
                    reads=[("xt", b, s, 0), ("xt", b, s, 1), ("rstd", 2, s), ("c", "gfin")],
                    writes=[("xt", b, s, 0), ("xt", b, s, 1)])
            return [lambda s=s: final_scale(s) for s in range(4)]

        emit_xload(0)
        const_loads(["cw", "fw", "fb", "gfin"])
        emit_xload(1)
        for G in range(R):
            emit_load(G)
        emit_norm1(0)
        transposes(0, g1)
        phase_P(0)

        last_store = [None, None]
        deferred = None
        for ti in range(NT):
            b = ti % 2
            P.tag = f"C{ti}"
            j_hooks = None
            if deferred is not None:
                prev = ti - 1

                def tail(prev=prev, fs=deferred[3]):
                    fs()
                    last_store[prev % 2] = emit_xstore(prev)
                    emit_xload(prev + 2)
                j_hooks = [deferred[0], deferred[1], deferred[2], tail]
            phase_C(ti, j_hooks)
            P.tag = f"P2_{ti}"
            phase_P2(ti)
            P.tag = f"M{ti}"
            oc_hooks = None
            if ti + 1 < NT:
                nb = (ti + 1) % 2
                oc_hooks = [lambda s=s, nb=nb: norm_sumsq(0, nb, s) for s in range(3)]
                oc_hooks.append(lambda nb=nb: (norm_sumsq(0, nb, 3), norm_rstd(0)))
                oc_hooks += [lambda s=s, nb=nb: norm_scale(0, nb, s) for s in range(4)]
            phase_M(ti, oc_hooks)
            P.tag = f"T1n{ti}"
            if ti + 1 < NT:
                transposes(0, g1)
            P.tag = f"O{ti}"
            phase_O(ti)
            pst = None
            if ti + 1 < NT:
                P.tag = f"Pf{ti+1}"
                pst = phase_P(ti + 1, range(4, 8), "front")
            P.tag = f"T2_{ti}"
            transposes(1, g2)
            P.tag = f"F{ti}"
            phase_F(ti)
            P.tag = f"D{ti}"
            if ti + 1 < NT:
                deferred = phase_D(ti,
                                   mid_hook=(lambda ti=ti: phase_P(ti + 1, range(0, 4), "all")),
                                   pre_hook=(lambda ti=ti, pst=pst: phase_P(ti + 1, range(4, 8), "back", pst)))
            else:
                deferred = phase_D(ti)
        P.tag = "tail"
        for fs in deferred:
            fs()
        last_store[(NT - 1) % 2] = emit_xstore(NT - 1)

        P.wait_only("sp", [ev for ev in last_store if ev is not None])

        block = E(nc.Block())

        def replay(stream, eng):
            for waits, fn, inc_sem, inc_amt in stream.ops:
                for sem, val in waits:
                    eng.wait_ge(sems[sem], val)
                if fn is not None:
                    ins = fn(eng)
                    ins.then_inc(sems[inc_sem], inc_amt)

        @block.tensor
        def _(eng):
            replay(P.streams["pe"], eng)

        @block.scalar
        def _(eng):
            replay(P.streams["act"], eng)

        @block.vector
        def _(eng):
            replay(P.streams["dve"], eng)

        @block.gpsimd
        def _(eng):
            replay(P.streams["pool"], eng)

        @block.sync
        def _(eng):
            replay(P.streams["sp"], eng)

    return nc


def _get_program():
    if "nc" not in _CACHE:
        _CACHE["nc"] = build_program()
    return _CACHE["nc"]


def _chunked(v, n):
    return np.ascontiguousarray(np.asarray(v, dtype=np.float32).reshape(n, 128).T)


def make_shared(I):
    f = lambda a: np.ascontiguousarray(np.asarray(a, dtype=np.float32))
    vecs = np.concatenate([_chunked(f(I["norm_mix"])[0], 8), _chunked(f(I["norm_ffn"])[0], 8),
                           _chunked(f(I["pool_scale"])[0], 8)], axis=1)
    cw = np.stack([_chunked(f(I["conv_w"])[0, k], 8) for k in range(3)], axis=1).reshape(128, 24)
    fw = np.stack([_chunked(f(I["ffn_conv_w"])[0, k], 44) for k in range(3)], axis=1).reshape(128, 132)
    fb = _chunked(f(I["ffn_conv_b"])[0], 44)
    gfin = np.ascontiguousarray(np.broadcast_to(f(I["norm_final"])[None, :], (128, D)))
    aux = np.zeros((128, 144), dtype=np.float32)
    aux[:, 0:128] = np.eye(128, dtype=np.float32)
    aux[:, 128:144] = (1.0 / np.arange(1, 17, dtype=np.float64)).astype(np.float32)[None, :]
    return {
        "w_in": f(I["w_in"])[0], "pool_w": f(I["pool_w"])[0].reshape(1024, 256),
        "w_pool_proj": f(I["w_pool_proj"])[0], "w_conv_out": f(I["w_conv_out"])[0], "w_o": f(I["w_o"])[0],
        "w_up": f(I["w_up"])[0], "w_down": f(I["w_down"])[0],
        "vecs": np.ascontiguousarray(vecs), "cw": np.ascontiguousarray(cw), "fw": np.ascontiguousarray(fw),
        "fb": fb, "gfin": gfin, "aux": aux,
    }


def kernel(x, norm_mix, w_in, pool_w, pool_scale, w_pool_proj, conv_w, w_conv_out, w_o,
           norm_ffn, w_up, ffn_conv_w, ffn_conv_b, w_down, norm_final):
    x = np.ascontiguousarray(np.asarray(x, dtype=np.float32))
    shared = make_shared(dict(norm_mix=norm_mix, w_in=w_in, pool_w=pool_w, pool_scale=pool_scale,
                              w_pool_proj=w_pool_proj, conv_w=conv_w, w_conv_out=w_conv_out, w_o=w_o,
                              norm_ffn=norm_ffn, w_up=w_up, ffn_conv_w=ffn_conv_w, ffn_conv_b=ffn_conv_b,
                              w_down=w_down, norm_final=norm_final))
    xf = x.reshape(N_CORES, TOK_PER_CORE, D)
    in_maps = [dict(shared, x=np.ascontiguousarray(xf[c])) for c in range(N_CORES)]
    nc = _get_program()
    res = run_bass_kernel_spmd(nc, in_maps, core_ids=list(range(N_CORES)))
    out = np.stack([np.asarray(r["out"], dtype=np.float32) for r in res.results], axis=0)
    return out.reshape(x.shape)
```
